# Optimizing a Trainium2 kernel written in Bass

```python
import jax, jax.numpy as jnp
from jax import lax
import numpy as np

D_MODEL = 1024
BATCH = 32
SEQ = 2048
DEPTH = 2

D_MIX = D_MODEL
W_A = D_MIX // 4
W_B = D_MIX // 4
W_C = D_MIX // 4
W_D = D_MIX // 4
HEAD_DIM = 64
HEADS_C = W_C // HEAD_DIM
HEADS_D = W_D // HEAD_DIM
K_SHORT = 3
K_CONFORMER = 31
CHUNK = 128
Q_BLOCK = 128
D_FF = 2816
K_FFN = 3
RMS_EPS = 1e-6
LN_EPS = 1e-5
IN_A = 3 * W_A
IN_B = 2 * W_B
IN_C = 2 * W_C
IN_D = 3 * W_D
IN_TOTAL = IN_A + IN_B + IN_C + IN_D

kernel_name = "hybrid_parallel_conv_sgu_stickbreaking"


def _rmsnorm(x, g):
    xf = x.astype(jnp.float32)
    y = xf * lax.rsqrt(jnp.mean(xf * xf, axis=-1, keepdims=True) + RMS_EPS)
    return (y * g.astype(jnp.float32)).astype(x.dtype)


def _layernorm(x, g, b):
    xf = x.astype(jnp.float32)
    mu = jnp.mean(xf, axis=-1, keepdims=True)
    xc = xf - mu
    var = jnp.mean(xc * xc, axis=-1, keepdims=True)
    y = xc * lax.rsqrt(var + LN_EPS) * g.astype(jnp.float32) + b.astype(jnp.float32)
    return y.astype(x.dtype)


def _causal_dwconv(x, w):
    k = w.shape[0]
    return lax.conv_general_dilated(
        x, w[:, None, :].astype(x.dtype), window_strides=(1,), padding=[(k - 1, 0)],
        dimension_numbers=("NWC", "WIO", "NWC"), feature_group_count=x.shape[-1])


def _stick_breaking(q, k, v):
    s_len = q.shape[2]
    scale = q.shape[-1] ** -0.5
    outs = []
    for i in range(s_len // Q_BLOCK):
        q0 = i * Q_BLOCK
        kv_len = q0 + Q_BLOCK
        qb = q[:, :, q0:kv_len]
        kb = k[:, :, :kv_len]
        vb = v[:, :, :kv_len]
        z = jnp.einsum("bhtd,bhsd->bhts", qb, kb).astype(jnp.float32) * scale
        t_idx = q0 + jnp.arange(Q_BLOCK)[:, None]
        s_idx = jnp.arange(kv_len)[None, :]
        mask = s_idx < t_idx
        log_beta = jax.nn.log_sigmoid(z)
        log_one_minus = jnp.where(mask, log_beta - z, 0.0)
        after = lax.cumsum(log_one_minus, axis=3, reverse=True) - log_one_minus
        a = jnp.exp(jnp.where(mask, log_beta + after, -jnp.inf))
        outs.append(jnp.einsum("bhts,bhsd->bhtd", a.astype(vb.dtype), vb))
    return jnp.concatenate(outs, axis=2)


def setup_inputs(seed: int = 0) -> dict:
    key = jax.random.key(seed)
    ks = jax.random.split(key, 20)
    f32 = jnp.float32
    nrm = lambda k, shape, s: jax.random.normal(k, shape, f32) * s
    x = jax.random.normal(ks[0], (BATCH, SEQ, D_MODEL), f32)
    norm1_g = 1.0 + nrm(ks[1], (DEPTH, D_MODEL), 0.05)
    w_in = nrm(ks[2], (DEPTH, D_MODEL, IN_TOTAL), D_MODEL ** -0.5)
    conv_a_w = nrm(ks[3], (DEPTH, K_SHORT, W_A), K_SHORT ** -0.5)
    conv_b_w = nrm(ks[4], (DEPTH, K_CONFORMER, W_B), K_CONFORMER ** -0.5)
    conv_b_b = nrm(ks[5], (DEPTH, W_B), 0.02)
    ln_b_g = 1.0 + nrm(ks[6], (DEPTH, W_B), 0.05)
    ln_b_b = nrm(ks[7], (DEPTH, W_B), 0.02)
    ln_c_g = 1.0 + nrm(ks[8], (DEPTH, W_C), 0.05)
    ln_c_b = nrm(ks[9], (DEPTH, W_C), 0.02)
    sgu_w = nrm(ks[10], (DEPTH, HEADS_C, CHUNK, CHUNK), 0.5 * CHUNK ** -0.5)
    sgu_b = 1.0 + nrm(ks[11], (DEPTH, HEADS_C, CHUNK), 0.1)
    w_out = nrm(ks[12], (DEPTH, D_MIX, D_MODEL), D_MIX ** -0.5)
    norm2_g = 1.0 + nrm(ks[13], (DEPTH, D_MODEL), 0.05)
    w_up = nrm(ks[14], (DEPTH, D_MODEL, 2 * D_FF), D_MODEL ** -0.5)
    conv_f_w = nrm(ks[15], (DEPTH, K_FFN, 2 * D_FF), K_FFN ** -0.5)
    w_down = nrm(ks[16], (DEPTH, D_FF, D_MODEL), D_FF ** -0.5)
    final_g = 1.0 + nrm(ks[17], (D_MODEL,), 0.05)
    return {"x": x, "norm1_g": norm1_g, "w_in": w_in, "conv_a_w": conv_a_w,
            "conv_b_w": conv_b_w, "conv_b_b": conv_b_b, "ln_b_g": ln_b_g, "ln_b_b": ln_b_b,
            "ln_c_g": ln_c_g, "ln_c_b": ln_c_b, "sgu_w": sgu_w, "sgu_b": sgu_b,
            "w_out": w_out, "norm2_g": norm2_g, "w_up": w_up, "conv_f_w": conv_f_w,
            "w_down": w_down, "final_g": final_g}


def reference(x, norm1_g, w_in, conv_a_w, conv_b_w, conv_b_b, ln_b_g, ln_b_b,
              ln_c_g, ln_c_b, sgu_w, sgu_b, w_out, norm2_g, w_up, conv_f_w,
              w_down, final_g):
    bsz, s_len, _ = x.shape
    n_chunks = s_len // CHUNK
    tri = jnp.tril(jnp.ones((CHUNK, CHUNK), dtype=x.dtype))
    for l in range(DEPTH):
        h = _rmsnorm(x, norm1_g[l])
        p = jnp.einsum("bsd,de->bse", h, w_in[l])
        p_a, p_b, p_c, p_d = jnp.split(p, [IN_A, IN_A + IN_B, IN_A + IN_B + IN_C], axis=-1)

        gate_b, gate_c, h_a = jnp.split(p_a, 3, axis=-1)
        y_a = gate_b * _causal_dwconv(gate_c * h_a, conv_a_w[l])

        val_b, gat_b = jnp.split(p_b, 2, axis=-1)
        glu = val_b * jax.nn.sigmoid(gat_b)
        cb = _causal_dwconv(glu, conv_b_w[l]) + conv_b_b[l]
        y_b = jax.nn.silu(_layernorm(cb, ln_b_g[l], ln_b_b[l]))

        uv = jax.nn.gelu(p_c, approximate=False)
        u, v_c = jnp.split(uv, 2, axis=-1)
        v_c = _layernorm(v_c, ln_c_g[l], ln_c_b[l])
        v_c = v_c.reshape(bsz, n_chunks, CHUNK, HEADS_C, HEAD_DIM)
        ws = sgu_w[l] * tri
        sp = jnp.einsum("gts,bnsgc->bntgc", ws, v_c) + sgu_b[l].T[None, None, :, :, None]
        y_c = u * sp.reshape(bsz, s_len, W_C)

        q, k, v = jnp.split(p_d, 3, axis=-1)
        to_heads = lambda t: t.reshape(bsz, s_len, HEADS_D, HEAD_DIM).transpose(0, 2, 1, 3)
        o_d = _stick_breaking(to_heads(q), to_heads(k), to_heads(v))
        y_d = o_d.transpose(0, 2, 1, 3).reshape(bsz, s_len, W_D)

        mix = jnp.concatenate([y_a, y_b, y_c, y_d], axis=-1)
        x = x + jnp.einsum("bse,ed->bsd", mix, w_out[l])

        h2 = _rmsnorm(x, norm2_g[l])
        up = _causal_dwconv(jnp.einsum("bsd,df->bsf", h2, w_up[l]), conv_f_w[l])
        g_f, v_f = jnp.split(up, 2, axis=-1)
        x = x + jnp.einsum("bsf,fd->bsd", jax.nn.silu(g_f) * v_f, w_down[l])
    return _rmsnorm(x, final_g)
```

```python
import numpy as np
from contextlib import ExitStack
import concourse.bass as bass
import concourse.mybir as mybir
from concourse.bass_utils import run_bass_kernel_spmd

F32 = mybir.dt.float32
BF16 = mybir.dt.bfloat16
AF = mybir.ActivationFunctionType
ALU = mybir.AluOpType

D = 1024
T = 512
DFF = 2816
NFB = 22
NCH = 51
NSLOT = 6
PRE_INTERLEAVE = False
CAST_MIXED = True
FFN_PIPE = True
RMS_EPS = 1e-6
LN_EPS = 1e-5
ENGS = ["pe", "act", "dve", "pool", "sp"]


class Op:
    __slots__ = ("eng", "fn", "deps", "sem", "semval", "idx", "signal", "cnt", "tag")

    def __init__(self, eng, fn, sem):
        self.eng = eng
        self.fn = fn
        self.sem = sem
        self.semval = 0
        self.signal = False
        self.cnt = 0
        self.idx = 0
        self.deps = ()


class Prog:
    def __init__(self):
        self.ops = {e: [] for e in ENGS}
        self.lastw = {}
        self.readers = {}
        self.semcnt = {}
        self.all = []

    def op(self, eng, fn, r=(), w=(), sem=None):
        o = Op(eng, fn, sem)
        o.tag = getattr(self, "curtag", "")
        deps = set()
        for k in r:
            lw = self.lastw.get(k)
            if lw is not None:
                deps.add(lw)
        for k in w:
            lw = self.lastw.get(k)
            if lw is not None:
                deps.add(lw)
            for rd in self.readers.get(k, ()):
                deps.add(rd)
        o.deps = deps
        for k in r:
            self.readers.setdefault(k, []).append(o)
        for k in w:
            self.lastw[k] = o
            self.readers[k] = []
        if sem is not None:
            self.semcnt[sem] = self.semcnt.get(sem, 0) + 16
            o.semval = self.semcnt[sem]
        o.idx = len(self.ops[eng])
        self.ops[eng].append(o)
        self.all.append(o)
        return o

    def emit(self, nc, block, esem, dsem):
        for o in self.all:
            latest = {}
            for d in o.deps:
                if d.sem is not None:
                    continue
                if d.eng == o.eng:
                    if o.eng in ("pe", "sp"):
                        continue
                    if o.idx - d.idx > 2:
                        continue
                cur = latest.get(d.eng)
                if cur is None or d.idx > cur.idx:
                    latest[d.eng] = d
            o.deps = set(d for d in o.deps if d.sem is not None) | set(latest.values())
            for d in latest.values():
                d.signal = True
        for e in ENGS:
            c = 0
            for o in self.ops[e]:
                if o.signal and o.sem is None:
                    c += 1
                o.cnt = c

        def body(ename):
            def f(eng):
                known = {}
                for o in self.ops[ename]:
                    waits = {}
                    for d in o.deps:
                        if d.sem is not None:
                            key, val = ("d", d.sem), d.semval
                        else:
                            if d.eng == o.eng:
                                if o.eng in ("pe", "sp"):
                                    continue
                                if o.idx - d.idx > 2:
                                    continue
                            key, val = ("e", d.eng), d.cnt
                        if known.get(key, 0) >= val:
                            continue
                        if waits.get(key, 0) < val:
                            waits[key] = val
                    if waits and getattr(self, "dbgwaits", None) is not None:
                        self.dbgwaits.append((ename, o.idx, o.tag, dict(waits),
                                              [(d.eng, d.idx, d.tag, d.sem) for d in o.deps]))
                    for key, val in waits.items():
                        s = dsem[key[1]] if key[0] == "d" else esem[key[1]]
                        eng.wait_ge(s, val)
                        known[key] = val
                    ins = o.fn(eng)
                    if o.sem is not None:
                        ins.then_inc(dsem[o.sem], 16)
                    elif o.signal:
                        ins.then_inc(esem[ename], 1)
            return f

        block.tensor(body("pe"))
        block.scalar(body("act"))
        block.vector(body("dve"))
        block.gpsimd(body("pool"))
        block.sync(body("sp"))


def build(NB, S, DBG=False):
    NT = S // T
    nc = bass.Bass("TRN2", target_bir_lowering=False)

    def din(name, shape):
        return nc.dram_tensor(name, list(shape), F32, kind="ExternalInput").ap()

    x_d = din("x", (NB, S, D))
    norm1_g = din("norm1_g", (2, D))
    w_in = din("w_in", (2, D, 2560))
    conv_a_w = din("conv_a_w", (2, 3, 256))
    conv_b_w = din("conv_b_w", (2, 31, 256))
    conv_b_b = din("conv_b_b", (2, 256))
    ln_b_g = din("ln_b_g", (2, 256))
    ln_b_b = din("ln_b_b", (2, 256))
    ln_c_g = din("ln_c_g", (2, 256))
    ln_c_b = din("ln_c_b", (2, 256))
    sgu_w = din("sgu_w", (2, 4, 128, 128))
    sgu_b = din("sgu_b", (2, 4, 128))
    w_out = din("w_out", (2, D, D))
    norm2_g = din("norm2_g", (2, D))
    w_up = din("w_up", (2, D, 2 * DFF))
    conv_f_w = din("conv_f_w", (2, 3, 2 * DFF))
    w_down = din("w_down", (2, DFF, D))
    final_g = din("final_g", (D,))
    y_d = nc.dram_tensor("y", [NB, S, D], F32, kind="ExternalOutput").ap()
    wsc = nc.dram_tensor("wsc", [2 * NCH, 128, 2048], BF16, kind="Internal").ap()
    dbg = {}
    if DBG:
        dbg["hT"] = nc.dram_tensor("d_hT", [128, 8, T], BF16, kind="ExternalOutput").ap()
        dbg["mixT"] = nc.dram_tensor("d_mixT", [128, 8, T], BF16, kind="ExternalOutput").ap()
        dbg["x1"] = nc.dram_tensor("d_x1", [128, 4, D], F32, kind="ExternalOutput").ap()
        dbg["actT"] = nc.dram_tensor("d_actT", [128, NFB, T], BF16, kind="ExternalOutput").ap()
        dbg["x2"] = nc.dram_tensor("d_x2", [128, 4, D], F32, kind="ExternalOutput").ap()

    P = Prog()
    es = ExitStack()
    with es:
        def sb(name, shape, dt=F32):
            return es.enter_context(nc.sbuf_tensor(name, list(shape), dt))

        xb = [sb(f"xb{k}", (128, 4, D)) for k in range(2)]
        hn = [sb(f"hn{k}", (128, D), BF16) for k in range(2)]
        hT = sb("hT", (128, 8, T), BF16)
        mixT = sb("mixT", (128, 8, T), BF16)
        actT = sb("actT", (128, NFB, T), BF16)
        kT = [sb(f"kT{l}", (128, 2, S), BF16) for l in range(2)]
        vS = [sb(f"vS{l}", (128, S // 128, 256), BF16) for l in range(2)]
        qTp = sb("qTp", (128, 4, T), BF16)
        wsl = [sb(f"wsl{k}", (128, 2048), BF16) for k in range(NSLOT)]
        st32 = [sb(f"st32_{k}", (128, 2048)) for k in range(2)]
        st16 = [sb(f"st16_{k}", (128, 2048), BF16) for k in range(2)]
        Fb = [sb(f"F{k}", (128, 516)) for k in range(12)]
        Hb = [sb(f"H{k}", (128, T), BF16) for k in range(4)]
        vnp = [sb(f"vnp{k}", (128, 4, 128), BF16) for k in range(4)]
        ssq = sb("ssq", (128, 4))
        rstd = sb("rstd", (128, 4))
        stat = sb("stat", (128, 4, 8))
        epsr = sb("epsr", (128, 1))
        epsl = sb("epsl", (128, 1))
        one1 = sb("one1", (128, 1))
        ones16 = sb("ones16", (128, 128), BF16)
        avg16 = sb("avg16", (128, 128), BF16)
        ident = sb("ident", (128, 128), BF16)
        ntri = sb("ntri", (128, 128), BF16)
        mask01 = sb("mask01", (128, 128), BF16)
        maskge = sb("maskge", (128, 128), BF16)
        sel = sb("sel", (1, 2, 128), BF16)
        gT1 = [sb(f"gT1_{l}", (128, 8)) for l in range(2)]
        gT2 = [sb(f"gT2_{l}", (128, 8)) for l in range(2)]
        caw = [sb(f"caw{l}", (128, 2, 3)) for l in range(2)]
        cbw = [sb(f"cbw{l}", (128, 2, 31)) for l in range(2)]
        cbb = [sb(f"cbb{l}", (128, 2)) for l in range(2)]
        lbg = [sb(f"lbg{l}", (128, 2)) for l in range(2)]
        lbb = [sb(f"lbb{l}", (128, 2)) for l in range(2)]
        lcg = [sb(f"lcg{l}", (128, 256)) for l in range(2)]
        lcb = [sb(f"lcb{l}", (128, 256)) for l in range(2)]
        cfw = [sb(f"cfw{l}", (128, 2 * NFB, 3)) for l in range(2)]
        WsT = [sb(f"WsT{l}", (128, 4, 128), BF16) for l in range(2)]
        sgub = [sb(f"sgub{l}", (1, 4, 128), BF16) for l in range(2)]
        caH = [sb(f"caH{l}", (128, 2, 2)) for l in range(2)]
        gluH = [sb(f"gluH{l}", (128, 2, 30), BF16) for l in range(2)]
        G16 = [sb(f"G16_{k}", (128, 544), BF16) for k in range(2)]
        G16o = [sb(f"G16o_{k}", (128, 544), BF16) for k in range(2)]
        uH = [sb(f"uH{l}", (128, NFB, 2, 2)) for l in range(2)]
        B = [es.enter_context(nc.psum_tensor(f"B{k}", [128, 512], F32)) for k in range(8)]

        esem = {e: es.enter_context(nc.semaphore(f"se_{e}")) for e in ENGS}
        dnames = ([f"w{k}" for k in range(NSLOT)] + ["ld0", "ld1", "so0", "so1", "xl0", "xl1", "ys0", "ys1", "par", "sg0", "sg1", "sg2", "sg3", "sgb", "dbg"])
        dsem = {n: es.enter_context(nc.semaphore(f"sd_{n}")) for n in dnames}

        def ACT(fn, r, w):
            return P.op("act", fn, r, w)

        def DVE(fn, r, w):
            return P.op("dve", fn, r, w)

        def POOL(fn, r, w):
            return P.op("pool", fn, r, w)

        def PE(fn, r, w):
            return P.op("pe", fn, r, w)

        def act_fn(out, in_, func, **kw):
            return lambda e: e.activation(out=out, in_=in_, func=func, **kw)

        def mm(out, lhsT, rhs, start, stop):
            return lambda e: e.matmul(out, lhsT=lhsT, rhs=rhs, start=start, stop=stop)

        POOL(lambda e: e.memset(ones16[:], 1.0), [], ["ones16"])
        POOL(lambda e: e.memset(avg16[:], 1.0 / 256.0), [], ["avg16"])
        POOL(lambda e: e.memset(epsr[:], RMS_EPS), [], ["epsr"])
        POOL(lambda e: e.memset(epsl[:], LN_EPS), [], ["epsl"])
        POOL(lambda e: e.memset(one1[:], 1.0), [], ["one1"])
        POOL(lambda e: e.affine_select(out=ident[:], in_=ones16[:], pattern=[[-1, 128]], compare_op=ALU.is_equal,
                                       fill=0.0, base=0, channel_multiplier=1), ["ones16"], ["ident"])
        POOL(lambda e: e.memset(ntri[:], -1.0), [], ["ntri"])
        POOL(lambda e: e.affine_select(out=ntri[:], in_=ntri[:], pattern=[[-1, 128]], compare_op=ALU.is_ge,
                                       fill=0.0, base=0, channel_multiplier=1), ["ntri"], ["ntri"])
        POOL(lambda e: e.affine_select(out=mask01[:], in_=ones16[:], pattern=[[1, 128]], compare_op=ALU.is_gt,
                                       fill=0.0, base=0, channel_multiplier=-1), ["ones16"], ["mask01"])
        POOL(lambda e: e.affine_select(out=maskge[:], in_=ones16[:], pattern=[[1, 128]], compare_op=ALU.is_ge,
                                       fill=0.0, base=0, channel_multiplier=-1), ["ones16"], ["maskge"])
        POOL(lambda e: e.memset(sel[:], 0.0), [], ["sel"])
        POOL(lambda e: e.memset(sel[:, 0, 0:64], 1.0), ["sel"], ["sel"])
        POOL(lambda e: e.memset(sel[:, 1, 64:128], 1.0), ["sel"], ["sel"])
        POOL(lambda e: e.memset(qTp[:], 0.0), [], ["qTp0", "qTp1", "qTp2", "qTp3"])
        for j in range(4):
            POOL(lambda e, j=j: e.memset(vnp[j][:], 0.0), [], [f"vnp{j}"])

        def pload(out, in_, w):
            P.op("act", lambda e: e.dma_start(out=out, in_=in_), [], w, sem=None)

        par_ops = []

        def par(out, in_):
            par_ops.append((out, in_))

        nc_ctx = nc.allow_non_contiguous_dma(reason="small one-time parameter loads")
        es.enter_context(nc_ctx)
        for l in range(2):
            par(gT1[l][:], norm1_g[l].rearrange("(c p) -> p c", p=128))
            par(gT2[l][:], norm2_g[l].rearrange("(c p) -> p c", p=128))
            for k in range(3):
                par(caw[l][:, :, k], conv_a_w[l, k].rearrange("(c p) -> p c", p=128))
            for k in range(31):
                par(cbw[l][:, :, k], conv_b_w[l, k].rearrange("(c p) -> p c", p=128))
            par(cbb[l][:], conv_b_b[l].rearrange("(c p) -> p c", p=128))
            par(lbg[l][:], ln_b_g[l].rearrange("(c p) -> p c", p=128))
            par(lbb[l][:], ln_b_b[l].rearrange("(c p) -> p c", p=128))
            par(lcg[l][:], ln_c_g[l].partition_broadcast(128))
            par(lcb[l][:], ln_c_b[l].partition_broadcast(128))
            for k in range(3):
                for q in range(4):
                    par(cfw[l][:, q * 11:(q + 1) * 11, k],
                        conv_f_w[l, k, q * 11 * 128:(q + 1) * 11 * 128].rearrange("(c p) -> p c", p=128))
        npar = len(par_ops)
        par_recs = []
        for (o_, i_) in par_ops:
            par_recs.append(P.op("act", lambda e, o_=o_, i_=i_: e.dma_start(out=o_, in_=i_), [], [], sem="par"))
        last = par_recs[-1]
        for k in ["gT1", "gT2", "caw", "cbw", "cbb", "lbg", "lbb", "lcg", "lcb", "cfw"]:
            P.lastw[k] = last
            P.readers[k] = []

        def DMA(q, out, in_, r, w, sem):
            return P.op(q, lambda e: e.dma_start(out=out, in_=in_), r, w, sem=sem)

        for l in range(2):
            for g in range(4):
                DMA("pool", Fb[g][:, 0:128], sgu_w[l, g], [], [f"F{g}"], f"sg{g}")
            for g in range(4):
                ACT(act_fn(Hb[g][:, 0:128], Fb[g][:, 0:128], AF.Copy), [f"F{g}"], [f"H{g}"])
                PE(lambda e, g=g: e.transpose(B[g][:].bitcast(BF16)[:, 0:128], Hb[g][:, 0:128], ident[:]),
                   [f"H{g}", "ident"], [f"B{g}"])
                DVE(lambda e, l=l, g=g: e.tensor_tensor(out=WsT[l][:, g, :], in0=B[g][:].bitcast(BF16)[:, 0:128],
                                                        in1=maskge[:], op=ALU.mult),
                    [f"B{g}", "maskge"], [f"WsT{l}"])
            DMA("pool", sgub[l][:], sgu_b[l:l + 1], [], [f"sgub{l}"], "sgb")

        def chunk_src(l, k):
            if k < 10:
                src = w_in[l][:, k * 256:(k + 1) * 256].rearrange("(dc p) e -> p dc e", p=128)
                return [(lambda t: t[:].rearrange("p (dc e) -> p dc e", dc=8), src)], gT1[l]
            if k < 14:
                c = k - 10
                src = w_out[l][c * 256:(c + 1) * 256, :].rearrange("(b p) d -> p b d", p=128)
                return [(lambda t: t[:].rearrange("p (b d) -> p b d", b=2), src)], None
            if k < 36:
                fb = k - 14
                s1 = w_up[l][:, fb * 128:(fb + 1) * 128].rearrange("(dc p) e -> p dc e", p=128)
                s2 = w_up[l][:, DFF + fb * 128:DFF + (fb + 1) * 128].rearrange("(dc p) e -> p dc e", p=128)
                return [(lambda t: t[:].rearrange("p (dc e) -> p dc e", dc=8)[:, :, 0:128], s1),
                        (lambda t: t[:].rearrange("p (dc e) -> p dc e", dc=8)[:, :, 128:256], s2)], gT2[l]
            if k >= 47:
                return [], "diag"
            c = k - 36
            src = w_down[l][c * 256:(c + 1) * 256, :].rearrange("(b p) d -> p b d", p=128)
            return [(lambda t: t[:].rearrange("p (b d) -> p b d", b=2), src)], None

        ORDER = [7, 8, 9, 5, 6, 3, 4, 0, 1, 2, 47, 48, 49, 50] + list(range(10, 47))
        PRE = 8

        def prepass_load(g):
            l, k = g // NCH, ORDER[g % NCH]
            s = g % 2
            parts, gain = chunk_src(l, k)
            for pi, (dv, src) in enumerate(parts):
                wn_ = [(f"st32_{s}", pi)] if len(parts) == 2 else [(f"st32_{s}", 0), (f"st32_{s}", 1)]
                DMA("pool", dv(st32[s]), src, [], wn_, f"ld{s}")

        def prepass(g, do_load=True):
            l, k = g // NCH, ORDER[g % NCH]
            s = g % 2
            parts, gain = chunk_src(l, k)
            if do_load:
                prepass_load(g)
            if gain == "diag":
                c = k - 47
                for mm_ in range(16):
                    m = c * 16 + mm_
                    if m >= 62:
                        break
                    cb_, kk = divmod(m, 31)
                    DVE(lambda e, s=s, mm_=mm_, cb_=cb_, kk=kk: e.tensor_scalar(
                        out=st16[s][:, mm_ * 128:(mm_ + 1) * 128], in0=ident[:], scalar1=cbw[l][:, cb_, kk:kk + 1],
                        scalar2=None, op0=ALU.mult), ["ident", "cbw"], [(f"st16_{s}", mm_ // 2)])
                DMA("pool", wsc[l * NCH + k], st16[s][:], [(f"st16_{s}", dc) for dc in range(8)], [("wsc", l, k)], f"so{s}")
                return
            rd = [(f"st32_{s}", 0), (f"st32_{s}", 1)]
            on_act = (g % 2 == 1)
            CENG = DVE if CAST_MIXED else POOL
            if not CAST_MIXED:
                on_act = False
            allw = [(f"st16_{s}", dc) for dc in range(8)]
            if gain is None:
                if on_act:
                    ACT(act_fn(st16[s][:], st32[s][:], AF.Copy), rd, allw)
                else:
                    CENG(lambda e, s=s: e.tensor_copy(out=st16[s][:], in_=st32[s][:]), rd, allw)
            else:
                gname = "gT1" if k < 10 else "gT2"
                for dc in range(8):
                    o_ = st16[s][:, dc * 256:(dc + 1) * 256]
                    i_ = st32[s][:, dc * 256:(dc + 1) * 256]
                    if on_act:
                        ACT(act_fn(o_, i_, AF.Copy, scale=gain[:, dc:dc + 1]), rd + [gname], [(f"st16_{s}", dc)])
                    else:
                        CENG(lambda e, o_=o_, i_=i_, gain=gain, dc=dc: e.tensor_scalar(
                            out=o_, in0=i_, scalar1=gain[:, dc:dc + 1], scalar2=None, op0=ALU.mult),
                            rd + [gname], [(f"st16_{s}", dc)])
            DMA("pool", wsc[l * NCH + k], st16[s][:], allw, [("wsc", l, k)], f"so{s}")

        if PRE_INTERLEAVE:
            for g in range(PRE):
                prepass(g)
        else:
            prepass_load(0)
            prepass_load(1)
            for g in range(2 * NCH):
                prepass(g, do_load=False)
                if g + 2 < 2 * NCH:
                    prepass_load(g + 2)

        aE = st32[0][:, 0:512]
        aL = [st32[0][:, 512:1024], st32[0][:, 1024:1536]]
        aSP = [st16[0][:, 0:512], st16[0][:, 512:1024]]
        aA = [st16[0][:, 1024:1536], st16[0][:, 1536:2048]]
        uS = [st16[1][:, 0:512], st16[1][:, 512:1024]]
        carry = [st32[1][:, 0:512], st32[1][:, 512:1024]]
        fence32b = P.lastw.get(("st16_1", 0))
        assert fence32b is not None
        for nm_ in ["carry0", "carry1"]:
            P.lastw[nm_] = fence32b
        fgb = st32[1][:, 1024:2048]
        P.lastw["fgb"] = fence32b
        DMA("sp", fgb, final_g.partition_broadcast(128), [], ["fgb"], "sgb")
        fence32 = P.lastw.get(("st16_0", 0))
        fence16 = P.lastw.get(("wsc", 1, ORDER[(2 * NCH - 2) % NCH]))
        fence16b = P.lastw.get(("wsc", 1, ORDER[(2 * NCH - 1) % NCH]))
        assert fence32 is not None and fence16 is not None and fence16b is not None
        for nm_ in ["aE", "aL0", "aL1"]:
            P.lastw[nm_] = fence32
        for nm_ in ["aSP0", "aSP1", "aA0", "aA1"]:
            P.lastw[nm_] = fence16
        for nm_ in ["uS0", "uS1"]:
            P.lastw[nm_] = fence16b

        wstate = {"n": 0}

        def wfetch(l, k):
            sl = wstate["n"] % NSLOT
            wstate["n"] += 1
            DMA("sp", wsl[sl][:], wsc[l * NCH + k], [("wsc", l, k)], [f"wsl{sl}"], f"w{sl}")
            return sl

        stream = []
        fetched = {"i": 0}
        slot_of = {}

        def prefetch_upto(pos):
            while fetched["i"] <= pos and fetched["i"] < len(stream):
                l_, k_ = stream[fetched["i"]]
                slot_of[fetched["i"]] = wfetch(l_, k_)
                fetched["i"] += 1

        upos = {"i": 0}

        def next_chunk(prefetch=True):
            pos = upos["i"]
            upos["i"] += 1
            if PRE_INTERLEAVE and pos + PRE < 2 * NCH:
                prepass(pos + PRE)
            if prefetch:
                prefetch_upto(pos + NSLOT - 1)
            sl = slot_of[pos]
            return wsl[sl], f"wsl{sl}"

        def catch_up():
            prefetch_upto(upos["i"] - 1 + NSLOT - 1)

        for b in range(NB):
            for i in range(NT):
                for l in range(2):
                    for k in ORDER:
                        stream.append((l, k))

        def rms_stats(xt, xname):
            DVE(lambda e: e.memset(ssq[:], 0.0), [], [("ssq", j) for j in range(4)])

            def lnexp(j):
                ACT(act_fn(rstd[:, j:j + 1], ssq[:, j:j + 1], AF.Ln, bias=epsr[:, 0:1], scale=1.0 / D),
                    [("ssq", j), "epsr"], [("rstd", j)])
                ACT(act_fn(rstd[:, j:j + 1], rstd[:, j:j + 1], AF.Exp, scale=-0.5), [("rstd", j)], [("rstd", j)])

            for j in range(4):
                ACT(act_fn(Fb[11][:].bitcast(BF16)[:, 0:D], xt[:, j, :], AF.Square, accum_out=ssq[:, j:j + 1]),
                    [(xname, j)], [("ssq", j)])
                if j >= 1:
                    lnexp(j - 1)
            lnexp(3)

        def norm_T(xt, xname):
            rms_stats(xt, xname)
            for j in range(4):
                hb = hn[j % 2]
                hname = f"hn{j % 2}"
                DVE(lambda e, j=j, hb=hb: e.tensor_scalar(out=hb[:], in0=xt[:, j, :], scalar1=rstd[:, j:j + 1],
                                                          scalar2=None, op0=ALU.mult), [(xname, j), ("rstd", j)], [hname])
                bk = j % 2
                for c in range(8):
                    PE(lambda e, c=c, hb=hb, bk=bk: e.transpose(B[bk][:].bitcast(BF16)[:, c * 128:(c + 1) * 128],
                                                                hb[:, c * 128:(c + 1) * 128], ident[:]),
                       [hname, "ident"], [f"B{bk}"])
                ev = ACT if j % 2 == 0 else DVE
                if j % 2 == 0:
                    ACT(act_fn(hT[:, :, j * 128:(j + 1) * 128],
                               B[bk][:].bitcast(BF16).rearrange("p (c t) -> p c t", c=8), AF.Copy),
                        [f"B{bk}"], ["hT"])
                else:
                    DVE(lambda e, j=j, bk=bk: e.tensor_copy(out=hT[:, :, j * 128:(j + 1) * 128],
                                                            in_=B[bk][:].bitcast(BF16).rearrange("p (c t) -> p c t", c=8)),
                        [f"B{bk}"], ["hT"])

        def fm_block(wt, wname, col0, bank):
            wv = wt[:].rearrange("p (dc e) -> p dc e", dc=8)
            for dc in range(8):
                PE(mm(B[bank][:], wv[:, dc, col0:col0 + 128], hT[:, dc, :], dc == 0, dc == 7),
                   [wname, "hT"], [f"B{bank}"])

        def tm_block(wt, wname, j, bank, half):
            wv = wt[:].rearrange("p (dc e) -> p dc e", dc=8)
            for dc in range(8):
                PE(mm(B[bank][:, half * 256:(half + 1) * 256], hT[:, dc, j * 128:(j + 1) * 128], wv[:, dc, :],
                      dc == 0, dc == 7), [wname, "hT"], [f"B{bank}"])

        def dwconv(eng, buf, bname, acc, aname, wt, wname, widx, K, first_bias=None):
            wsel = lambda k: wt[:, widx, k:k + 1]
            if first_bias is None:
                eng(lambda e: e.tensor_scalar(out=acc[:, 0:T], in0=buf[:, K - 1:K - 1 + T], scalar1=wsel(K - 1),
                                              scalar2=None, op0=ALU.mult), [bname, wname], [aname])
            else:
                eng(lambda e: e.tensor_scalar(out=acc[:, 0:T], in0=buf[:, K - 1:K - 1 + T], scalar1=wsel(K - 1),
                                              scalar2=first_bias, op0=ALU.mult, op1=ALU.add),
                    [bname, wname, "cbb"], [aname])
            for k in range(K - 2, -1, -1):
                eng(lambda e, k=k: e.scalar_tensor_tensor(out=acc[:, 0:T], in0=buf[:, k:k + T], scalar=wsel(k),
                                                          in1=acc[:, 0:T], op0=ALU.mult, op1=ALU.add),
                    [bname, wname, aname], [aname])

        def halo_in(eng, buf, bname, hal, hname, H, first):
            if first:
                eng(lambda e: e.memset(buf[:, 0:H], 0.0), [], [bname])
            else:
                eng(lambda e: e.tensor_copy(out=buf[:, 0:H], in_=hal), [hname], [bname])

        def halo_out(eng, buf, bname, hal, hname, H):
            eng(lambda e: e.tensor_copy(out=hal, in_=buf[:, T:T + H]), [bname], [hname])

        def proj_residual(xt, xname, nchunks, lhs_of, lres_of):
            nkb = 2 * nchunks
            for c in range(nchunks - 2):
                wt, wn = next_chunk()
                wv = wt[:].rearrange("p (b d) -> p b d", b=2)
                for kbl in range(2):
                    kb = 2 * c + kbl
                    for j in range(4):
                        for dh in range(2):
                            bk = j * 2 + dh
                            PE(mm(B[bk][:], lhs_of(kb, j), wv[:, kbl, dh * 512:(dh + 1) * 512], kb == 0, False),
                               [lres_of(kb), wn], [f"B{bk}"])
            wA = next_chunk()
            wB = next_chunk(prefetch=False)
            for j in range(4):
                for dh in range(2):
                    bk = j * 2 + dh
                    for ci, (wt, wn) in enumerate((wA, wB)):
                        wv = wt[:].rearrange("p (b d) -> p b d", b=2)
                        for kbl in range(2):
                            kb = 2 * (nchunks - 2 + ci) + kbl
                            PE(mm(B[bk][:], lhs_of(kb, j), wv[:, kbl, dh * 512:(dh + 1) * 512], kb == 0, kb == nkb - 1),
                               [lres_of(kb), wn], [f"B{bk}"])
                    DVE(lambda e, j=j, dh=dh, bk=bk: e.tensor_tensor(out=xt[:, j, dh * 512:(dh + 1) * 512],
                                                                     in0=xt[:, j, dh * 512:(dh + 1) * 512],
                                                                     in1=B[bk][:], op=ALU.add),
                        [(xname, j), f"B{bk}"], [(xname, j)])
            catch_up()

        def layer(xt, xname, l, i):
            first = (i == 0)
            norm_T(xt, xname)
            nb = {"n": 0}

            def nbank():
                b_ = 2 + nb["n"] % 4
                nb["n"] += 1
                return b_

            wt, wn = next_chunk()
            for hp in range(2):
                bk = nbank()
                fm_block(wt, wn, hp * 128, bk)
                for gg in range(2):
                    h = 2 * hp + gg
                    DVE(lambda e, gg=gg, h=h, bk=bk: e.tensor_scalar(
                        out=qTp[gg * 64:(gg + 1) * 64, h, :], in0=B[bk][gg * 64:(gg + 1) * 64, :], scalar1=0.125,
                        scalar2=None, op0=ALU.mult), [f"B{bk}"], [f"qTp{h}"])
            wt, wn = next_chunk()
            for hp in range(2):
                bk = nbank()
                fm_block(wt, wn, hp * 128, bk)
                DVE(lambda e, hp=hp, bk=bk: e.tensor_copy(out=kT[l][:, hp, i * T:(i + 1) * T], in_=B[bk][:]),
                    [f"B{bk}"], [f"kT{l}"])
            wt, wn = next_chunk()
            for j in range(4):
                bk, half = 6 + j // 2, j % 2
                tm_block(wt, wn, j, bk, half)
                DVE(lambda e, j=j, bk=bk, half=half: e.tensor_copy(out=vS[l][:, i * 4 + j, :],
                                                                   in_=B[bk][:, half * 256:(half + 1) * 256]),
                    [f"B{bk}"], [f"vS{l}"])
            wt, wn = next_chunk()
            for hp in range(2):
                bk = nbank()
                fm_block(wt, wn, hp * 128, bk)
                ACT(act_fn(uS[hp], B[bk][:], AF.Gelu), [f"B{bk}"], [f"uS{hp}"])
            wt, wn = next_chunk()
            for j in range(4):
                bk, half = 4 + j // 2, j % 2
                tm_block(wt, wn, j, bk, half)
                vg = Fb[j]
                vn_ = f"F{j}"
                ACT(act_fn(vg[:, 0:256], B[bk][:, half * 256:(half + 1) * 256], AF.Gelu), [f"B{bk}"], [vn_])
            for j in range(4):
                vg = Fb[j]
                vn_ = f"F{j}"
                DVE(lambda e, j=j, vg=vg: e.bn_stats(out=stat[:, j, 0:6], in_=vg[:, 0:256]), [vn_], [("stat", j)])
                DVE(lambda e, j=j: e.bn_aggr(out=stat[:, j, 6:8], in_=stat[:, j, 0:6]), [("stat", j)], [("stat", j)])
            for j in range(4):
                ACT(act_fn(stat[:, j, 7:8], stat[:, j, 7:8], AF.Ln, bias=epsl[:, 0:1]), [("stat", j), "epsl"], [("stat", j)])
            for j in range(4):
                ACT(act_fn(stat[:, j, 7:8], stat[:, j, 7:8], AF.Exp, scale=-0.5), [("stat", j)], [("stat", j)])

            def gen_mix():
                for j in range(4):
                    vg = Fb[j]
                    vn_ = f"F{j}"
                    DVE(lambda e, j=j, vg=vg: e.tensor_scalar(out=vg[:, 0:256], in0=vg[:, 0:256], scalar1=stat[:, j, 6:7],
                                                              scalar2=stat[:, j, 7:8], op0=ALU.subtract, op1=ALU.mult),
                        [vn_, ("stat", j)], [vn_])
                    POOL(lambda e, vg=vg: e.tensor_tensor(out=vg[:, 0:256], in0=vg[:, 0:256], in1=lcg[l][:], op=ALU.mult),
                         [vn_, "lcg"], [vn_])
                    for gg in range(2):
                        POOL(lambda e, j=j, vg=vg, gg=gg: e.tensor_tensor(
                            out=vnp[j][:, gg:4:2, gg * 64:(gg + 1) * 64],
                            in0=vg[:, 0:256].rearrange("p (g c) -> p g c", g=4)[:, gg:4:2, :],
                            in1=lcb[l][:].rearrange("p (g c) -> p g c", g=4)[:, gg:4:2, :], op=ALU.add),
                            [vn_, "lcb"], [f"vnp{j}"])
                    yield
                for hp in range(2):
                    for j in range(4):
                        for gg in range(2):
                            g = 2 * hp + gg
                            PE(mm(B[hp][:, j * 128:(j + 1) * 128], vnp[j][:, g, :], WsT[l][:, g, :], gg == 0, False),
                               [f"vnp{j}", f"WsT{l}"], [f"B{hp}"])
                        for gg in range(2):
                            g = 2 * hp + gg
                            PE(mm(B[hp][:, j * 128:(j + 1) * 128], sel[:, gg, :], sgub[l][:, g, :], False, gg == 1),
                               ["sel", f"sgub{l}"], [f"B{hp}"])
                    DVE(lambda e, hp=hp: e.tensor_tensor(out=mixT[:, 4 + hp, :], in0=uS[hp], in1=B[hp][:], op=ALU.mult),
                        [f"uS{hp}", f"B{hp}"], [("mixT", 4 + hp)])
                    yield
                mb = {"n": 0}

                def mbank():
                    b_ = mb["n"] % 2
                    mb["n"] += 1
                    return b_

                wt, wn = next_chunk()
                for cb in range(2):
                    bk = mbank()
                    fm_block(wt, wn, cb * 128, bk)
                    DVE(lambda e, cb=cb, bk=bk: e.tensor_copy(out=Fb[cb][:, 0:T], in_=B[bk][:]), [f"B{bk}"], [f"F{cb}"])
                    yield
                wt, wn = next_chunk()
                for cb in range(2):
                    bk = mbank()
                    fm_block(wt, wn, cb * 128, bk)
                    sg = Fb[2 + cb]
                    sn = f"F{2 + cb}"
                    ACT(act_fn(sg[:, 0:T], B[bk][:], AF.Exp, scale=-1.0), [f"B{bk}"], [sn])
                    DVE(lambda e, sg=sg: e.tensor_scalar(out=sg[:, 0:T], in0=sg[:, 0:T], scalar1=1.0, scalar2=None,
                                                         op0=ALU.add), [sn], [sn])
                    DVE(lambda e, sg=sg: e.reciprocal(out=sg[:, 0:T], in_=sg[:, 0:T]), [sn], [sn])
                    gl = G16[cb]
                    gn = f"G16_{cb}"
                    halo_in(POOL, gl, gn, gluH[l][:, cb, :], f"gluH{l}", 30, first)
                    POOL(lambda e, cb=cb, gl=gl: e.tensor_tensor(out=gl[:, 30:30 + T], in0=Fb[cb][:, 0:T],
                                                                 in1=Fb[2 + cb][:, 0:T], op=ALU.mult),
                         [f"F{cb}", sn], [gn])
                    halo_out(POOL, gl, gn, gluH[l][:, cb, :], f"gluH{l}", 30)
                    POOL(lambda e, cb=cb, gl=gl: e.tensor_copy(out=G16o[cb][:, 0:541], in_=gl[:, 1:542]),
                         [gn], [f"G16o_{cb}"])
                    yield
                wt, wn = next_chunk()
                for cb in range(2):
                    bk = mbank()
                    fm_block(wt, wn, cb * 128, bk)
                    DVE(lambda e, cb=cb, bk=bk: e.tensor_copy(out=Fb[cb][:, 0:T], in_=B[bk][:]), [f"B{bk}"], [f"F{cb}"])
                    yield
                wt, wn = next_chunk()
                for cb in range(2):
                    bk = mbank()
                    fm_block(wt, wn, cb * 128, bk)
                    DVE(lambda e, cb=cb, bk=bk: e.tensor_copy(out=Fb[2 + cb][:, 0:T], in_=B[bk][:]), [f"B{bk}"], [f"F{2 + cb}"])
                    yield
                wt, wn = next_chunk()
                for cb in range(2):
                    bk = mbank()
                    fm_block(wt, wn, cb * 128, bk)
                    ca = Fb[4 + cb]
                    cn = f"F{4 + cb}"
                    halo_in(POOL, ca, cn, caH[l][:, cb, :], f"caH{l}", 2, first)
                    DVE(lambda e, cb=cb, bk=bk, ca=ca: e.tensor_tensor(out=ca[:, 2:2 + T], in0=Fb[2 + cb][:, 0:T],
                                                                        in1=B[bk][:], op=ALU.mult),
                        [f"F{2 + cb}", f"B{bk}"], [cn])
                    halo_out(POOL, ca, cn, caH[l][:, cb, :], f"caH{l}", 2)
                    yield
                    dwconv(DVE, ca, cn, Fb[6 + cb], f"F{6 + cb}", caw[l], "caw", cb, 3)
                    POOL(lambda e, cb=cb: e.tensor_tensor(out=mixT[:, cb, :], in0=Fb[cb][:, 0:T], in1=Fb[6 + cb][:, 0:T],
                                                          op=ALU.mult), [f"F{cb}", f"F{6 + cb}"], [("mixT", cb)])
                    yield
                dwt = None
                for m in range(62):
                    cb, k = divmod(m, 31)
                    if m % 16 == 0:
                        dwt = next_chunk()
                    gl = G16[cb]
                    gn = f"G16_{cb}"
                    rhs_ = gl[:, k:k + T] if k % 2 == 0 else G16o[cb][:, k - 1:k - 1 + T]
                    PE(mm(B[cb][:], dwt[0][:, (m % 16) * 128:(m % 16 + 1) * 128], rhs_, k == 0, k == 30),
                       [gn, f"G16o_{cb}", dwt[1]], [f"B{cb}"])
                    if k % 8 == 7:
                        yield
                    if k == 30:
                        acc, an = Fb[6 + cb], f"F{6 + cb}"
                        ACT(act_fn(Hb[2 + cb][:], B[cb][:], AF.Square, bias=cbb[l][:, cb:cb + 1]), [f"B{cb}", "cbb"],
                            [f"H{2 + cb}", f"B{cb}"])
                        DVE(lambda e, cb=cb, acc=acc: e.tensor_scalar(out=acc[:, 0:T], in0=B[cb][:], scalar1=cbb[l][:, cb:cb + 1],
                                                                      scalar2=None, op0=ALU.add), [f"B{cb}", "cbb"], [an, f"B{cb}"])
                        DVE(lambda e, cb=cb, acc=acc: e.tensor_copy(out=Hb[cb][:], in_=acc[:, 0:T]), [an], [f"H{cb}"])
                        yield
                for cb in range(2):
                    PE(mm(B[0][:], avg16[:], Hb[cb][:], cb == 0, cb == 1), [f"H{cb}", "avg16"], ["B0"])
                for cb in range(2):
                    PE(mm(B[1][:], avg16[:], Hb[2 + cb][:], cb == 0, cb == 1), [f"H{2 + cb}", "avg16"], ["B1"])
                ACT(act_fn(Fb[0][:, 0:T], B[0][:], AF.Square), ["B0"], ["F0", "B0"])
                DVE(lambda e: e.tensor_tensor(out=Fb[1][:, 0:T], in0=B[1][:], in1=Fb[0][:, 0:T], op=ALU.subtract),
                    ["B1", "F0"], ["F1"])
                ACT(act_fn(Fb[1][:, 0:T], Fb[1][:, 0:T], AF.Ln, bias=epsl[:, 0:1]), ["F1", "epsl"], ["F1"])
                ACT(act_fn(Fb[1][:, 0:T], Fb[1][:, 0:T], AF.Exp, scale=-0.5), ["F1"], ["F1"])
                yield
                for cb in range(2):
                    tt, tn = Fb[2 + cb], f"F{2 + cb}"
                    sg, sn = Fb[4 + cb], f"F{4 + cb}"
                    DVE(lambda e, cb=cb, tt=tt: e.tensor_tensor(out=tt[:, 0:T], in0=Fb[6 + cb][:, 0:T], in1=B[0][:],
                                                                op=ALU.subtract), [f"F{6 + cb}", "B0"], [tn])
                    POOL(lambda e, tt=tt: e.tensor_tensor(out=tt[:, 0:T], in0=tt[:, 0:T], in1=Fb[1][:, 0:T], op=ALU.mult),
                         [tn, "F1"], [tn])
                    ACT(act_fn(tt[:, 0:T], tt[:, 0:T], AF.Identity, scale=lbg[l][:, cb:cb + 1], bias=lbb[l][:, cb:cb + 1]),
                        [tn, "lbg", "lbb"], [tn])
                    ACT(act_fn(sg[:, 0:T], tt[:, 0:T], AF.Exp, scale=-1.0), [tn], [sn])
                    DVE(lambda e, sg=sg: e.tensor_scalar(out=sg[:, 0:T], in0=sg[:, 0:T], scalar1=1.0, scalar2=None,
                                                         op0=ALU.add), [sn], [sn])
                    DVE(lambda e, sg=sg: e.reciprocal(out=sg[:, 0:T], in_=sg[:, 0:T]), [sn], [sn])
                    POOL(lambda e, cb=cb, tt=tt, sg=sg: e.tensor_tensor(out=mixT[:, 2 + cb, :], in0=tt[:, 0:T],
                                                                        in1=sg[:, 0:T], op=ALU.mult),
                         [tn, sn], [("mixT", 2 + cb)])
                    yield

            gen = gen_mix()
            NYIELD = 36

            def advance(k):
                for _ in range(k):
                    try:
                        next(gen)
                    except StopIteration:
                        return False
                return True

            steps = []
            for h in range(4):
                for kb in range(4 * i + 3, -1, -1):
                    steps.append((h, kb))
            nst = len(steps)

            def geom(h, kb):
                jj = kb - 4 * i
                c0 = max(jj, 0) * 128
                return c0, T - c0, jj >= 0

            def S1(n):
                h, kb = steps[n]
                c0, N, diag = geom(h, kb)
                hp = h // 2
                zb = 2 + n % 2
                E, En = aE, "aE"
                SPt, SPn = aSP[n % 2], f"aSP{n % 2}"
                PE(mm(B[zb][:, 0:N], kT[l][:, hp, kb * 128:(kb + 1) * 128], qTp[:, h, c0:T], True, True),
                   [f"kT{l}", f"qTp{h}"], [f"B{zb}"])
                ACT(act_fn(E[:, 0:N], B[zb][:, 0:N], AF.Exp), [f"B{zb}"], [En])
                ACT(act_fn(SPt[:, 0:N], E[:, 0:N], AF.Ln, bias=one1[:, 0:1]), [En, "one1"], [SPn])
                if diag:
                    POOL(lambda e: e.tensor_tensor(out=SPt[:, 0:128], in0=SPt[:, 0:128], in1=mask01[:], op=ALU.mult),
                         [SPn, "mask01"], [SPn])

            def S2(n):
                h, kb = steps[n]
                c0, N, diag = geom(h, kb)
                hp = h // 2
                cbk = 4 + n % 2
                rbk = 6
                SPt, SPn = aSP[n % 2], f"aSP{n % 2}"
                L, Ln_ = aL[n % 2], f"aL{n % 2}"
                A, An = aA[n % 2], f"aA{n % 2}"
                cr, crn = carry[h % 2], f"carry{h % 2}"
                PE(mm(B[cbk][:, 0:N], ntri[:], SPt[:, 0:N], True, False), [SPn, "ntri"], [f"B{cbk}"])
                PE(mm(B[cbk][:, 0:N], kT[l][:, hp, kb * 128:(kb + 1) * 128], qTp[:, h, c0:T], False, True),
                   [f"kT{l}", f"qTp{h}"], [f"B{cbk}"])
                if kb > 0:
                    PE(mm(B[rbk][:, 0:N], ones16[:], SPt[:, 0:N], True, True), [SPn, "ones16"], [f"B{rbk}"])
                topk = (kb == 4 * i + 3)
                if topk:
                    DVE(lambda e: e.memset(cr, 0.0), [], [crn])
                DVE(lambda e: e.tensor_tensor(out=L[:, 0:N], in0=B[cbk][:, 0:N], in1=cr[:, c0:T], op=ALU.subtract),
                    [f"B{cbk}", crn], [Ln_])
                if kb > 0:
                    DVE(lambda e: e.tensor_tensor(out=cr[:, c0:T], in0=B[rbk][:, 0:N], in1=cr[:, c0:T], op=ALU.add),
                        [f"B{rbk}", crn], [crn])
                ACT(act_fn(A[:, 0:N], L[:, 0:N], AF.Exp), [Ln_], [An])
                if diag:
                    POOL(lambda e: e.tensor_tensor(out=A[:, 0:128], in0=A[:, 0:128], in1=mask01[:], op=ALU.mult),
                         [An, "mask01"], [An])

            def S3(n):
                h, kb = steps[n]
                c0, N, diag = geom(h, kb)
                hp, gg = h // 2, h % 2
                A, An = aA[n % 2], f"aA{n % 2}"
                PE(mm(B[7][gg * 64:(gg + 1) * 64, c0:T], vS[l][:, kb, h * 64:(h + 1) * 64], A[:, 0:N],
                      kb == 4 * i + 3, kb == 0), [An, f"vS{l}"], ["B7"])
                if kb == 0 and gg == 1:
                    ACT(act_fn(mixT[:, 6 + hp, :], B[7][:], AF.Copy), ["B7"], [("mixT", 6 + hp)])

            per = -(-NYIELD // nst)
            for n in range(nst + 2):
                if n < nst:
                    S1(n)
                if 0 <= n - 1 < nst:
                    S2(n - 1)
                if 0 <= n - 2 < nst:
                    S3(n - 2)
                advance(per)
            while advance(1):
                pass

            if DBG and l == 0 and i == 0 and not dbg.get("done1"):
                DMA("pool", dbg["hT"], hT[:], ["hT"], [], "dbg")
                DMA("pool", dbg["mixT"], mixT[:], [("mixT", e_) for e_ in range(8)], [], "dbg")
            proj_residual(xt, xname, 4, lambda kb, j: mixT[:, kb, j * 128:(j + 1) * 128], lambda kb: ("mixT", kb))
            if DBG and l == 0 and i == 0 and not dbg.get("done1"):
                DMA("pool", dbg["x1"], xt[:], [(xname, j_) for j_ in range(4)], [], "dbg")
            norm_T(xt, xname)
            ffn_pend = []

            def ffn_finish(fb, par_, tb, tbn):
                ACT(act_fn(Hb[par_][:], tb[0][:, 0:T], AF.Silu), [tbn[0]], [f"H{par_}"])
                POOL(lambda e: e.tensor_tensor(out=actT[:, fb, :], in0=Hb[par_][:], in1=tb[1][:, 0:T], op=ALU.mult),
                     [f"H{par_}", tbn[1]], [("actT", fb)])

            def ffn_bufs(fb):
                par_ = fb % 3
                return (par_, [Fb[4 * par_], Fb[4 * par_ + 1]], [f"F{4 * par_}", f"F{4 * par_ + 1}"],
                        [Fb[4 * par_ + 2], Fb[4 * par_ + 3]], [f"F{4 * par_ + 2}", f"F{4 * par_ + 3}"])

            def ffn_halo_in(fb):
                par_, ub, ubn, tb, tbn = ffn_bufs(fb)
                for gv in range(2):
                    halo_in(POOL, ub[gv], ubn[gv], uH[l][:, fb, gv, :], f"uH{l}", 2, first)

            ffn_halo_in(0)
            for fb in range(NFB):
                P.curtag = f"ffn{fb}"
                wt, wn = next_chunk()
                par_, ub, ubn, tb, tbn = ffn_bufs(fb)
                for gv in range(2):
                    bk = 2 + 2 * par_ + gv
                    fm_block(wt, wn, gv * 128, bk)
                    widx = gv * NFB + fb
                    ACT(act_fn(ub[gv][:, 2:2 + T], B[bk][:], AF.Copy), [f"B{bk}"], [ubn[gv]])
                    ACT(act_fn(tb[gv][:, 0:T], B[bk][:], AF.Copy, scale=cfw[l][:, widx, 2:3]), [f"B{bk}", "cfw"], [tbn[gv]])
                    halo_out(POOL, ub[gv], ubn[gv], uH[l][:, fb, gv, :], f"uH{l}", 2)
                    for k in (1, 0):
                        DVE(lambda e, k=k, gv=gv, widx=widx, tb=tb, ub=ub: e.scalar_tensor_tensor(
                            out=tb[gv][:, 0:T], in0=ub[gv][:, k:k + T], scalar=cfw[l][:, widx, k:k + 1],
                            in1=tb[gv][:, 0:T], op0=ALU.mult, op1=ALU.add), [ubn[gv], tbn[gv], "cfw"], [tbn[gv]])
                if fb + 1 < NFB:
                    ffn_halo_in(fb + 1)
                if ffn_pend:
                    ffn_finish(*ffn_pend[0])
                    ffn_pend.clear()
                ffn_pend.append((fb, par_, tb, tbn))
            if ffn_pend:
                ffn_finish(*ffn_pend[0])
                ffn_pend.clear()
            P.curtag = "wdown"
            proj_residual(xt, xname, 11, lambda kb, j: actT[:, kb, j * 128:(j + 1) * 128], lambda kb: ("actT", kb))
            if DBG and l == 0 and i == 0 and not dbg.get("done1"):
                DMA("pool", dbg["actT"], actT[:], [("actT", f_) for f_ in range(NFB)], [], "dbg")
                dbg["last"] = DMA("pool", dbg["x2"], xt[:], [(xname, j_) for j_ in range(4)], [], "dbg")
                dbg["done1"] = True

        it = 0
        nxt = DMA("sp", xb[0][:], x_d[0, 0:T, :].rearrange("(j p) d -> p j d", p=128), [], [("xb0", j_) for j_ in range(4)], "xl0")
        stores = []
        for b in range(NB):
            for i in range(NT):
                cur = it % 2
                xt, xname = xb[cur], f"xb{cur}"
                if it + 1 < NB * NT:
                    b2, i2 = divmod(it + 1, NT)
                    o2 = (it + 1) % 2
                    DMA("sp", xb[o2][:], x_d[b2, i2 * T:(i2 + 1) * T, :].rearrange("(j p) d -> p j d", p=128),
                        [], [(f"xb{o2}", j_) for j_ in range(4)], f"xl{o2}")
                for l in range(2):
                    layer(xt, xname, l, i)
                rms_stats(xt, xname)
                for j in range(4):
                    DVE(lambda e, j=j, xt=xt: e.scalar_tensor_tensor(out=xt[:, j, :], in0=xt[:, j, :],
                                                                      scalar=rstd[:, j:j + 1], in1=fgb,
                                                                      op0=ALU.mult, op1=ALU.mult),
                        [(xname, j), ("rstd", j), "fgb"], [(xname, j)])
                stores.append(DMA("pool", y_d[b, i * T:(i + 1) * T, :].rearrange("(j p) d -> p j d", p=128), xt[:],
                                  [(xname, j_) for j_ in range(4)], [], f"ys{cur}"))
                it += 1
        fin = P.op("pool", lambda e: e.engine_nop(), [], [])
        fin.deps = set(stores[-2:]) if len(stores) >= 2 else set(stores)
        if DBG:
            fin.deps.add(dbg["last"])

        block = es.enter_context(nc.Block())
        P.emit(nc, block, esem, dsem)
        build.P = P
    return nc


_INPUT_NAMES = ["x", "norm1_g", "w_in", "conv_a_w", "conv_b_w", "conv_b_b", "ln_b_g", "ln_b_b", "ln_c_g", "ln_c_b",
                "sgu_w", "sgu_b", "w_out", "norm2_g", "w_up", "conv_f_w", "w_down", "final_g"]

_CACHE = {}


def run(inputs, n_cores, NB, S, DBG=False):
    key = (NB, S, DBG)
    if key not in _CACHE:
        _CACHE[key] = build(NB, S, DBG)
    nc = _CACHE[key]
    x = np.ascontiguousarray(np.asarray(inputs["x"], dtype=np.float32))
    in_maps = []
    for c in range(n_cores):
        m = {k: np.ascontiguousarray(np.asarray(inputs[k], dtype=np.float32)) for k in _INPUT_NAMES if k != "x"}
        m["x"] = np.ascontiguousarray(x[c * NB:(c + 1) * NB])
        in_maps.append(m)
    res = run_bass_kernel_spmd(nc, in_maps, core_ids=list(range(n_cores)))
    if DBG:
        return np.concatenate([r["y"] for r in res.results], axis=0), res.results[0]
    return np.concatenate([r["y"] for r in res.results], axis=0)


def kernel(**inputs):
    x = inputs["x"]
    Btot, S, _ = x.shape
    n_cores = 8
    return run(inputs, n_cores, Btot // n_cores, S).astype(np.float32)
```

```python
import numpy as np
from contextlib import ExitStack
import concourse.bass as bass
import concourse.mybir as mybir
from concourse.bass_utils import run_bass_kernel_spmd

F32 = mybir.dt.float32
BF16 = mybir.dt.bfloat16
AF = mybir.ActivationFunctionType
ALU = mybir.AluOpType

D = 1024
T = 512
DFF = 2816
NFB = 22
NCH = 51
NSLOT = 6
PRE_INTERLEAVE = False
CAST_MIXED = True
FFN_PIPE = True
RMS_EPS = 1e-6
LN_EPS = 1e-5
ENGS = ["pe", "act", "dve", "pool", "sp"]


class Op:
    __slots__ = ("eng", "fn", "deps", "sem", "semval", "idx", "signal", "cnt", "tag")

    def __init__(self, eng, fn, sem):
        self.eng = eng
        self.fn = fn
        self.sem = sem
        self.semval = 0
        self.signal = False
        self.cnt = 0
        self.idx = 0
        self.deps = ()


class Prog:
    def __init__(self):
        self.ops = {e: [] for e in ENGS}
        self.lastw = {}
        self.readers = {}
        self.semcnt = {}
        self.all = []

    def op(self, eng, fn, r=(), w=(), sem=None):
        o = Op(eng, fn, sem)
        o.tag = getattr(self, "curtag", "")
        deps = set()
        for k in r:
            lw = self.lastw.get(k)
            if lw is not None:
                deps.add(lw)
        for k in w:
            lw = self.lastw.get(k)
            if lw is not None:
                deps.add(lw)
            for rd in self.readers.get(k, ()):
                deps.add(rd)
        o.deps = deps
        for k in r:
            self.readers.setdefault(k, []).append(o)
        for k in w:
            self.lastw[k] = o
            self.readers[k] = []
        if sem is not None:
            self.semcnt[sem] = self.semcnt.get(sem, 0) + 16
            o.semval = self.semcnt[sem]
        o.idx = len(self.ops[eng])
        self.ops[eng].append(o)
        self.all.append(o)
        return o

    def emit(self, nc, block, esem, dsem):
        for o in self.all:
            latest = {}
            for d in o.deps:
                if d.sem is not None:
                    continue
                if d.eng == o.eng:
                    if o.eng in ("pe", "sp"):
                        continue
                    if o.idx - d.idx > 2:
                        continue
                cur = latest.get(d.eng)
                if cur is None or d.idx > cur.idx:
                    latest[d.eng] = d
            o.deps = set(d for d in o.deps if d.sem is not None) | set(latest.values())
            for d in latest.values():
                d.signal = True
        for e in ENGS:
            c = 0
            for o in self.ops[e]:
                if o.signal and o.sem is None:
                    c += 1
                o.cnt = c

        def body(ename):
            def f(eng):
                known = {}
                for o in self.ops[ename]:
                    waits = {}
                    for d in o.deps:
                        if d.sem is not None:
                            key, val = ("d", d.sem), d.semval
                        else:
                            if d.eng == o.eng:
                                if o.eng in ("pe", "sp"):
                                    continue
                                if o.idx - d.idx > 2:
                                    continue
                            key, val = ("e", d.eng), d.cnt
                        if known.get(key, 0) >= val:
                            continue
                        if waits.get(key, 0) < val:
                            waits[key] = val
                    if waits and getattr(self, "dbgwaits", None) is not None:
                        self.dbgwaits.append((ename, o.idx, o.tag, dict(waits),
                                              [(d.eng, d.idx, d.tag, d.sem) for d in o.deps]))
                    for key, val in waits.items():
                        s = dsem[key[1]] if key[0] == "d" else esem[key[1]]
                        eng.wait_ge(s, val)
                        known[key] = val
                    ins = o.fn(eng)
                    if o.sem is not None:
                        ins.then_inc(dsem[o.sem], 16)
                    elif o.signal:
                        ins.then_inc(esem[ename], 1)
            return f

        block.tensor(body("pe"))
        block.scalar(body("act"))
        block.vector(body("dve"))
        block.gpsimd(body("pool"))
        block.sync(body("sp"))


def build(NB, S, DBG=False):
    NT = S // T
    nc = bass.Bass("TRN2", target_bir_lowering=False)

    def din(name, shape):
        return nc.dram_tensor(name, list(shape), F32, kind="ExternalInput").ap()

    x_d = din("x", (NB, S, D))
    norm1_g = din("norm1_g", (2, D))
    w_in = din("w_in", (2, D, 2560))
    conv_a_w = din("conv_a_w", (2, 3, 256))
    conv_b_w = din("conv_b_w", (2, 31, 256))
    conv_b_b = din("conv_b_b", (2, 256))
    ln_b_g = din("ln_b_g", (2, 256))
    ln_b_b = din("ln_b_b", (2, 256))
    ln_c_g = din("ln_c_g", (2, 256))
    ln_c_b = din("ln_c_b", (2, 256))
    sgu_w = din("sgu_w", (2, 4, 128, 128))
    sgu_b = din("sgu_b", (2, 4, 128))
    w_out = din("w_out", (2, D, D))
    norm2_g = din("norm2_g", (2, D))
    w_up = din("w_up", (2, D, 2 * DFF))
    conv_f_w = din("conv_f_w", (2, 3, 2 * DFF))
    w_down = din("w_down", (2, DFF, D))
    final_g = din("final_g", (D,))
    y_d = nc.dram_tensor("y", [NB, S, D], F32, kind="ExternalOutput").ap()
    wsc = nc.dram_tensor("wsc", [2 * NCH, 128, 2048], BF16, kind="Internal").ap()
    dbg = {}
    if DBG:
        dbg["hT"] = nc.dram_tensor("d_hT", [128, 8, T], BF16, kind="ExternalOutput").ap()
        dbg["mixT"] = nc.dram_tensor("d_mixT", [128, 8, T], BF16, kind="ExternalOutput").ap()
        dbg["x1"] = nc.dram_tensor("d_x1", [128, 4, D], F32, kind="ExternalOutput").ap()
        dbg["actT"] = nc.dram_tensor("d_actT", [128, NFB, T], BF16, kind="ExternalOutput").ap()
        dbg["x2"] = nc.dram_tensor("d_x2", [128, 4, D], F32, kind="ExternalOutput").ap()

    P = Prog()
    es = ExitStack()
    with es:
        def sb(name, shape, dt=F32):
            return es.enter_context(nc.sbuf_tensor(name, list(shape), dt))

        xb = [sb(f"xb{k}", (128, 4, D)) for k in range(2)]
        hn = [sb(f"hn{k}", (128, D), BF16) for k in range(2)]
        hT = sb("hT", (128, 8, T), BF16)
        mixT = sb("mixT", (128, 8, T), BF16)
        actT = sb("actT", (128, NFB, T), BF16)
        kT = [sb(f"kT{l}", (128, 2, S), BF16) for l in range(2)]
        vS = [sb(f"vS{l}", (128, S // 128, 256), BF16) for l in range(2)]
        qTp = sb("qTp", (128, 4, T), BF16)
        wsl = [sb(f"wsl{k}", (128, 2048), BF16) for k in range(NSLOT)]
        st32 = [sb(f"st32_{k}", (128, 2048)) for k in range(2)]
        st16 = [sb(f"st16_{k}", (128, 2048), BF16) for k in range(2)]
        Fb = [sb(f"F{k}", (128, 516)) for k in range(12)]
        Hb = [sb(f"H{k}", (128, T), BF16) for k in range(4)]
        vnp = [sb(f"vnp{k}", (128, 4, 128), BF16) for k in range(4)]
        ssq = sb("ssq", (128, 4))
        rstd = sb("rstd", (128, 4))
        stat = sb("stat", (128, 4, 8))
        epsr = sb("epsr", (128, 1))
        epsl = sb("epsl", (128, 1))
        one1 = sb("one1", (128, 1))
        ones16 = sb("ones16", (128, 128), BF16)
        avg16 = sb("avg16", (128, 128), BF16)
        ident = sb("ident", (128, 128), BF16)
        ntri = sb("ntri", (128, 128), BF16)
        mask01 = sb("mask01", (128, 128), BF16)
        maskge = sb("maskge", (128, 128), BF16)
        sel = sb("sel", (1, 2, 128), BF16)
        gT1 = [sb(f"gT1_{l}", (128, 8)) for l in range(2)]
        gT2 = [sb(f"gT2_{l}", (128, 8)) for l in range(2)]
        caw = [sb(f"caw{l}", (128, 2, 3)) for l in range(2)]
        cbw = [sb(f"cbw{l}", (128, 2, 31)) for l in range(2)]
        cbb = [sb(f"cbb{l}", (128, 2)) for l in range(2)]
        lbg = [sb(f"lbg{l}", (128, 2)) for l in range(2)]
        lbb = [sb(f"lbb{l}", (128, 2)) for l in range(2)]
        lcg = [sb(f"lcg{l}", (128, 256)) for l in range(2)]
        lcb = [sb(f"lcb{l}", (128, 256)) for l in range(2)]
        cfw = [sb(f"cfw{l}", (128, 2 * NFB, 3)) for l in range(2)]
        WsT = [sb(f"WsT{l}", (128, 4, 128), BF16) for l in range(2)]
        sgub = [sb(f"sgub{l}", (1, 4, 128), BF16) for l in range(2)]
        caH = [sb(f"caH{l}", (128, 2, 2)) for l in range(2)]
        gluH = [sb(f"gluH{l}", (128, 2, 30), BF16) for l in range(2)]
        G16 = [sb(f"G16_{k}", (128, 544), BF16) for k in range(2)]
        G16o = [sb(f"G16o_{k}", (128, 544), BF16) for k in range(2)]
        uH = [sb(f"uH{l}", (128, NFB, 2, 2)) for l in range(2)]
        B = [es.enter_context(nc.psum_tensor(f"B{k}", [128, 512], F32)) for k in range(8)]

        esem = {e: es.enter_context(nc.semaphore(f"se_{e}")) for e in ENGS}
        dnames = ([f"w{k}" for k in range(NSLOT)] + ["ld0", "ld1", "so0", "so1", "xl0", "xl1", "ys0", "ys1", "par", "sg0", "sg1", "sg2", "sg3", "sgb", "dbg"])
        dsem = {n: es.enter_context(nc.semaphore(f"sd_{n}")) for n in dnames}

        def ACT(fn, r, w):
            return P.op("act", fn, r, w)

        def DVE(fn, r, w):
            return P.op("dve", fn, r, w)

        def POOL(fn, r, w):
            return P.op("pool", fn, r, w)

        def PE(fn, r, w):
            return P.op("pe", fn, r, w)

        def act_fn(out, in_, func, **kw):
            return lambda e: e.activation(out=out, in_=in_, func=func, **kw)

        def mm(out, lhsT, rhs, start, stop):
            return lambda e: e.matmul(out, lhsT=lhsT, rhs=rhs, start=start, stop=stop)

        POOL(lambda e: e.memset(ones16[:], 1.0), [], ["ones16"])
        POOL(lambda e: e.memset(avg16[:], 1.0 / 256.0), [], ["avg16"])
        POOL(lambda e: e.memset(epsr[:], RMS_EPS), [], ["epsr"])
        POOL(lambda e: e.memset(epsl[:], LN_EPS), [], ["epsl"])
        POOL(lambda e: e.memset(one1[:], 1.0), [], ["one1"])
        POOL(lambda e: e.affine_select(out=ident[:], in_=ones16[:], pattern=[[-1, 128]], compare_op=ALU.is_equal,
                                       fill=0.0, base=0, channel_multiplier=1), ["ones16"], ["ident"])
        POOL(lambda e: e.memset(ntri[:], -1.0), [], ["ntri"])
        POOL(lambda e: e.affine_select(out=ntri[:], in_=ntri[:], pattern=[[-1, 128]], compare_op=ALU.is_ge,
                                       fill=0.0, base=0, channel_multiplier=1), ["ntri"], ["ntri"])
        POOL(lambda e: e.affine_select(out=mask01[:], in_=ones16[:], pattern=[[1, 128]], compare_op=ALU.is_gt,
                                       fill=0.0, base=0, channel_multiplier=-1), ["ones16"], ["mask01"])
        POOL(lambda e: e.affine_select(out=maskge[:], in_=ones16[:], pattern=[[1, 128]], compare_op=ALU.is_ge,
                                       fill=0.0, base=0, channel_multiplier=-1), ["ones16"], ["maskge"])
        POOL(lambda e: e.memset(sel[:], 0.0), [], ["sel"])
        POOL(lambda e: e.memset(sel[:, 0, 0:64], 1.0), ["sel"], ["sel"])
        POOL(lambda e: e.memset(sel[:, 1, 64:128], 1.0), ["sel"], ["sel"])
        POOL(lambda e: e.memset(qTp[:], 0.0), [], ["qTp0", "qTp1", "qTp2", "qTp3"])
        for j in range(4):
            POOL(lambda e, j=j: e.memset(vnp[j][:], 0.0), [], [f"vnp{j}"])

        def pload(out, in_, w):
            P.op("act", lambda e: e.dma_start(out=out, in_=in_), [], w, sem=None)

        par_ops = []

        def par(out, in_):
            par_ops.append((out, in_))

        nc_ctx = nc.allow_non_contiguous_dma(reason="small one-time parameter loads")
        es.enter_context(nc_ctx)
        for l in range(2):
            par(gT1[l][:], norm1_g[l].rearrange("(c p) -> p c", p=128))
            par(gT2[l][:], norm2_g[l].rearrange("(c p) -> p c", p=128))
            for k in range(3):
                par(caw[l][:, :, k], conv_a_w[l, k].rearrange("(c p) -> p c", p=128))
            for k in range(31):
                par(cbw[l][:, :, k], conv_b_w[l, k].rearrange("(c p) -> p c", p=128))
            par(cbb[l][:], conv_b_b[l].rearrange("(c p) -> p c", p=128))
            par(lbg[l][:], ln_b_g[l].rearrange("(c p) -> p c", p=128))
            par(lbb[l][:], ln_b_b[l].rearrange("(c p) -> p c", p=128))
            par(lcg[l][:], ln_c_g[l].partition_broadcast(128))
            par(lcb[l][:], ln_c_b[l].partition_broadcast(128))
            for k in range(3):
                for q in range(4):
                    par(cfw[l][:, q * 11:(q + 1) * 11, k],
                        conv_f_w[l, k, q * 11 * 128:(q + 1) * 11 * 128].rearrange("(c p) -> p c", p=128))
        npar = len(par_ops)
        par_recs = []
        for (o_, i_) in par_ops:
            par_recs.append(P.op("act", lambda e, o_=o_, i_=i_: e.dma_start(out=o_, in_=i_), [], [], sem="par"))
        last = par_recs[-1]
        for k in ["gT1", "gT2", "caw", "cbw", "cbb", "lbg", "lbb", "lcg", "lcb", "cfw"]:
            P.lastw[k] = last
            P.readers[k] = []

        def DMA(q, out, in_, r, w, sem):
            return P.op(q, lambda e: e.dma_start(out=out, in_=in_), r, w, sem=sem)

        for l in range(2):
            for g in range(4):
                DMA("pool", Fb[g][:, 0:128], sgu_w[l, g], [], [f"F{g}"], f"sg{g}")
            for g in range(4):
                ACT(act_fn(Hb[g][:, 0:128], Fb[g][:, 0:128], AF.Copy), [f"F{g}"], [f"H{g}"])
                PE(lambda e, g=g: e.transpose(B[g][:].bitcast(BF16)[:, 0:128], Hb[g][:, 0:128], ident[:]),
                   [f"H{g}", "ident"], [f"B{g}"])
                DVE(lambda e, l=l, g=g: e.tensor_tensor(out=WsT[l][:, g, :], in0=B[g][:].bitcast(BF16)[:, 0:128],
                                                        in1=maskge[:], op=ALU.mult),
                    [f"B{g}", "maskge"], [f"WsT{l}"])
            DMA("pool", sgub[l][:], sgu_b[l:l + 1], [], [f"sgub{l}"], "sgb")

        def chunk_src(l, k):
            if k < 10:
                src = w_in[l][:, k * 256:(k + 1) * 256].rearrange("(dc p) e -> p dc e", p=128)
                return [(lambda t: t[:].rearrange("p (dc e) -> p dc e", dc=8), src)], gT1[l]
            if k < 14:
                c = k - 10
                src = w_out[l][c * 256:(c + 1) * 256, :].rearrange("(b p) d -> p b d", p=128)
                return [(lambda t: t[:].rearrange("p (b d) -> p b d", b=2), src)], None
            if k < 36:
                fb = k - 14
                s1 = w_up[l][:, fb * 128:(fb + 1) * 128].rearrange("(dc p) e -> p dc e", p=128)
                s2 = w_up[l][:, DFF + fb * 128:DFF + (fb + 1) * 128].rearrange("(dc p) e -> p dc e", p=128)
                return [(lambda t: t[:].rearrange("p (dc e) -> p dc e", dc=8)[:, :, 0:128], s1),
                        (lambda t: t[:].rearrange("p (dc e) -> p dc e", dc=8)[:, :, 128:256], s2)], gT2[l]
            if k >= 47:
                return [], "diag"
            c = k - 36
            src = w_down[l][c * 256:(c + 1) * 256, :].rearrange("(b p) d -> p b d", p=128)
            return [(lambda t: t[:].rearrange("p (b d) -> p b d", b=2), src)], None

        ORDER = [7, 8, 9, 5, 6, 3, 4, 0, 1, 2, 47, 48, 49, 50] + list(range(10, 47))
        PRE = 8

        def prepass_load(g):
            l, k = g // NCH, ORDER[g % NCH]
            s = g % 2
            parts, gain = chunk_src(l, k)
            for pi, (dv, src) in enumerate(parts):
                wn_ = [(f"st32_{s}", pi)] if len(parts) == 2 else [(f"st32_{s}", 0), (f"st32_{s}", 1)]
                DMA("pool", dv(st32[s]), src, [], wn_, f"ld{s}")

        def prepass(g, do_load=True):
            l, k = g // NCH, ORDER[g % NCH]
            s = g % 2
            parts, gain = chunk_src(l, k)
            if do_load:
                prepass_load(g)
            if gain == "diag":
                c = k - 47
                for mm_ in range(16):
                    m = c * 16 + mm_
                    if m >= 62:
                        break
                    cb_, kk = divmod(m, 31)
                    DVE(lambda e, s=s, mm_=mm_, cb_=cb_, kk=kk: e.tensor_scalar(
                        out=st16[s][:, mm_ * 128:(mm_ + 1) * 128], in0=ident[:], scalar1=cbw[l][:, cb_, kk:kk + 1],
                        scalar2=None, op0=ALU.mult), ["ident", "cbw"], [(f"st16_{s}", mm_ // 2)])
                DMA("pool", wsc[l * NCH + k], st16[s][:], [(f"st16_{s}", dc) for dc in range(8)], [("wsc", l, k)], f"so{s}")
                return
            rd = [(f"st32_{s}", 0), (f"st32_{s}", 1)]
            on_act = (g % 2 == 1)
            CENG = DVE if CAST_MIXED else POOL
            if not CAST_MIXED:
                on_act = False
            allw = [(f"st16_{s}", dc) for dc in range(8)]
            if gain is None:
                if on_act:
                    ACT(act_fn(st16[s][:], st32[s][:], AF.Copy), rd, allw)
                else:
                    CENG(lambda e, s=s: e.tensor_copy(out=st16[s][:], in_=st32[s][:]), rd, allw)
            else:
                gname = "gT1" if k < 10 else "gT2"
                for dc in range(8):
                    o_ = st16[s][:, dc * 256:(dc + 1) * 256]
                    i_ = st32[s][:, dc * 256:(dc + 1) * 256]
                    if on_act:
                        ACT(act_fn(o_, i_, AF.Copy, scale=gain[:, dc:dc + 1]), rd + [gname], [(f"st16_{s}", dc)])
                    else:
                        CENG(lambda e, o_=o_, i_=i_, gain=gain, dc=dc: e.tensor_scalar(
                            out=o_, in0=i_, scalar1=gain[:, dc:dc + 1], scalar2=None, op0=ALU.mult),
                            rd + [gname], [(f"st16_{s}", dc)])
            DMA("pool", wsc[l * NCH + k], st16[s][:], allw, [("wsc", l, k)], f"so{s}")

        if PRE_INTERLEAVE:
            for g in range(PRE):
                prepass(g)
        else:
            prepass_load(0)
            prepass_load(1)
            for g in range(2 * NCH):
                prepass(g, do_load=False)
                if g + 2 < 2 * NCH:
                    prepass_load(g + 2)

        aE = st32[0][:, 0:512]
        aL = [st32[0][:, 512:1024], st32[0][:, 1024:1536]]
        aSP = [st16[0][:, 0:512], st16[0][:, 512:1024]]
        aA = [st16[0][:, 1024:1536], st16[0][:, 1536:2048]]
        uS = [st16[1][:, 0:512], st16[1][:, 512:1024]]
        carry = [st32[1][:, 0:512], st32[1][:, 512:1024]]
        fence32b = P.lastw.get(("st16_1", 0))
        assert fence32b is not None
        for nm_ in ["carry0", "carry1"]:
            P.lastw[nm_] = fence32b
        fgb = st32[1][:, 1024:2048]
        P.lastw["fgb"] = fence32b
        DMA("sp", fgb, final_g.partition_broadcast(128), [], ["fgb"], "sgb")
        fence32 = P.lastw.get(("st16_0", 0))
        fence16 = P.lastw.get(("wsc", 1, ORDER[(2 * NCH - 2) % NCH]))
        fence16b = P.lastw.get(("wsc", 1, ORDER[(2 * NCH - 1) % NCH]))
        assert fence32 is not None and fence16 is not None and fence16b is not None
        for nm_ in ["aE", "aL0", "aL1"]:
            P.lastw[nm_] = fence32
        for nm_ in ["aSP0", "aSP1", "aA0", "aA1"]:
            P.lastw[nm_] = fence16
        for nm_ in ["uS0", "uS1"]:
            P.lastw[nm_] = fence16b

        wstate = {"n": 0}

        def wfetch(l, k):
            sl = wstate["n"] % NSLOT
            wstate["n"] += 1
            DMA("sp", wsl[sl][:], wsc[l * NCH + k], [("wsc", l, k)], [f"wsl{sl}"], f"w{sl}")
            return sl

        stream = []
        fetched = {"i": 0}
        slot_of = {}

        def prefetch_upto(pos):
            while fetched["i"] <= pos and fetched["i"] < len(stream):
                l_, k_ = stream[fetched["i"]]
                slot_of[fetched["i"]] = wfetch(l_, k_)
                fetched["i"] += 1

        upos = {"i": 0}

        def next_chunk(prefetch=True):
            pos = upos["i"]
            upos["i"] += 1
            if PRE_INTERLEAVE and pos + PRE < 2 * NCH:
                prepass(pos + PRE)
            if prefetch:
                prefetch_upto(pos + NSLOT - 1)
            sl = slot_of[pos]
            return wsl[sl], f"wsl{sl}"

        def catch_up():
            prefetch_upto(upos["i"] - 1 + NSLOT - 1)

        for b in range(NB):
            for i in range(NT):
                for l in range(2):
                    for k in ORDER:
                        stream.append((l, k))

        def rms_stats(xt, xname):
            DVE(lambda e: e.memset(ssq[:], 0.0), [], [("ssq", j) for j in range(4)])

            def lnexp(j):
                ACT(act_fn(rstd[:, j:j + 1], ssq[:, j:j + 1], AF.Ln, bias=epsr[:, 0:1], scale=1.0 / D),
                    [("ssq", j), "epsr"], [("rstd", j)])
                ACT(act_fn(rstd[:, j:j + 1], rstd[:, j:j + 1], AF.Exp, scale=-0.5), [("rstd", j)], [("rstd", j)])

            for j in range(4):
                ACT(act_fn(Fb[11][:].bitcast(BF16)[:, 0:D], xt[:, j, :], AF.Square, accum_out=ssq[:, j:j + 1]),
                    [(xname, j)], [("ssq", j)])
                if j >= 1:
                    lnexp(j - 1)
            lnexp(3)

        def norm_T(xt, xname):
            rms_stats(xt, xname)
            for j in range(4):
                hb = hn[j % 2]
                hname = f"hn{j % 2}"
                DVE(lambda e, j=j, hb=hb: e.tensor_scalar(out=hb[:], in0=xt[:, j, :], scalar1=rstd[:, j:j + 1],
                                                          scalar2=None, op0=ALU.mult), [(xname, j), ("rstd", j)], [hname])
                bk = j % 2
                for c in range(8):
                    PE(lambda e, c=c, hb=hb, bk=bk: e.transpose(B[bk][:].bitcast(BF16)[:, c * 128:(c + 1) * 128],
                                                                hb[:, c * 128:(c + 1) * 128], ident[:]),
                       [hname, "ident"], [f"B{bk}"])
                ev = ACT if j % 2 == 0 else DVE
                if j % 2 == 0:
                    ACT(act_fn(hT[:, :, j * 128:(j + 1) * 128],
                               B[bk][:].bitcast(BF16).rearrange("p (c t) -> p c t", c=8), AF.Copy),
                        [f"B{bk}"], ["hT"])
                else:
                    DVE(lambda e, j=j, bk=bk: e.tensor_copy(out=hT[:, :, j * 128:(j + 1) * 128],
                                                            in_=B[bk][:].bitcast(BF16).rearrange("p (c t) -> p c t", c=8)),
                        [f"B{bk}"], ["hT"])

        def fm_block(wt, wname, col0, bank):
            wv = wt[:].rearrange("p (dc e) -> p dc e", dc=8)
            for dc in range(8):
                PE(mm(B[bank][:], wv[:, dc, col0:col0 + 128], hT[:, dc, :], dc == 0, dc == 7),
                   [wname, "hT"], [f"B{bank}"])

        def tm_block(wt, wname, j, bank, half):
            wv = wt[:].rearrange("p (dc e) -> p dc e", dc=8)
            for dc in range(8):
                PE(mm(B[bank][:, half * 256:(half + 1) * 256], hT[:, dc, j * 128:(j + 1) * 128], wv[:, dc, :],
                      dc == 0, dc == 7), [wname, "hT"], [f"B{bank}"])

        def dwconv(eng, buf, bname, acc, aname, wt, wname, widx, K, first_bias=None):
            wsel = lambda k: wt[:, widx, k:k + 1]
            if first_bias is None:
                eng(lambda e: e.tensor_scalar(out=acc[:, 0:T], in0=buf[:, K - 1:K - 1 + T], scalar1=wsel(K - 1),
                                              scalar2=None, op0=ALU.mult), [bname, wname], [aname])
            else:
                eng(lambda e: e.tensor_scalar(out=acc[:, 0:T], in0=buf[:, K - 1:K - 1 + T], scalar1=wsel(K - 1),
                                              scalar2=first_bias, op0=ALU.mult, op1=ALU.add),
                    [bname, wname, "cbb"], [aname])
            for k in range(K - 2, -1, -1):
                eng(lambda e, k=k: e.scalar_tensor_tensor(out=acc[:, 0:T], in0=buf[:, k:k + T], scalar=wsel(k),
                                                          in1=acc[:, 0:T], op0=ALU.mult, op1=ALU.add),
                    [bname, wname, aname], [aname])

        def halo_in(eng, buf, bname, hal, hname, H, first):
            if first:
                eng(lambda e: e.memset(buf[:, 0:H], 0.0), [], [bname])
            else:
                eng(lambda e: e.tensor_copy(out=buf[:, 0:H], in_=hal), [hname], [bname])

        def halo_out(eng, buf, bname, hal, hname, H):
            eng(lambda e: e.tensor_copy(out=hal, in_=buf[:, T:T + H]), [bname], [hname])

        def proj_residual(xt, xname, nchunks, lhs_of, lres_of):
            nkb = 2 * nchunks
            for c in range(nchunks - 2):
                wt, wn = next_chunk()
                wv = wt[:].rearrange("p (b d) -> p b d", b=2)
                for kbl in range(2):
                    kb = 2 * c + kbl
                    for j in range(4):
                        for dh in range(2):
                            bk = j * 2 + dh
                            PE(mm(B[bk][:], lhs_of(kb, j), wv[:, kbl, dh * 512:(dh + 1) * 512], kb == 0, False),
                               [lres_of(kb), wn], [f"B{bk}"])
            wA = next_chunk()
            wB = next_chunk(prefetch=False)
            for j in range(4):
                for dh in range(2):
                    bk = j * 2 + dh
                    for ci, (wt, wn) in enumerate((wA, wB)):
                        wv = wt[:].rearrange("p (b d) -> p b d", b=2)
                        for kbl in range(2):
                            kb = 2 * (nchunks - 2 + ci) + kbl
                            PE(mm(B[bk][:], lhs_of(kb, j), wv[:, kbl, dh * 512:(dh + 1) * 512], kb == 0, kb == nkb - 1),
                               [lres_of(kb), wn], [f"B{bk}"])
                    DVE(lambda e, j=j, dh=dh, bk=bk: e.tensor_tensor(out=xt[:, j, dh * 512:(dh + 1) * 512],
                                                                     in0=xt[:, j, dh * 512:(dh + 1) * 512],
                                                                     in1=B[bk][:], op=ALU.add),
                        [(xname, j), f"B{bk}"], [(xname, j)])
            catch_up()

        def layer(xt, xname, l, i):
            first = (i == 0)
            norm_T(xt, xname)
            nb = {"n": 0}

            def nbank():
                b_ = 2 + nb["n"] % 4
                nb["n"] += 1
                return b_

            wt, wn = next_chunk()
            for hp in range(2):
                bk = nbank()
                fm_block(wt, wn, hp * 128, bk)
                for gg in range(2):
                    h = 2 * hp + gg
                    DVE(lambda e, gg=gg, h=h, bk=bk: e.tensor_scalar(
                        out=qTp[gg * 64:(gg + 1) * 64, h, :], in0=B[bk][gg * 64:(gg + 1) * 64, :], scalar1=0.125,
                        scalar2=None, op0=ALU.mult), [f"B{bk}"], [f"qTp{h}"])
            wt, wn = next_chunk()
            for hp in range(2):
                bk = nbank()
                fm_block(wt, wn, hp * 128, bk)
                DVE(lambda e, hp=hp, bk=bk: e.tensor_copy(out=kT[l][:, hp, i * T:(i + 1) * T], in_=B[bk][:]),
                    [f"B{bk}"], [f"kT{l}"])
            wt, wn = next_chunk()
            for j in range(4):
                bk, half = 6 + j // 2, j % 2
                tm_block(wt, wn, j, bk, half)
                DVE(lambda e, j=j, bk=bk, half=half: e.tensor_copy(out=vS[l][:, i * 4 + j, :],
                                                                   in_=B[bk][:, half * 256:(half + 1) * 256]),
                    [f"B{bk}"], [f"vS{l}"])
            wt, wn = next_chunk()
            for hp in range(2):
                bk = nbank()
                fm_block(wt, wn, hp * 128, bk)
                ACT(act_fn(uS[hp], B[bk][:], AF.Gelu), [f"B{bk}"], [f"uS{hp}"])
            wt, wn = next_chunk()
            for j in range(4):
                bk, half = 4 + j // 2, j % 2
                tm_block(wt, wn, j, bk, half)
                vg = Fb[j]
                vn_ = f"F{j}"
                ACT(act_fn(vg[:, 0:256], B[bk][:, half * 256:(half + 1) * 256], AF.Gelu), [f"B{bk}"], [vn_])
            for j in range(4):
                vg = Fb[j]
                vn_ = f"F{j}"
                DVE(lambda e, j=j, vg=vg: e.bn_stats(out=stat[:, j, 0:6], in_=vg[:, 0:256]), [vn_], [("stat", j)])
                DVE(lambda e, j=j: e.bn_aggr(out=stat[:, j, 6:8], in_=stat[:, j, 0:6]), [("stat", j)], [("stat", j)])
            for j in range(4):
                ACT(act_fn(stat[:, j, 7:8], stat[:, j, 7:8], AF.Ln, bias=epsl[:, 0:1]), [("stat", j), "epsl"], [("stat", j)])
            for j in range(4):
                ACT(act_fn(stat[:, j, 7:8], stat[:, j, 7:8], AF.Exp, scale=-0.5), [("stat", j)], [("stat", j)])

            def gen_mix():
                for j in range(4):
                    vg = Fb[j]
                    vn_ = f"F{j}"
                    DVE(lambda e, j=j, vg=vg: e.tensor_scalar(out=vg[:, 0:256], in0=vg[:, 0:256], scalar1=stat[:, j, 6:7],
                                                              scalar2=stat[:, j, 7:8], op0=ALU.subtract, op1=ALU.mult),
                        [vn_, ("stat", j)], [vn_])
                    POOL(lambda e, vg=vg: e.tensor_tensor(out=vg[:, 0:256], in0=vg[:, 0:256], in1=lcg[l][:], op=ALU.mult),
                         [vn_, "lcg"], [vn_])
                    for gg in range(2):
                        POOL(lambda e, j=j, vg=vg, gg=gg: e.tensor_tensor(
                            out=vnp[j][:, gg:4:2, gg * 64:(gg + 1) * 64],
                            in0=vg[:, 0:256].rearrange("p (g c) -> p g c", g=4)[:, gg:4:2, :],
                            in1=lcb[l][:].rearrange("p (g c) -> p g c", g=4)[:, gg:4:2, :], op=ALU.add),
                            [vn_, "lcb"], [f"vnp{j}"])
                    yield
                for hp in range(2):
                    for j in range(4):
                        for gg in range(2):
                            g = 2 * hp + gg
                            PE(mm(B[hp][:, j * 128:(j + 1) * 128], vnp[j][:, g, :], WsT[l][:, g, :], gg == 0, False),
                               [f"vnp{j}", f"WsT{l}"], [f"B{hp}"])
                        for gg in range(2):
                            g = 2 * hp + gg
                            PE(mm(B[hp][:, j * 128:(j + 1) * 128], sel[:, gg, :], sgub[l][:, g, :], False, gg == 1),
                               ["sel", f"sgub{l}"], [f"B{hp}"])
                    DVE(lambda e, hp=hp: e.tensor_tensor(out=mixT[:, 4 + hp, :], in0=uS[hp], in1=B[hp][:], op=ALU.mult),
                        [f"uS{hp}", f"B{hp}"], [("mixT", 4 + hp)])
                    yield
                mb = {"n": 0}

                def mbank():
                    b_ = mb["n"] % 2
                    mb["n"] += 1
                    return b_

                wt, wn = next_chunk()
                for cb in range(2):
                    bk = mbank()
                    fm_block(wt, wn, cb * 128, bk)
                    DVE(lambda e, cb=cb, bk=bk: e.tensor_copy(out=Fb[cb][:, 0:T], in_=B[bk][:]), [f"B{bk}"], [f"F{cb}"])
                    yield
                wt, wn = next_chunk()
                for cb in range(2):
                    bk = mbank()
                    fm_block(wt, wn, cb * 128, bk)
                    sg = Fb[2 + cb]
                    sn = f"F{2 + cb}"
                    ACT(act_fn(sg[:, 0:T], B[bk][:], AF.Exp, scale=-1.0), [f"B{bk}"], [sn])
                    ACT(act_fn(sg[:, 0:T], sg[:, 0:T], AF.Ln, bias=one1[:, 0:1]), [sn, "one1"], [sn])
                    ACT(act_fn(sg[:, 0:T], sg[:, 0:T], AF.Exp, scale=-1.0), [sn], [sn])
                    gl = G16[cb]
                    gn = f"G16_{cb}"
                    halo_in(POOL, gl, gn, gluH[l][:, cb, :], f"gluH{l}", 30, first)
                    POOL(lambda e, cb=cb, gl=gl: e.tensor_tensor(out=gl[:, 30:30 + T], in0=Fb[cb][:, 0:T],
                                                                 in1=Fb[2 + cb][:, 0:T], op=ALU.mult),
                         [f"F{cb}", sn], [gn])
                    halo_out(POOL, gl, gn, gluH[l][:, cb, :], f"gluH{l}", 30)
                    POOL(lambda e, cb=cb, gl=gl: e.tensor_copy(out=G16o[cb][:, 0:541], in_=gl[:, 1:542]),
                         [gn], [f"G16o_{cb}"])
                    yield
                wt, wn = next_chunk()
                for cb in range(2):
                    bk = mbank()
                    fm_block(wt, wn, cb * 128, bk)
                    DVE(lambda e, cb=cb, bk=bk: e.tensor_copy(out=Fb[cb][:, 0:T], in_=B[bk][:]), [f"B{bk}"], [f"F{cb}"])
                    yield
                wt, wn = next_chunk()
                for cb in range(2):
                    bk = mbank()
                    fm_block(wt, wn, cb * 128, bk)
                    DVE(lambda e, cb=cb, bk=bk: e.tensor_copy(out=Fb[2 + cb][:, 0:T], in_=B[bk][:]), [f"B{bk}"], [f"F{2 + cb}"])
                    yield
                wt, wn = next_chunk()
                for cb in range(2):
                    bk = mbank()
                    fm_block(wt, wn, cb * 128, bk)
                    ca = Fb[4 + cb]
                    cn = f"F{4 + cb}"
                    halo_in(POOL, ca, cn, caH[l][:, cb, :], f"caH{l}", 2, first)
                    DVE(lambda e, cb=cb, bk=bk, ca=ca: e.tensor_tensor(out=ca[:, 2:2 + T], in0=Fb[2 + cb][:, 0:T],
                                                                        in1=B[bk][:], op=ALU.mult),
                        [f"F{2 + cb}", f"B{bk}"], [cn])
                    halo_out(POOL, ca, cn, caH[l][:, cb, :], f"caH{l}", 2)
                    yield
                    dwconv(DVE, ca, cn, Fb[6 + cb], f"F{6 + cb}", caw[l], "caw", cb, 3)
                    POOL(lambda e, cb=cb: e.tensor_tensor(out=mixT[:, cb, :], in0=Fb[cb][:, 0:T], in1=Fb[6 + cb][:, 0:T],
                                                          op=ALU.mult), [f"F{cb}", f"F{6 + cb}"], [("mixT", cb)])
                    yield
                dwt = None
                for m in range(62):
                    cb, k = divmod(m, 31)
                    if m % 16 == 0:
                        dwt = next_chunk()
                    gl = G16[cb]
                    gn = f"G16_{cb}"
                    rhs_ = gl[:, k:k + T] if k % 2 == 0 else G16o[cb][:, k - 1:k - 1 + T]
                    PE(mm(B[cb][:], dwt[0][:, (m % 16) * 128:(m % 16 + 1) * 128], rhs_, k == 0, k == 30),
                       [gn, f"G16o_{cb}", dwt[1]], [f"B{cb}"])
                    if k % 8 == 7:
                        yield
                    if k == 30:
                        acc, an = Fb[6 + cb], f"F{6 + cb}"
                        ACT(act_fn(Hb[2 + cb][:], B[cb][:], AF.Square, bias=cbb[l][:, cb:cb + 1]), [f"B{cb}", "cbb"],
                            [f"H{2 + cb}", f"B{cb}"])
                        DVE(lambda e, cb=cb, acc=acc: e.tensor_scalar(out=acc[:, 0:T], in0=B[cb][:], scalar1=cbb[l][:, cb:cb + 1],
                                                                      scalar2=None, op0=ALU.add), [f"B{cb}", "cbb"], [an, f"B{cb}"])
                        DVE(lambda e, cb=cb, acc=acc: e.tensor_copy(out=Hb[cb][:], in_=acc[:, 0:T]), [an], [f"H{cb}"])
                        yield
                for cb in range(2):
                    PE(mm(B[0][:], avg16[:], Hb[cb][:], cb == 0, cb == 1), [f"H{cb}", "avg16"], ["B0"])
                for cb in range(2):
                    PE(mm(B[1][:], avg16[:], Hb[2 + cb][:], cb == 0, cb == 1), [f"H{2 + cb}", "avg16"], ["B1"])
                ACT(act_fn(Fb[0][:, 0:T], B[0][:], AF.Square), ["B0"], ["F0", "B0"])
                DVE(lambda e: e.tensor_tensor(out=Fb[1][:, 0:T], in0=B[1][:], in1=Fb[0][:, 0:T], op=ALU.subtract),
                    ["B1", "F0"], ["F1"])
                ACT(act_fn(Fb[1][:, 0:T], Fb[1][:, 0:T], AF.Ln, bias=epsl[:, 0:1]), ["F1", "epsl"], ["F1"])
                ACT(act_fn(Fb[1][:, 0:T], Fb[1][:, 0:T], AF.Exp, scale=-0.5), ["F1"], ["F1"])
                yield
                for cb in range(2):
                    tt, tn = Fb[2 + cb], f"F{2 + cb}"
                    sg, sn = Fb[4 + cb], f"F{4 + cb}"
                    DVE(lambda e, cb=cb, tt=tt: e.tensor_tensor(out=tt[:, 0:T], in0=Fb[6 + cb][:, 0:T], in1=B[0][:],
                                                                op=ALU.subtract), [f"F{6 + cb}", "B0"], [tn])
                    POOL(lambda e, tt=tt: e.tensor_tensor(out=tt[:, 0:T], in0=tt[:, 0:T], in1=Fb[1][:, 0:T], op=ALU.mult),
                         [tn, "F1"], [tn])
                    ACT(act_fn(tt[:, 0:T], tt[:, 0:T], AF.Identity, scale=lbg[l][:, cb:cb + 1], bias=lbb[l][:, cb:cb + 1]),
                        [tn, "lbg", "lbb"], [tn])
                    ACT(act_fn(sg[:, 0:T], tt[:, 0:T], AF.Exp, scale=-1.0), [tn], [sn])
                    ACT(act_fn(sg[:, 0:T], sg[:, 0:T], AF.Ln, bias=one1[:, 0:1]), [sn, "one1"], [sn])
                    ACT(act_fn(sg[:, 0:T], sg[:, 0:T], AF.Exp, scale=-1.0), [sn], [sn])
                    POOL(lambda e, cb=cb, tt=tt, sg=sg: e.tensor_tensor(out=mixT[:, 2 + cb, :], in0=tt[:, 0:T],
                                                                        in1=sg[:, 0:T], op=ALU.mult),
                         [tn, sn], [("mixT", 2 + cb)])
                    yield

            gen = gen_mix()
            NYIELD = 36

            def advance(k):
                for _ in range(k):
                    try:
                        next(gen)
                    except StopIteration:
                        return False
                return True

            steps = []
            for h in range(4):
                for kb in range(4 * i + 3, -1, -1):
                    steps.append((h, kb))
            nst = len(steps)

            def geom(h, kb):
                jj = kb - 4 * i
                c0 = max(jj, 0) * 128
                return c0, T - c0, jj >= 0

            def S1(n):
                h, kb = steps[n]
                c0, N, diag = geom(h, kb)
                hp = h // 2
                zb = 2 + n % 2
                E, En = aE, "aE"
                SPt, SPn = aSP[n % 2], f"aSP{n % 2}"
                PE(mm(B[zb][:, 0:N], kT[l][:, hp, kb * 128:(kb + 1) * 128], qTp[:, h, c0:T], True, True),
                   [f"kT{l}", f"qTp{h}"], [f"B{zb}"])
                ACT(act_fn(E[:, 0:N], B[zb][:, 0:N], AF.Exp), [f"B{zb}"], [En])
                ACT(act_fn(SPt[:, 0:N], E[:, 0:N], AF.Ln, bias=one1[:, 0:1]), [En, "one1"], [SPn])
                if diag:
                    POOL(lambda e: e.tensor_tensor(out=SPt[:, 0:128], in0=SPt[:, 0:128], in1=mask01[:], op=ALU.mult),
                         [SPn, "mask01"], [SPn])

            def S2(n):
                h, kb = steps[n]
                c0, N, diag = geom(h, kb)
                hp = h // 2
                cbk = 4 + n % 2
                rbk = 6
                SPt, SPn = aSP[n % 2], f"aSP{n % 2}"
                L, Ln_ = aL[n % 2], f"aL{n % 2}"
                A, An = aA[n % 2], f"aA{n % 2}"
                cr, crn = carry[h % 2], f"carry{h % 2}"
                PE(mm(B[cbk][:, 0:N], ntri[:], SPt[:, 0:N], True, False), [SPn, "ntri"], [f"B{cbk}"])
                PE(mm(B[cbk][:, 0:N], kT[l][:, hp, kb * 128:(kb + 1) * 128], qTp[:, h, c0:T], False, True),
                   [f"kT{l}", f"qTp{h}"], [f"B{cbk}"])
                if kb > 0:
                    PE(mm(B[rbk][:, 0:N], ones16[:], SPt[:, 0:N], True, True), [SPn, "ones16"], [f"B{rbk}"])
                topk = (kb == 4 * i + 3)
                if topk:
                    DVE(lambda e: e.memset(cr, 0.0), [], [crn])
                DVE(lambda e: e.tensor_tensor(out=L[:, 0:N], in0=B[cbk][:, 0:N], in1=cr[:, c0:T], op=ALU.subtract),
                    [f"B{cbk}", crn], [Ln_])
                if kb > 0:
                    DVE(lambda e: e.tensor_tensor(out=cr[:, c0:T], in0=B[rbk][:, 0:N], in1=cr[:, c0:T], op=ALU.add),
                        [f"B{rbk}", crn], [crn])
                ACT(act_fn(A[:, 0:N], L[:, 0:N], AF.Exp), [Ln_], [An])
                if diag:
                    POOL(lambda e: e.tensor_tensor(out=A[:, 0:128], in0=A[:, 0:128], in1=mask01[:], op=ALU.mult),
                         [An, "mask01"], [An])

            def S3(n):
                h, kb = steps[n]
                c0, N, diag = geom(h, kb)
                hp, gg = h // 2, h % 2
                A, An = aA[n % 2], f"aA{n % 2}"
                PE(mm(B[7][gg * 64:(gg + 1) * 64, c0:T], vS[l][:, kb, h * 64:(h + 1) * 64], A[:, 0:N],
                      kb == 4 * i + 3, kb == 0), [An, f"vS{l}"], ["B7"])
                if kb == 0 and gg == 1:
                    ACT(act_fn(mixT[:, 6 + hp, :], B[7][:], AF.Copy), ["B7"], [("mixT", 6 + hp)])

            per = -(-NYIELD // nst)
            for n in range(nst + 2):
                if n < nst:
                    S1(n)
                if 0 <= n - 1 < nst:
                    S2(n - 1)
                if 0 <= n - 2 < nst:
                    S3(n - 2)
                advance(per)
            while advance(1):
                pass

            if DBG and l == 0 and i == 0 and not dbg.get("done1"):
                DMA("pool", dbg["hT"], hT[:], ["hT"], [], "dbg")
                DMA("pool", dbg["mixT"], mixT[:], [("mixT", e_) for e_ in range(8)], [], "dbg")
            proj_residual(xt, xname, 4, lambda kb, j: mixT[:, kb, j * 128:(j + 1) * 128], lambda kb: ("mixT", kb))
            if DBG and l == 0 and i == 0 and not dbg.get("done1"):
                DMA("pool", dbg["x1"], xt[:], [(xname, j_) for j_ in range(4)], [], "dbg")
            norm_T(xt, xname)
            ffn_pend = []

            def ffn_finish(fb, par_, tb, tbn):
                ACT(act_fn(Hb[par_][:], tb[0][:, 0:T], AF.Silu), [tbn[0]], [f"H{par_}"])
                POOL(lambda e: e.tensor_tensor(out=actT[:, fb, :], in0=Hb[par_][:], in1=tb[1][:, 0:T], op=ALU.mult),
                     [f"H{par_}", tbn[1]], [("actT", fb)])

            def ffn_bufs(fb):
                par_ = fb % 3
                return (par_, [Fb[4 * par_], Fb[4 * par_ + 1]], [f"F{4 * par_}", f"F{4 * par_ + 1}"],
                        [Fb[4 * par_ + 2], Fb[4 * par_ + 3]], [f"F{4 * par_ + 2}", f"F{4 * par_ + 3}"])

            def ffn_halo_in(fb):
                par_, ub, ubn, tb, tbn = ffn_bufs(fb)
                for gv in range(2):
                    halo_in(POOL, ub[gv], ubn[gv], uH[l][:, fb, gv, :], f"uH{l}", 2, first)

            ffn_halo_in(0)
            for fb in range(NFB):
                P.curtag = f"ffn{fb}"
                wt, wn = next_chunk()
                par_, ub, ubn, tb, tbn = ffn_bufs(fb)
                for gv in range(2):
                    bk = 2 + 2 * par_ + gv
                    fm_block(wt, wn, gv * 128, bk)
                    widx = gv * NFB + fb
                    ACT(act_fn(ub[gv][:, 2:2 + T], B[bk][:], AF.Copy), [f"B{bk}"], [ubn[gv]])
                    ACT(act_fn(tb[gv][:, 0:T], B[bk][:], AF.Copy, scale=cfw[l][:, widx, 2:3]), [f"B{bk}", "cfw"], [tbn[gv]])
                    halo_out(POOL, ub[gv], ubn[gv], uH[l][:, fb, gv, :], f"uH{l}", 2)
                    for k in (1, 0):
                        DVE(lambda e, k=k, gv=gv, widx=widx, tb=tb, ub=ub: e.scalar_tensor_tensor(
                            out=tb[gv][:, 0:T], in0=ub[gv][:, k:k + T], scalar=cfw[l][:, widx, k:k + 1],
                            in1=tb[gv][:, 0:T], op0=ALU.mult, op1=ALU.add), [ubn[gv], tbn[gv], "cfw"], [tbn[gv]])
                if fb + 1 < NFB:
                    ffn_halo_in(fb + 1)
                if ffn_pend:
                    ffn_finish(*ffn_pend[0])
                    ffn_pend.clear()
                ffn_pend.append((fb, par_, tb, tbn))
            if ffn_pend:
                ffn_finish(*ffn_pend[0])
                ffn_pend.clear()
            P.curtag = "wdown"
            proj_residual(xt, xname, 11, lambda kb, j: actT[:, kb, j * 128:(j + 1) * 128], lambda kb: ("actT", kb))
            if DBG and l == 0 and i == 0 and not dbg.get("done1"):
                DMA("pool", dbg["actT"], actT[:], [("actT", f_) for f_ in range(NFB)], [], "dbg")
                dbg["last"] = DMA("pool", dbg["x2"], xt[:], [(xname, j_) for j_ in range(4)], [], "dbg")
                dbg["done1"] = True

        it = 0
        nxt = DMA("sp", xb[0][:], x_d[0, 0:T, :].rearrange("(j p) d -> p j d", p=128), [], [("xb0", j_) for j_ in range(4)], "xl0")
        stores = []
        for b in range(NB):
            for i in range(NT):
                cur = it % 2
                xt, xname = xb[cur], f"xb{cur}"
                if it + 1 < NB * NT:
                    b2, i2 = divmod(it + 1, NT)
                    o2 = (it + 1) % 2
                    DMA("sp", xb[o2][:], x_d[b2, i2 * T:(i2 + 1) * T, :].rearrange("(j p) d -> p j d", p=128),
                        [], [(f"xb{o2}", j_) for j_ in range(4)], f"xl{o2}")
                for l in range(2):
                    layer(xt, xname, l, i)
                rms_stats(xt, xname)
                for j in range(4):
                    DVE(lambda e, j=j, xt=xt: e.scalar_tensor_tensor(out=xt[:, j, :], in0=xt[:, j, :],
                                                                      scalar=rstd[:, j:j + 1], in1=fgb,
                                                                      op0=ALU.mult, op1=ALU.mult),
                        [(xname, j), ("rstd", j), "fgb"], [(xname, j)])
                stores.append(DMA("pool", y_d[b, i * T:(i + 1) * T, :].rearrange("(j p) d -> p j d", p=128), xt[:],
                                  [(xname, j_) for j_ in range(4)], [], f"ys{cur}"))
                it += 1
        fin = P.op("pool", lambda e: e.engine_nop(), [], [])
        fin.deps = set(stores[-2:]) if len(stores) >= 2 else set(stores)
        if DBG:
            fin.deps.add(dbg["last"])

        block = es.enter_context(nc.Block())
        P.emit(nc, block, esem, dsem)
        build.P = P
    return nc


_INPUT_NAMES = ["x", "norm1_g", "w_in", "conv_a_w", "conv_b_w", "conv_b_b", "ln_b_g", "ln_b_b", "ln_c_g", "ln_c_b",
                "sgu_w", "sgu_b", "w_out", "norm2_g", "w_up", "conv_f_w", "w_down", "final_g"]

_CACHE = {}


def run(inputs, n_cores, NB, S, DBG=False):
    key = (NB, S, DBG)
    if key not in _CACHE:
        _CACHE[key] = build(NB, S, DBG)
    nc = _CACHE[key]
    x = np.ascontiguousarray(np.asarray(inputs["x"], dtype=np.float32))
    in_maps = []
    for c in range(n_cores):
        m = {k: np.ascontiguousarray(np.asarray(inputs[k], dtype=np.float32)) for k in _INPUT_NAMES if k != "x"}
        m["x"] = np.ascontiguousarray(x[c * NB:(c + 1) * NB])
        in_maps.append(m)
    res = run_bass_kernel_spmd(nc, in_maps, core_ids=list(range(n_cores)))
    if DBG:
        return np.concatenate([r["y"] for r in res.results], axis=0), res.results[0]
    return np.concatenate([r["y"] for r in res.results], axis=0)


def kernel(**inputs):
    x = inputs["x"]
    Btot, S, _ = x.shape
    n_cores = 8
    return run(inputs, n_cores, Btot // n_cores, S).astype(np.float32)
```

```python
import numpy as np
from contextlib import ExitStack
import concourse.bass as bass
import concourse.mybir as mybir
from concourse.bass_utils import run_bass_kernel_spmd

F32 = mybir.dt.float32
BF16 = mybir.dt.bfloat16
AF = mybir.ActivationFunctionType
ALU = mybir.AluOpType

D = 1024
T = 512
DFF = 2816
NFB = 22
NCH = 51
NSLOT = 6
PRE_INTERLEAVE = False
CAST_MIXED = True
FFN_PIPE = True
RMS_EPS = 1e-6
LN_EPS = 1e-5
ENGS = ["pe", "act", "dve", "pool", "sp"]


class Op:
    __slots__ = ("eng", "fn", "deps", "sem", "semval", "idx", "signal", "cnt", "tag")

    def __init__(self, eng, fn, sem):
        self.eng = eng
        self.fn = fn
        self.sem = sem
        self.semval = 0
        self.signal = False
        self.cnt = 0
        self.idx = 0
        self.deps = ()


class Prog:
    def __init__(self):
        self.ops = {e: [] for e in ENGS}
        self.lastw = {}
        self.readers = {}
        self.semcnt = {}
        self.all = []

    def op(self, eng, fn, r=(), w=(), sem=None):
        o = Op(eng, fn, sem)
        o.tag = getattr(self, "curtag", "")
        deps = set()
        for k in r:
            lw = self.lastw.get(k)
            if lw is not None:
                deps.add(lw)
        for k in w:
            lw = self.lastw.get(k)
            if lw is not None:
                deps.add(lw)
            for rd in self.readers.get(k, ()):
                deps.add(rd)
        o.deps = deps
        for k in r:
            self.readers.setdefault(k, []).append(o)
        for k in w:
            self.lastw[k] = o
            self.readers[k] = []
        if sem is not None:
            self.semcnt[sem] = self.semcnt.get(sem, 0) + 16
            o.semval = self.semcnt[sem]
        o.idx = len(self.ops[eng])
        self.ops[eng].append(o)
        self.all.append(o)
        return o

    def emit(self, nc, block, esem, dsem):
        for o in self.all:
            latest = {}
            for d in o.deps:
                if d.sem is not None:
                    continue
                if d.eng == o.eng:
                    if o.eng in ("pe", "sp"):
                        continue
                    if o.idx - d.idx > 2:
                        continue
                cur = latest.get(d.eng)
                if cur is None or d.idx > cur.idx:
                    latest[d.eng] = d
            o.deps = set(d for d in o.deps if d.sem is not None) | set(latest.values())
            for d in latest.values():
                d.signal = True
        for e in ENGS:
            c = 0
            for o in self.ops[e]:
                if o.signal and o.sem is None:
                    c += 1
                o.cnt = c

        def body(ename):
            def f(eng):
                known = {}
                for o in self.ops[ename]:
                    waits = {}
                    for d in o.deps:
                        if d.sem is not None:
                            key, val = ("d", d.sem), d.semval
                        else:
                            if d.eng == o.eng:
                                if o.eng in ("pe", "sp"):
                                    continue
                                if o.idx - d.idx > 2:
                                    continue
                            key, val = ("e", d.eng), d.cnt
                        if known.get(key, 0) >= val:
                            continue
                        if waits.get(key, 0) < val:
                            waits[key] = val
                    if waits and getattr(self, "dbgwaits", None) is not None:
                        self.dbgwaits.append((ename, o.idx, o.tag, dict(waits),
                                              [(d.eng, d.idx, d.tag, d.sem) for d in o.deps]))
                    for key, val in waits.items():
                        s = dsem[key[1]] if key[0] == "d" else esem[key[1]]
                        eng.wait_ge(s, val)
                        known[key] = val
                    ins = o.fn(eng)
                    if o.sem is not None:
                        ins.then_inc(dsem[o.sem], 16)
                    elif o.signal:
                        ins.then_inc(esem[ename], 1)
            return f

        block.tensor(body("pe"))
        block.scalar(body("act"))
        block.vector(body("dve"))
        block.gpsimd(body("pool"))
        block.sync(body("sp"))


def build(NB, S, DBG=False):
    NT = S // T
    nc = bass.Bass("TRN2", target_bir_lowering=False)

    def din(name, shape):
        return nc.dram_tensor(name, list(shape), F32, kind="ExternalInput").ap()

    x_d = din("x", (NB, S, D))
    norm1_g = din("norm1_g", (2, D))
    w_in = din("w_in", (2, D, 2560))
    conv_a_w = din("conv_a_w", (2, 3, 256))
    conv_b_w = din("conv_b_w", (2, 31, 256))
    conv_b_b = din("conv_b_b", (2, 256))
    ln_b_g = din("ln_b_g", (2, 256))
    ln_b_b = din("ln_b_b", (2, 256))
    ln_c_g = din("ln_c_g", (2, 256))
    ln_c_b = din("ln_c_b", (2, 256))
    sgu_w = din("sgu_w", (2, 4, 128, 128))
    sgu_b = din("sgu_b", (2, 4, 128))
    w_out = din("w_out", (2, D, D))
    norm2_g = din("norm2_g", (2, D))
    w_up = din("w_up", (2, D, 2 * DFF))
    conv_f_w = din("conv_f_w", (2, 3, 2 * DFF))
    w_down = din("w_down", (2, DFF, D))
    final_g = din("final_g", (D,))
    y_d = nc.dram_tensor("y", [NB, S, D], F32, kind="ExternalOutput").ap()
    wsc = nc.dram_tensor("wsc", [2 * NCH, 128, 2048], BF16, kind="Internal").ap()
    dbg = {}
    if DBG:
        dbg["hT"] = nc.dram_tensor("d_hT", [128, 8, T], BF16, kind="ExternalOutput").ap()
        dbg["mixT"] = nc.dram_tensor("d_mixT", [128, 8, T], BF16, kind="ExternalOutput").ap()
        dbg["x1"] = nc.dram_tensor("d_x1", [128, 4, D], F32, kind="ExternalOutput").ap()
        dbg["actT"] = nc.dram_tensor("d_actT", [128, NFB, T], BF16, kind="ExternalOutput").ap()
        dbg["x2"] = nc.dram_tensor("d_x2", [128, 4, D], F32, kind="ExternalOutput").ap()

    P = Prog()
    es = ExitStack()
    with es:
        def sb(name, shape, dt=F32):
            return es.enter_context(nc.sbuf_tensor(name, list(shape), dt))

        xb = [sb(f"xb{k}", (128, 4, D)) for k in range(2)]
        hn = [sb(f"hn{k}", (128, D), BF16) for k in range(2)]
        hT = sb("hT", (128, 8, T), BF16)
        mixT = sb("mixT", (128, 8, T), BF16)
        actT = sb("actT", (128, NFB, T), BF16)
        kT = [sb(f"kT{l}", (128, 2, S), BF16) for l in range(2)]
        vS = [sb(f"vS{l}", (128, S // 128, 256), BF16) for l in range(2)]
        qTp = sb("qTp", (128, 4, T), BF16)
        wsl = [sb(f"wsl{k}", (128, 2048), BF16) for k in range(NSLOT)]
        st32 = [sb(f"st32_{k}", (128, 2048)) for k in range(2)]
        st16 = [sb(f"st16_{k}", (128, 2048), BF16) for k in range(2)]
        Fb = [sb(f"F{k}", (128, 516)) for k in range(12)]
        Hb = [sb(f"H{k}", (128, T), BF16) for k in range(4)]
        vnp = [sb(f"vnp{k}", (128, 4, 128), BF16) for k in range(4)]
        ssq = sb("ssq", (128, 4))
        rstd = sb("rstd", (128, 4))
        stat = sb("stat", (128, 4, 8))
        epsr = sb("epsr", (128, 1))
        epsl = sb("epsl", (128, 1))
        one1 = sb("one1", (128, 1))
        ones16 = sb("ones16", (128, 128), BF16)
        avg16 = sb("avg16", (128, 128), BF16)
        ident = sb("ident", (128, 128), BF16)
        ntri = sb("ntri", (128, 128), BF16)
        mask01 = sb("mask01", (128, 128), BF16)
        maskge = sb("maskge", (128, 128), BF16)
        sel = sb("sel", (1, 2, 128), BF16)
        gT1 = [sb(f"gT1_{l}", (128, 8)) for l in range(2)]
        gT2 = [sb(f"gT2_{l}", (128, 8)) for l in range(2)]
        caw = [sb(f"caw{l}", (128, 2, 3)) for l in range(2)]
        cbw = [sb(f"cbw{l}", (128, 2, 31)) for l in range(2)]
        cbb = [sb(f"cbb{l}", (128, 2)) for l in range(2)]
        lbg = [sb(f"lbg{l}", (128, 2)) for l in range(2)]
        lbb = [sb(f"lbb{l}", (128, 2)) for l in range(2)]
        lcg = [sb(f"lcg{l}", (128, 256)) for l in range(2)]
        lcb = [sb(f"lcb{l}", (128, 256)) for l in range(2)]
        cfw = [sb(f"cfw{l}", (128, 2 * NFB, 3)) for l in range(2)]
        WsT = [sb(f"WsT{l}", (128, 4, 128), BF16) for l in range(2)]
        sgub = [sb(f"sgub{l}", (1, 4, 128), BF16) for l in range(2)]
        caH = [sb(f"caH{l}", (128, 2, 2)) for l in range(2)]
        gluH = [sb(f"gluH{l}", (128, 2, 30), BF16) for l in range(2)]
        G16 = [sb(f"G16_{k}", (128, 544), BF16) for k in range(2)]
        G16o = [sb(f"G16o_{k}", (128, 544), BF16) for k in range(2)]
        uH = [sb(f"uH{l}", (128, NFB, 2, 2)) for l in range(2)]
        B = [es.enter_context(nc.psum_tensor(f"B{k}", [128, 512], F32)) for k in range(8)]

        esem = {e: es.enter_context(nc.semaphore(f"se_{e}")) for e in ENGS}
        dnames = ([f"w{k}" for k in range(NSLOT)] + ["ld0", "ld1", "so0", "so1", "xl0", "xl1", "ys0", "ys1", "par", "sg0", "sg1", "sg2", "sg3", "sgb", "dbg"])
        dsem = {n: es.enter_context(nc.semaphore(f"sd_{n}")) for n in dnames}

        def ACT(fn, r, w):
            return P.op("act", fn, r, w)

        def DVE(fn, r, w):
            return P.op("dve", fn, r, w)

        def POOL(fn, r, w):
            return P.op("pool", fn, r, w)

        def PE(fn, r, w):
            return P.op("pe", fn, r, w)

        def act_fn(out, in_, func, **kw):
            return lambda e: e.activation(out=out, in_=in_, func=func, **kw)

        def mm(out, lhsT, rhs, start, stop):
            return lambda e: e.matmul(out, lhsT=lhsT, rhs=rhs, start=start, stop=stop)

        POOL(lambda e: e.memset(ones16[:], 1.0), [], ["ones16"])
        POOL(lambda e: e.memset(avg16[:], 1.0 / 256.0), [], ["avg16"])
        POOL(lambda e: e.memset(epsr[:], RMS_EPS), [], ["epsr"])
        POOL(lambda e: e.memset(epsl[:], LN_EPS), [], ["epsl"])
        POOL(lambda e: e.memset(one1[:], 1.0), [], ["one1"])
        POOL(lambda e: e.affine_select(out=ident[:], in_=ones16[:], pattern=[[-1, 128]], compare_op=ALU.is_equal,
                                       fill=0.0, base=0, channel_multiplier=1), ["ones16"], ["ident"])
        POOL(lambda e: e.memset(ntri[:], -1.0), [], ["ntri"])
        POOL(lambda e: e.affine_select(out=ntri[:], in_=ntri[:], pattern=[[-1, 128]], compare_op=ALU.is_ge,
                                       fill=0.0, base=0, channel_multiplier=1), ["ntri"], ["ntri"])
        POOL(lambda e: e.affine_select(out=mask01[:], in_=ones16[:], pattern=[[1, 128]], compare_op=ALU.is_gt,
                                       fill=0.0, base=0, channel_multiplier=-1), ["ones16"], ["mask01"])
        POOL(lambda e: e.affine_select(out=maskge[:], in_=ones16[:], pattern=[[1, 128]], compare_op=ALU.is_ge,
                                       fill=0.0, base=0, channel_multiplier=-1), ["ones16"], ["maskge"])
        POOL(lambda e: e.memset(sel[:], 0.0), [], ["sel"])
        POOL(lambda e: e.memset(sel[:, 0, 0:64], 1.0), ["sel"], ["sel"])
        POOL(lambda e: e.memset(sel[:, 1, 64:128], 1.0), ["sel"], ["sel"])
        POOL(lambda e: e.memset(qTp[:], 0.0), [], ["qTp0", "qTp1", "qTp2", "qTp3"])
        for j in range(4):
            POOL(lambda e, j=j: e.memset(vnp[j][:], 0.0), [], [f"vnp{j}"])

        def pload(out, in_, w):
            P.op("act", lambda e: e.dma_start(out=out, in_=in_), [], w, sem=None)

        par_ops = []

        def par(out, in_):
            par_ops.append((out, in_))

        nc_ctx = nc.allow_non_contiguous_dma(reason="small one-time parameter loads")
        es.enter_context(nc_ctx)
        for l in range(2):
            par(gT1[l][:], norm1_g[l].rearrange("(c p) -> p c", p=128))
            par(gT2[l][:], norm2_g[l].rearrange("(c p) -> p c", p=128))
            for k in range(3):
                par(caw[l][:, :, k], conv_a_w[l, k].rearrange("(c p) -> p c", p=128))
            for k in range(31):
                par(cbw[l][:, :, k], conv_b_w[l, k].rearrange("(c p) -> p c", p=128))
            par(cbb[l][:], conv_b_b[l].rearrange("(c p) -> p c", p=128))
            par(lbg[l][:], ln_b_g[l].rearrange("(c p) -> p c", p=128))
            par(lbb[l][:], ln_b_b[l].rearrange("(c p) -> p c", p=128))
            par(lcg[l][:], ln_c_g[l].partition_broadcast(128))
            par(lcb[l][:], ln_c_b[l].partition_broadcast(128))
            for k in range(3):
                for q in range(4):
                    par(cfw[l][:, q * 11:(q + 1) * 11, k],
                        conv_f_w[l, k, q * 11 * 128:(q + 1) * 11 * 128].rearrange("(c p) -> p c", p=128))
        npar = len(par_ops)
        par_recs = []
        for (o_, i_) in par_ops:
            par_recs.append(P.op("act", lambda e, o_=o_, i_=i_: e.dma_start(out=o_, in_=i_), [], [], sem="par"))
        last = par_recs[-1]
        for k in ["gT1", "gT2", "caw", "cbw", "cbb", "lbg", "lbb", "lcg", "lcb", "cfw"]:
            P.lastw[k] = last
            P.readers[k] = []

        def DMA(q, out, in_, r, w, sem):
            return P.op(q, lambda e: e.dma_start(out=out, in_=in_), r, w, sem=sem)

        for l in range(2):
            for g in range(4):
                DMA("pool", Fb[g][:, 0:128], sgu_w[l, g], [], [f"F{g}"], f"sg{g}")
            for g in range(4):
                ACT(act_fn(Hb[g][:, 0:128], Fb[g][:, 0:128], AF.Copy), [f"F{g}"], [f"H{g}"])
                PE(lambda e, g=g: e.transpose(B[g][:].bitcast(BF16)[:, 0:128], Hb[g][:, 0:128], ident[:]),
                   [f"H{g}", "ident"], [f"B{g}"])
                DVE(lambda e, l=l, g=g: e.tensor_tensor(out=WsT[l][:, g, :], in0=B[g][:].bitcast(BF16)[:, 0:128],
                                                        in1=maskge[:], op=ALU.mult),
                    [f"B{g}", "maskge"], [f"WsT{l}"])
            DMA("pool", sgub[l][:], sgu_b[l:l + 1], [], [f"sgub{l}"], "sgb")

        def chunk_src(l, k):
            if k < 10:
                src = w_in[l][:, k * 256:(k + 1) * 256].rearrange("(dc p) e -> p dc e", p=128)
                return [(lambda t: t[:].rearrange("p (dc e) -> p dc e", dc=8), src)], gT1[l]
            if k < 14:
                c = k - 10
                src = w_out[l][c * 256:(c + 1) * 256, :].rearrange("(b p) d -> p b d", p=128)
                return [(lambda t: t[:].rearrange("p (b d) -> p b d", b=2), src)], None
            if k < 36:
                fb = k - 14
                s1 = w_up[l][:, fb * 128:(fb + 1) * 128].rearrange("(dc p) e -> p dc e", p=128)
                s2 = w_up[l][:, DFF + fb * 128:DFF + (fb + 1) * 128].rearrange("(dc p) e -> p dc e", p=128)
                return [(lambda t: t[:].rearrange("p (dc e) -> p dc e", dc=8)[:, :, 0:128], s1),
                        (lambda t: t[:].rearrange("p (dc e) -> p dc e", dc=8)[:, :, 128:256], s2)], gT2[l]
            if k >= 47:
                return [], "diag"
            c = k - 36
            src = w_down[l][c * 256:(c + 1) * 256, :].rearrange("(b p) d -> p b d", p=128)
            return [(lambda t: t[:].rearrange("p (b d) -> p b d", b=2), src)], None

        ORDER = [7, 8, 9, 5, 6, 3, 4, 0, 1, 2, 47, 48, 49, 50] + list(range(10, 47))
        PRE = 8

        def prepass_load(g):
            l, k = g // NCH, ORDER[g % NCH]
            s = g % 2
            parts, gain = chunk_src(l, k)
            for pi, (dv, src) in enumerate(parts):
                wn_ = [(f"st32_{s}", pi)] if len(parts) == 2 else [(f"st32_{s}", 0), (f"st32_{s}", 1)]
                DMA("pool", dv(st32[s]), src, [], wn_, f"ld{s}")

        def prepass(g, do_load=True):
            l, k = g // NCH, ORDER[g % NCH]
            s = g % 2
            parts, gain = chunk_src(l, k)
            if do_load:
                prepass_load(g)
            if gain == "diag":
                c = k - 47
                for mm_ in range(16):
                    m = c * 16 + mm_
                    if m >= 62:
                        break
                    cb_, kk = divmod(m, 31)
                    DVE(lambda e, s=s, mm_=mm_, cb_=cb_, kk=kk: e.tensor_scalar(
                        out=st16[s][:, mm_ * 128:(mm_ + 1) * 128], in0=ident[:], scalar1=cbw[l][:, cb_, kk:kk + 1],
                        scalar2=None, op0=ALU.mult), ["ident", "cbw"], [(f"st16_{s}", mm_ // 2)])
                DMA("pool", wsc[l * NCH + k], st16[s][:], [(f"st16_{s}", dc) for dc in range(8)], [("wsc", l, k)], f"so{s}")
                return
            rd = [(f"st32_{s}", 0), (f"st32_{s}", 1)]
            on_act = (g % 2 == 1)
            CENG = DVE if CAST_MIXED else POOL
            if not CAST_MIXED:
                on_act = False
            allw = [(f"st16_{s}", dc) for dc in range(8)]
            if gain is None:
                if on_act:
                    ACT(act_fn(st16[s][:], st32[s][:], AF.Copy), rd, allw)
                else:
                    CENG(lambda e, s=s: e.tensor_copy(out=st16[s][:], in_=st32[s][:]), rd, allw)
            else:
                gname = "gT1" if k < 10 else "gT2"
                for dc in range(8):
                    o_ = st16[s][:, dc * 256:(dc + 1) * 256]
                    i_ = st32[s][:, dc * 256:(dc + 1) * 256]
                    if on_act:
                        ACT(act_fn(o_, i_, AF.Copy, scale=gain[:, dc:dc + 1]), rd + [gname], [(f"st16_{s}", dc)])
                    else:
                        CENG(lambda e, o_=o_, i_=i_, gain=gain, dc=dc: e.tensor_scalar(
                            out=o_, in0=i_, scalar1=gain[:, dc:dc + 1], scalar2=None, op0=ALU.mult),
                            rd + [gname], [(f"st16_{s}", dc)])
            DMA("pool", wsc[l * NCH + k], st16[s][:], allw, [("wsc", l, k)], f"so{s}")

        if PRE_INTERLEAVE:
            for g in range(PRE):
                prepass(g)
        else:
            prepass_load(0)
            prepass_load(1)
            for g in range(2 * NCH):
                prepass(g, do_load=False)
                if g + 2 < 2 * NCH:
                    prepass_load(g + 2)

        aE = st32[0][:, 0:512]
        aL = [st32[0][:, 512:1024], st32[0][:, 1024:1536]]
        aSP = [st16[0][:, 0:512], st16[0][:, 512:1024]]
        aA = [st16[0][:, 1024:1536], st16[0][:, 1536:2048]]
        uS = [st16[1][:, 0:512], st16[1][:, 512:1024]]
        carry = [st32[1][:, 0:512], st32[1][:, 512:1024]]
        fence32b = P.lastw.get(("st16_1", 0))
        assert fence32b is not None
        for nm_ in ["carry0", "carry1"]:
            P.lastw[nm_] = fence32b
        fgb = st32[1][:, 1024:2048]
        P.lastw["fgb"] = fence32b
        DMA("sp", fgb, final_g.partition_broadcast(128), [], ["fgb"], "sgb")
        fence32 = P.lastw.get(("st16_0", 0))
        fence16 = P.lastw.get(("wsc", 1, ORDER[(2 * NCH - 2) % NCH]))
        fence16b = P.lastw.get(("wsc", 1, ORDER[(2 * NCH - 1) % NCH]))
        assert fence32 is not None and fence16 is not None and fence16b is not None
        for nm_ in ["aE", "aL0", "aL1"]:
            P.lastw[nm_] = fence32
        for nm_ in ["aSP0", "aSP1", "aA0", "aA1"]:
            P.lastw[nm_] = fence16
        for nm_ in ["uS0", "uS1"]:
            P.lastw[nm_] = fence16b

        wstate = {"n": 0}

        def wfetch(l, k):
            sl = wstate["n"] % NSLOT
            wstate["n"] += 1
            DMA("sp", wsl[sl][:], wsc[l * NCH + k], [("wsc", l, k)], [f"wsl{sl}"], f"w{sl}")
            return sl

        stream = []
        fetched = {"i": 0}
        slot_of = {}

        def prefetch_upto(pos):
            while fetched["i"] <= pos and fetched["i"] < len(stream):
                l_, k_ = stream[fetched["i"]]
                slot_of[fetched["i"]] = wfetch(l_, k_)
                fetched["i"] += 1

        upos = {"i": 0}

        def next_chunk(prefetch=True):
            pos = upos["i"]
            upos["i"] += 1
            if PRE_INTERLEAVE and pos + PRE < 2 * NCH:
                prepass(pos + PRE)
            if prefetch:
                prefetch_upto(pos + NSLOT - 1)
            sl = slot_of[pos]
            return wsl[sl], f"wsl{sl}"

        def catch_up():
            prefetch_upto(upos["i"] - 1 + NSLOT - 1)

        for b in range(NB):
            for i in range(NT):
                for l in range(2):
                    for k in ORDER:
                        stream.append((l, k))

        def rms_stats(xt, xname):
            DVE(lambda e: e.memset(ssq[:], 0.0), [], [("ssq", j) for j in range(4)])

            def lnexp(j):
                ACT(act_fn(rstd[:, j:j + 1], ssq[:, j:j + 1], AF.Ln, bias=epsr[:, 0:1], scale=1.0 / D),
                    [("ssq", j), "epsr"], [("rstd", j)])
                ACT(act_fn(rstd[:, j:j + 1], rstd[:, j:j + 1], AF.Exp, scale=-0.5), [("rstd", j)], [("rstd", j)])

            for j in range(4):
                ACT(act_fn(Fb[11][:].bitcast(BF16)[:, 0:D], xt[:, j, :], AF.Square, accum_out=ssq[:, j:j + 1]),
                    [(xname, j)], [("ssq", j)])
                if j >= 1:
                    lnexp(j - 1)
            lnexp(3)

        def norm_T(xt, xname):
            rms_stats(xt, xname)
            for j in range(4):
                hb = hn[j % 2]
                hname = f"hn{j % 2}"
                DVE(lambda e, j=j, hb=hb: e.tensor_scalar(out=hb[:], in0=xt[:, j, :], scalar1=rstd[:, j:j + 1],
                                                          scalar2=None, op0=ALU.mult), [(xname, j), ("rstd", j)], [hname])
                bk = j % 2
                for c in range(8):
                    PE(lambda e, c=c, hb=hb, bk=bk: e.transpose(B[bk][:].bitcast(BF16)[:, c * 128:(c + 1) * 128],
                                                                hb[:, c * 128:(c + 1) * 128], ident[:]),
                       [hname, "ident"], [f"B{bk}"])
                ev = ACT if j % 2 == 0 else DVE
                if j % 2 == 0:
                    ACT(act_fn(hT[:, :, j * 128:(j + 1) * 128],
                               B[bk][:].bitcast(BF16).rearrange("p (c t) -> p c t", c=8), AF.Copy),
                        [f"B{bk}"], ["hT"])
                else:
                    DVE(lambda e, j=j, bk=bk: e.tensor_copy(out=hT[:, :, j * 128:(j + 1) * 128],
                                                            in_=B[bk][:].bitcast(BF16).rearrange("p (c t) -> p c t", c=8)),
                        [f"B{bk}"], ["hT"])

        def fm_block(wt, wname, col0, bank):
            wv = wt[:].rearrange("p (dc e) -> p dc e", dc=8)
            for dc in range(8):
                PE(mm(B[bank][:], wv[:, dc, col0:col0 + 128], hT[:, dc, :], dc == 0, dc == 7),
                   [wname, "hT"], [f"B{bank}"])

        def tm_block(wt, wname, j, bank, half):
            wv = wt[:].rearrange("p (dc e) -> p dc e", dc=8)
            for dc in range(8):
                PE(mm(B[bank][:, half * 256:(half + 1) * 256], hT[:, dc, j * 128:(j + 1) * 128], wv[:, dc, :],
                      dc == 0, dc == 7), [wname, "hT"], [f"B{bank}"])

        def dwconv(eng, buf, bname, acc, aname, wt, wname, widx, K, first_bias=None):
            wsel = lambda k: wt[:, widx, k:k + 1]
            if first_bias is None:
                eng(lambda e: e.tensor_scalar(out=acc[:, 0:T], in0=buf[:, K - 1:K - 1 + T], scalar1=wsel(K - 1),
                                              scalar2=None, op0=ALU.mult), [bname, wname], [aname])
            else:
                eng(lambda e: e.tensor_scalar(out=acc[:, 0:T], in0=buf[:, K - 1:K - 1 + T], scalar1=wsel(K - 1),
                                              scalar2=first_bias, op0=ALU.mult, op1=ALU.add),
                    [bname, wname, "cbb"], [aname])
            for k in range(K - 2, -1, -1):
                eng(lambda e, k=k: e.scalar_tensor_tensor(out=acc[:, 0:T], in0=buf[:, k:k + T], scalar=wsel(k),
                                                          in1=acc[:, 0:T], op0=ALU.mult, op1=ALU.add),
                    [bname, wname, aname], [aname])

        def halo_in(eng, buf, bname, hal, hname, H, first):
            if first:
                eng(lambda e: e.memset(buf[:, 0:H], 0.0), [], [bname])
            else:
                eng(lambda e: e.tensor_copy(out=buf[:, 0:H], in_=hal), [hname], [bname])

        def halo_out(eng, buf, bname, hal, hname, H):
            eng(lambda e: e.tensor_copy(out=hal, in_=buf[:, T:T + H]), [bname], [hname])

        def proj_residual(xt, xname, nchunks, lhs_of, lres_of):
            nkb = 2 * nchunks
            for c in range(nchunks - 2):
                wt, wn = next_chunk()
                wv = wt[:].rearrange("p (b d) -> p b d", b=2)
                for kbl in range(2):
                    kb = 2 * c + kbl
                    for j in range(4):
                        for dh in range(2):
                            bk = j * 2 + dh
                            PE(mm(B[bk][:], lhs_of(kb, j), wv[:, kbl, dh * 512:(dh + 1) * 512], kb == 0, False),
                               [lres_of(kb), wn], [f"B{bk}"])
            wA = next_chunk()
            wB = next_chunk(prefetch=False)
            for j in range(4):
                for dh in range(2):
                    bk = j * 2 + dh
                    for ci, (wt, wn) in enumerate((wA, wB)):
                        wv = wt[:].rearrange("p (b d) -> p b d", b=2)
                        for kbl in range(2):
                            kb = 2 * (nchunks - 2 + ci) + kbl
                            PE(mm(B[bk][:], lhs_of(kb, j), wv[:, kbl, dh * 512:(dh + 1) * 512], kb == 0, kb == nkb - 1),
                               [lres_of(kb), wn], [f"B{bk}"])
                    DVE(lambda e, j=j, dh=dh, bk=bk: e.tensor_tensor(out=xt[:, j, dh * 512:(dh + 1) * 512],
                                                                     in0=xt[:, j, dh * 512:(dh + 1) * 512],
                                                                     in1=B[bk][:], op=ALU.add),
                        [(xname, j), f"B{bk}"], [(xname, j)])
            catch_up()

        def layer(xt, xname, l, i):
            first = (i == 0)
            norm_T(xt, xname)
            nb = {"n": 0}

            def nbank():
                b_ = 2 + nb["n"] % 6
                nb["n"] += 1
                return b_

            wt, wn = next_chunk()
            for hp in range(2):
                bk = nbank()
                fm_block(wt, wn, hp * 128, bk)
                for gg in range(2):
                    h = 2 * hp + gg
                    DVE(lambda e, gg=gg, h=h, bk=bk: e.tensor_scalar(
                        out=qTp[gg * 64:(gg + 1) * 64, h, :], in0=B[bk][gg * 64:(gg + 1) * 64, :], scalar1=0.125,
                        scalar2=None, op0=ALU.mult), [f"B{bk}"], [f"qTp{h}"])
            wt, wn = next_chunk()
            for hp in range(2):
                bk = nbank()
                fm_block(wt, wn, hp * 128, bk)
                DVE(lambda e, hp=hp, bk=bk: e.tensor_copy(out=kT[l][:, hp, i * T:(i + 1) * T], in_=B[bk][:]),
                    [f"B{bk}"], [f"kT{l}"])
            wt, wn = next_chunk()
            for j in range(4):
                bk, half = nbank(), 0
                tm_block(wt, wn, j, bk, half)
                DVE(lambda e, j=j, bk=bk, half=half: e.tensor_copy(out=vS[l][:, i * 4 + j, :],
                                                                   in_=B[bk][:, half * 256:(half + 1) * 256]),
                    [f"B{bk}"], [f"vS{l}"])
            wt, wn = next_chunk()
            for hp in range(2):
                bk = nbank()
                fm_block(wt, wn, hp * 128, bk)
                ACT(act_fn(uS[hp], B[bk][:], AF.Gelu), [f"B{bk}"], [f"uS{hp}"])
            wt, wn = next_chunk()
            for j in range(4):
                bk, half = nbank(), 0
                tm_block(wt, wn, j, bk, half)
                vg = Fb[j]
                vn_ = f"F{j}"
                ACT(act_fn(vg[:, 0:256], B[bk][:, half * 256:(half + 1) * 256], AF.Gelu), [f"B{bk}"], [vn_])
            for j in range(4):
                vg = Fb[j]
                vn_ = f"F{j}"
                DVE(lambda e, j=j, vg=vg: e.bn_stats(out=stat[:, j, 0:6], in_=vg[:, 0:256]), [vn_], [("stat", j)])
                DVE(lambda e, j=j: e.bn_aggr(out=stat[:, j, 6:8], in_=stat[:, j, 0:6]), [("stat", j)], [("stat", j)])
            for j in range(4):
                ACT(act_fn(stat[:, j, 7:8], stat[:, j, 7:8], AF.Ln, bias=epsl[:, 0:1]), [("stat", j), "epsl"], [("stat", j)])
            for j in range(4):
                ACT(act_fn(stat[:, j, 7:8], stat[:, j, 7:8], AF.Exp, scale=-0.5), [("stat", j)], [("stat", j)])

            def gen_mix():
                for j in range(4):
                    vg = Fb[j]
                    vn_ = f"F{j}"
                    DVE(lambda e, j=j, vg=vg: e.tensor_scalar(out=vg[:, 0:256], in0=vg[:, 0:256], scalar1=stat[:, j, 6:7],
                                                              scalar2=stat[:, j, 7:8], op0=ALU.subtract, op1=ALU.mult),
                        [vn_, ("stat", j)], [vn_])
                    POOL(lambda e, vg=vg: e.tensor_tensor(out=vg[:, 0:256], in0=vg[:, 0:256], in1=lcg[l][:], op=ALU.mult),
                         [vn_, "lcg"], [vn_])
                    for gg in range(2):
                        POOL(lambda e, j=j, vg=vg, gg=gg: e.tensor_tensor(
                            out=vnp[j][:, gg:4:2, gg * 64:(gg + 1) * 64],
                            in0=vg[:, 0:256].rearrange("p (g c) -> p g c", g=4)[:, gg:4:2, :],
                            in1=lcb[l][:].rearrange("p (g c) -> p g c", g=4)[:, gg:4:2, :], op=ALU.add),
                            [vn_, "lcb"], [f"vnp{j}"])
                    yield
                for hp in range(2):
                    for j in range(4):
                        for gg in range(2):
                            g = 2 * hp + gg
                            PE(mm(B[hp][:, j * 128:(j + 1) * 128], vnp[j][:, g, :], WsT[l][:, g, :], gg == 0, False),
                               [f"vnp{j}", f"WsT{l}"], [f"B{hp}"])
                        for gg in range(2):
                            g = 2 * hp + gg
                            PE(mm(B[hp][:, j * 128:(j + 1) * 128], sel[:, gg, :], sgub[l][:, g, :], False, gg == 1),
                               ["sel", f"sgub{l}"], [f"B{hp}"])
                    DVE(lambda e, hp=hp: e.tensor_tensor(out=mixT[:, 4 + hp, :], in0=uS[hp], in1=B[hp][:], op=ALU.mult),
                        [f"uS{hp}", f"B{hp}"], [("mixT", 4 + hp)])
                    yield
                mb = {"n": 0}

                def mbank():
                    b_ = mb["n"] % 2
                    mb["n"] += 1
                    return b_

                wt, wn = next_chunk()
                for cb in range(2):
                    bk = mbank()
                    fm_block(wt, wn, cb * 128, bk)
                    DVE(lambda e, cb=cb, bk=bk: e.tensor_copy(out=Fb[cb][:, 0:T], in_=B[bk][:]), [f"B{bk}"], [f"F{cb}"])
                    yield
                wt, wn = next_chunk()
                for cb in range(2):
                    bk = mbank()
                    fm_block(wt, wn, cb * 128, bk)
                    sg = Fb[2 + cb]
                    sn = f"F{2 + cb}"
                    ACT(act_fn(sg[:, 0:T], B[bk][:], AF.Exp, scale=-1.0), [f"B{bk}"], [sn])
                    ACT(act_fn(sg[:, 0:T], sg[:, 0:T], AF.Ln, bias=one1[:, 0:1]), [sn, "one1"], [sn])
                    ACT(act_fn(sg[:, 0:T], sg[:, 0:T], AF.Exp, scale=-1.0), [sn], [sn])
                    gl = G16[cb]
                    gn = f"G16_{cb}"
                    halo_in(POOL, gl, gn, gluH[l][:, cb, :], f"gluH{l}", 30, first)
                    POOL(lambda e, cb=cb, gl=gl: e.tensor_tensor(out=gl[:, 30:30 + T], in0=Fb[cb][:, 0:T],
                                                                 in1=Fb[2 + cb][:, 0:T], op=ALU.mult),
                         [f"F{cb}", sn], [gn])
                    halo_out(POOL, gl, gn, gluH[l][:, cb, :], f"gluH{l}", 30)
                    POOL(lambda e, cb=cb, gl=gl: e.tensor_copy(out=G16o[cb][:, 0:541], in_=gl[:, 1:542]),
                         [gn], [f"G16o_{cb}"])
                    yield
                wt, wn = next_chunk()
                for cb in range(2):
                    bk = mbank()
                    fm_block(wt, wn, cb * 128, bk)
                    DVE(lambda e, cb=cb, bk=bk: e.tensor_copy(out=Fb[cb][:, 0:T], in_=B[bk][:]), [f"B{bk}"], [f"F{cb}"])
                    yield
                wt, wn = next_chunk()
                for cb in range(2):
                    bk = mbank()
                    fm_block(wt, wn, cb * 128, bk)
                    DVE(lambda e, cb=cb, bk=bk: e.tensor_copy(out=Fb[2 + cb][:, 0:T], in_=B[bk][:]), [f"B{bk}"], [f"F{2 + cb}"])
                    yield
                wt, wn = next_chunk()
                for cb in range(2):
                    bk = mbank()
                    fm_block(wt, wn, cb * 128, bk)
                    ca = Fb[4 + cb]
                    cn = f"F{4 + cb}"
                    halo_in(POOL, ca, cn, caH[l][:, cb, :], f"caH{l}", 2, first)
                    DVE(lambda e, cb=cb, bk=bk, ca=ca: e.tensor_tensor(out=ca[:, 2:2 + T], in0=Fb[2 + cb][:, 0:T],
                                                                        in1=B[bk][:], op=ALU.mult),
                        [f"F{2 + cb}", f"B{bk}"], [cn])
                    halo_out(POOL, ca, cn, caH[l][:, cb, :], f"caH{l}", 2)
                    yield
                    dwconv(DVE, ca, cn, Fb[6 + cb], f"F{6 + cb}", caw[l], "caw", cb, 3)
                    POOL(lambda e, cb=cb: e.tensor_tensor(out=mixT[:, cb, :], in0=Fb[cb][:, 0:T], in1=Fb[6 + cb][:, 0:T],
                                                          op=ALU.mult), [f"F{cb}", f"F{6 + cb}"], [("mixT", cb)])
                    yield
                dwt = None
                for m in range(62):
                    cb, k = divmod(m, 31)
                    if m % 16 == 0:
                        dwt = next_chunk()
                    gl = G16[cb]
                    gn = f"G16_{cb}"
                    rhs_ = gl[:, k:k + T] if k % 2 == 0 else G16o[cb][:, k - 1:k - 1 + T]
                    PE(mm(B[cb][:], dwt[0][:, (m % 16) * 128:(m % 16 + 1) * 128], rhs_, k == 0, k == 30),
                       [gn, f"G16o_{cb}", dwt[1]], [f"B{cb}"])
                    if k % 8 == 7:
                        yield
                    if k == 30:
                        acc, an = Fb[6 + cb], f"F{6 + cb}"
                        ACT(act_fn(Hb[2 + cb][:], B[cb][:], AF.Square, bias=cbb[l][:, cb:cb + 1]), [f"B{cb}", "cbb"],
                            [f"H{2 + cb}", f"B{cb}"])
                        DVE(lambda e, cb=cb, acc=acc: e.tensor_scalar(out=acc[:, 0:T], in0=B[cb][:], scalar1=cbb[l][:, cb:cb + 1],
                                                                      scalar2=None, op0=ALU.add), [f"B{cb}", "cbb"], [an, f"B{cb}"])
                        DVE(lambda e, cb=cb, acc=acc: e.tensor_copy(out=Hb[cb][:], in_=acc[:, 0:T]), [an], [f"H{cb}"])
                        yield
                for cb in range(2):
                    PE(mm(B[0][:], avg16[:], Hb[cb][:], cb == 0, cb == 1), [f"H{cb}", "avg16"], ["B0"])
                for cb in range(2):
                    PE(mm(B[1][:], avg16[:], Hb[2 + cb][:], cb == 0, cb == 1), [f"H{2 + cb}", "avg16"], ["B1"])
                ACT(act_fn(Fb[0][:, 0:T], B[0][:], AF.Square), ["B0"], ["F0", "B0"])
                DVE(lambda e: e.tensor_tensor(out=Fb[1][:, 0:T], in0=B[1][:], in1=Fb[0][:, 0:T], op=ALU.subtract),
                    ["B1", "F0"], ["F1"])
                ACT(act_fn(Fb[1][:, 0:T], Fb[1][:, 0:T], AF.Ln, bias=epsl[:, 0:1]), ["F1", "epsl"], ["F1"])
                ACT(act_fn(Fb[1][:, 0:T], Fb[1][:, 0:T], AF.Exp, scale=-0.5), ["F1"], ["F1"])
                yield
                for cb in range(2):
                    tt, tn = Fb[2 + cb], f"F{2 + cb}"
                    sg, sn = Fb[4 + cb], f"F{4 + cb}"
                    DVE(lambda e, cb=cb, tt=tt: e.tensor_tensor(out=tt[:, 0:T], in0=Fb[6 + cb][:, 0:T], in1=B[0][:],
                                                                op=ALU.subtract), [f"F{6 + cb}", "B0"], [tn])
                    POOL(lambda e, tt=tt: e.tensor_tensor(out=tt[:, 0:T], in0=tt[:, 0:T], in1=Fb[1][:, 0:T], op=ALU.mult),
                         [tn, "F1"], [tn])
                    ACT(act_fn(tt[:, 0:T], tt[:, 0:T], AF.Identity, scale=lbg[l][:, cb:cb + 1], bias=lbb[l][:, cb:cb + 1]),
                        [tn, "lbg", "lbb"], [tn])
                    ACT(act_fn(sg[:, 0:T], tt[:, 0:T], AF.Exp, scale=-1.0), [tn], [sn])
                    ACT(act_fn(sg[:, 0:T], sg[:, 0:T], AF.Ln, bias=one1[:, 0:1]), [sn, "one1"], [sn])
                    ACT(act_fn(sg[:, 0:T], sg[:, 0:T], AF.Exp, scale=-1.0), [sn], [sn])
                    POOL(lambda e, cb=cb, tt=tt, sg=sg: e.tensor_tensor(out=mixT[:, 2 + cb, :], in0=tt[:, 0:T],
                                                                        in1=sg[:, 0:T], op=ALU.mult),
                         [tn, sn], [("mixT", 2 + cb)])
                    yield

            gen = gen_mix()
            NYIELD = 36

            def advance(k):
                for _ in range(k):
                    try:
                        next(gen)
                    except StopIteration:
                        return False
                return True

            steps = []
            for h in range(4):
                for kb in range(4 * i + 3, -1, -1):
                    steps.append((h, kb))
            nst = len(steps)

            def geom(h, kb):
                jj = kb - 4 * i
                c0 = max(jj, 0) * 128
                return c0, T - c0, jj >= 0

            def S1(n):
                h, kb = steps[n]
                c0, N, diag = geom(h, kb)
                hp = h // 2
                zb = 2 + n % 2
                E, En = aE, "aE"
                SPt, SPn = aSP[n % 2], f"aSP{n % 2}"
                PE(mm(B[zb][:, 0:N], kT[l][:, hp, kb * 128:(kb + 1) * 128], qTp[:, h, c0:T], True, True),
                   [f"kT{l}", f"qTp{h}"], [f"B{zb}"])
                ACT(act_fn(E[:, 0:N], B[zb][:, 0:N], AF.Exp), [f"B{zb}"], [En])
                ACT(act_fn(SPt[:, 0:N], E[:, 0:N], AF.Ln, bias=one1[:, 0:1]), [En, "one1"], [SPn])
                if diag:
                    POOL(lambda e: e.tensor_tensor(out=SPt[:, 0:128], in0=SPt[:, 0:128], in1=mask01[:], op=ALU.mult),
                         [SPn, "mask01"], [SPn])

            def S2(n):
                h, kb = steps[n]
                c0, N, diag = geom(h, kb)
                hp = h // 2
                cbk = 4 + n % 2
                rbk = 6
                SPt, SPn = aSP[n % 2], f"aSP{n % 2}"
                L, Ln_ = aL[n % 2], f"aL{n % 2}"
                A, An = aA[n % 2], f"aA{n % 2}"
                cr, crn = carry[h % 2], f"carry{h % 2}"
                PE(mm(B[cbk][:, 0:N], ntri[:], SPt[:, 0:N], True, False), [SPn, "ntri"], [f"B{cbk}"])
                PE(mm(B[cbk][:, 0:N], kT[l][:, hp, kb * 128:(kb + 1) * 128], qTp[:, h, c0:T], False, True),
                   [f"kT{l}", f"qTp{h}"], [f"B{cbk}"])
                if kb > 0:
                    PE(mm(B[rbk][:, 0:N], ones16[:], SPt[:, 0:N], True, True), [SPn, "ones16"], [f"B{rbk}"])
                topk = (kb == 4 * i + 3)
                if topk:
                    DVE(lambda e: e.memset(cr, 0.0), [], [crn])
                DVE(lambda e: e.tensor_tensor(out=L[:, 0:N], in0=B[cbk][:, 0:N], in1=cr[:, c0:T], op=ALU.subtract),
                    [f"B{cbk}", crn], [Ln_])
                if kb > 0:
                    DVE(lambda e: e.tensor_tensor(out=cr[:, c0:T], in0=B[rbk][:, 0:N], in1=cr[:, c0:T], op=ALU.add),
                        [f"B{rbk}", crn], [crn])
                ACT(act_fn(A[:, 0:N], L[:, 0:N], AF.Exp), [Ln_], [An])
                if diag:
                    POOL(lambda e: e.tensor_tensor(out=A[:, 0:128], in0=A[:, 0:128], in1=mask01[:], op=ALU.mult),
                         [An, "mask01"], [An])

            def S3(n):
                h, kb = steps[n]
                c0, N, diag = geom(h, kb)
                hp, gg = h // 2, h % 2
                A, An = aA[n % 2], f"aA{n % 2}"
                PE(mm(B[7][gg * 64:(gg + 1) * 64, c0:T], vS[l][:, kb, h * 64:(h + 1) * 64], A[:, 0:N],
                      kb == 4 * i + 3, kb == 0), [An, f"vS{l}"], ["B7"])
                if kb == 0 and gg == 1:
                    ACT(act_fn(mixT[:, 6 + hp, :], B[7][:], AF.Copy), ["B7"], [("mixT", 6 + hp)])

            per = -(-NYIELD // nst)
            advance(4)
            for n in range(nst + 2):
                if n < nst:
                    S1(n)
                if 0 <= n - 1 < nst:
                    S2(n - 1)
                if 0 <= n - 2 < nst:
                    S3(n - 2)
                advance(per)
            while advance(1):
                pass

            if DBG and l == 0 and i == 0 and not dbg.get("done1"):
                DMA("pool", dbg["hT"], hT[:], ["hT"], [], "dbg")
                DMA("pool", dbg["mixT"], mixT[:], [("mixT", e_) for e_ in range(8)], [], "dbg")
            proj_residual(xt, xname, 4, lambda kb, j: mixT[:, kb, j * 128:(j + 1) * 128], lambda kb: ("mixT", kb))
            if DBG and l == 0 and i == 0 and not dbg.get("done1"):
                DMA("pool", dbg["x1"], xt[:], [(xname, j_) for j_ in range(4)], [], "dbg")
            norm_T(xt, xname)
            ffn_pend = []

            def ffn_finish(fb, par_, tb, tbn):
                ACT(act_fn(Hb[par_][:], tb[0][:, 0:T], AF.Silu), [tbn[0]], [f"H{par_}"])
                POOL(lambda e: e.tensor_tensor(out=actT[:, fb, :], in0=Hb[par_][:], in1=tb[1][:, 0:T], op=ALU.mult),
                     [f"H{par_}", tbn[1]], [("actT", fb)])

            def ffn_bufs(fb):
                par_ = fb % 3
                return (par_, [Fb[4 * par_], Fb[4 * par_ + 1]], [f"F{4 * par_}", f"F{4 * par_ + 1}"],
                        [Fb[4 * par_ + 2], Fb[4 * par_ + 3]], [f"F{4 * par_ + 2}", f"F{4 * par_ + 3}"])

            def ffn_halo_in(fb):
                par_, ub, ubn, tb, tbn = ffn_bufs(fb)
                for gv in range(2):
                    halo_in(POOL, ub[gv], ubn[gv], uH[l][:, fb, gv, :], f"uH{l}", 2, first)

            ffn_halo_in(0)
            for fb in range(NFB):
                P.curtag = f"ffn{fb}"
                wt, wn = next_chunk()
                par_, ub, ubn, tb, tbn = ffn_bufs(fb)
                for gv in range(2):
                    bk = 2 + 2 * par_ + gv
                    fm_block(wt, wn, gv * 128, bk)
                    widx = gv * NFB + fb
                    ACT(act_fn(ub[gv][:, 2:2 + T], B[bk][:], AF.Copy), [f"B{bk}"], [ubn[gv]])
                    ACT(act_fn(tb[gv][:, 0:T], B[bk][:], AF.Copy, scale=cfw[l][:, widx, 2:3]), [f"B{bk}", "cfw"], [tbn[gv]])
                    halo_out(POOL, ub[gv], ubn[gv], uH[l][:, fb, gv, :], f"uH{l}", 2)
                    for k in (1, 0):
                        DVE(lambda e, k=k, gv=gv, widx=widx, tb=tb, ub=ub: e.scalar_tensor_tensor(
                            out=tb[gv][:, 0:T], in0=ub[gv][:, k:k + T], scalar=cfw[l][:, widx, k:k + 1],
                            in1=tb[gv][:, 0:T], op0=ALU.mult, op1=ALU.add), [ubn[gv], tbn[gv], "cfw"], [tbn[gv]])
                if fb + 1 < NFB:
                    ffn_halo_in(fb + 1)
                if ffn_pend:
                    ffn_finish(*ffn_pend[0])
                    ffn_pend.clear()
                ffn_pend.append((fb, par_, tb, tbn))
            if ffn_pend:
                ffn_finish(*ffn_pend[0])
                ffn_pend.clear()
            P.curtag = "wdown"
            proj_residual(xt, xname, 11, lambda kb, j: actT[:, kb, j * 128:(j + 1) * 128], lambda kb: ("actT", kb))
            if DBG and l == 0 and i == 0 and not dbg.get("done1"):
                DMA("pool", dbg["actT"], actT[:], [("actT", f_) for f_ in range(NFB)], [], "dbg")
                dbg["last"] = DMA("pool", dbg["x2"], xt[:], [(xname, j_) for j_ in range(4)], [], "dbg")
                dbg["done1"] = True

        it = 0
        nxt = DMA("sp", xb[0][:], x_d[0, 0:T, :].rearrange("(j p) d -> p j d", p=128), [], [("xb0", j_) for j_ in range(4)], "xl0")
        stores = []
        for b in range(NB):
            for i in range(NT):
                cur = it % 2
                xt, xname = xb[cur], f"xb{cur}"
                if it + 1 < NB * NT:
                    b2, i2 = divmod(it + 1, NT)
                    o2 = (it + 1) % 2
                    DMA("sp", xb[o2][:], x_d[b2, i2 * T:(i2 + 1) * T, :].rearrange("(j p) d -> p j d", p=128),
                        [], [(f"xb{o2}", j_) for j_ in range(4)], f"xl{o2}")
                for l in range(2):
                    layer(xt, xname, l, i)
                rms_stats(xt, xname)
                for j in range(4):
                    DVE(lambda e, j=j, xt=xt: e.scalar_tensor_tensor(out=xt[:, j, :], in0=xt[:, j, :],
                                                                      scalar=rstd[:, j:j + 1], in1=fgb,
                                                                      op0=ALU.mult, op1=ALU.mult),
                        [(xname, j), ("rstd", j), "fgb"], [(xname, j)])
                stores.append(DMA("pool", y_d[b, i * T:(i + 1) * T, :].rearrange("(j p) d -> p j d", p=128), xt[:],
                                  [(xname, j_) for j_ in range(4)], [], f"ys{cur}"))
                it += 1
        fin = P.op("pool", lambda e: e.engine_nop(), [], [])
        fin.deps = set(stores[-2:]) if len(stores) >= 2 else set(stores)
        if DBG:
            fin.deps.add(dbg["last"])

        block = es.enter_context(nc.Block())
        P.emit(nc, block, esem, dsem)
        build.P = P
    return nc


_INPUT_NAMES = ["x", "norm1_g", "w_in", "conv_a_w", "conv_b_w", "conv_b_b", "ln_b_g", "ln_b_b", "ln_c_g", "ln_c_b",
                "sgu_w", "sgu_b", "w_out", "norm2_g", "w_up", "conv_f_w", "w_down", "final_g"]

_CACHE = {}


def run(inputs, n_cores, NB, S, DBG=False):
    key = (NB, S, DBG)
    if key not in _CACHE:
        _CACHE[key] = build(NB, S, DBG)
    nc = _CACHE[key]
    x = np.ascontiguousarray(np.asarray(inputs["x"], dtype=np.float32))
    in_maps = []
    for c in range(n_cores):
        m = {k: np.ascontiguousarray(np.asarray(inputs[k], dtype=np.float32)) for k in _INPUT_NAMES if k != "x"}
        m["x"] = np.ascontiguousarray(x[c * NB:(c + 1) * NB])
        in_maps.append(m)
    res = run_bass_kernel_spmd(nc, in_maps, core_ids=list(range(n_cores)))
    if DBG:
        return np.concatenate([r["y"] for r in res.results], axis=0), res.results[0]
    return np.concatenate([r["y"] for r in res.results], axis=0)


def kernel(**inputs):
    x = inputs["x"]
    Btot, S, _ = x.shape
    n_cores = 8
    return run(inputs, n_cores, Btot // n_cores, S).astype(np.float32)
```

```python
import numpy as np
from contextlib import ExitStack
import concourse.bass as bass
import concourse.mybir as mybir
from concourse.bass_utils import run_bass_kernel_spmd

F32 = mybir.dt.float32
BF16 = mybir.dt.bfloat16
AF = mybir.ActivationFunctionType
ALU = mybir.AluOpType

D = 1024
T = 512
DFF = 2816
NFB = 22
NCH = 51
NSLOT = 6
PRE_INTERLEAVE = False
CAST_MIXED = True
FFN_PIPE = True
RMS_EPS = 1e-6
LN_EPS = 1e-5
ENGS = ["pe", "act", "dve", "pool", "sp"]


class Op:
    __slots__ = ("eng", "fn", "deps", "sem", "semval", "idx", "signal", "cnt", "tag")

    def __init__(self, eng, fn, sem):
        self.eng = eng
        self.fn = fn
        self.sem = sem
        self.semval = 0
        self.signal = False
        self.cnt = 0
        self.idx = 0
        self.deps = ()


class Prog:
    def __init__(self):
        self.ops = {e: [] for e in ENGS}
        self.lastw = {}
        self.readers = {}
        self.semcnt = {}
        self.all = []

    def op(self, eng, fn, r=(), w=(), sem=None):
        o = Op(eng, fn, sem)
        o.tag = getattr(self, "curtag", "")
        deps = set()
        for k in r:
            lw = self.lastw.get(k)
            if lw is not None:
                deps.add(lw)
        for k in w:
            lw = self.lastw.get(k)
            if lw is not None:
                deps.add(lw)
            for rd in self.readers.get(k, ()):
                deps.add(rd)
        o.deps = deps
        for k in r:
            self.readers.setdefault(k, []).append(o)
        for k in w:
            self.lastw[k] = o
            self.readers[k] = []
        if sem is not None:
            self.semcnt[sem] = self.semcnt.get(sem, 0) + 16
            o.semval = self.semcnt[sem]
        o.idx = len(self.ops[eng])
        self.ops[eng].append(o)
        self.all.append(o)
        return o

    def emit(self, nc, block, esem, dsem):
        for o in self.all:
            latest = {}
            for d in o.deps:
                if d.sem is not None:
                    continue
                if d.eng == o.eng:
                    if o.eng in ("pe", "sp"):
                        continue
                    if o.idx - d.idx > 2:
                        continue
                cur = latest.get(d.eng)
                if cur is None or d.idx > cur.idx:
                    latest[d.eng] = d
            o.deps = set(d for d in o.deps if d.sem is not None) | set(latest.values())
            for d in latest.values():
                d.signal = True
        for e in ENGS:
            c = 0
            for o in self.ops[e]:
                if o.signal and o.sem is None:
                    c += 1
                o.cnt = c

        def body(ename):
            def f(eng):
                known = {}
                for o in self.ops[ename]:
                    waits = {}
                    for d in o.deps:
                        if d.sem is not None:
                            key, val = ("d", d.sem), d.semval
                        else:
                            if d.eng == o.eng:
                                if o.eng in ("pe", "sp"):
                                    continue
                                if o.idx - d.idx > 2:
                                    continue
                            key, val = ("e", d.eng), d.cnt
                        if known.get(key, 0) >= val:
                            continue
                        if waits.get(key, 0) < val:
                            waits[key] = val
                    if waits and getattr(self, "dbgwaits", None) is not None:
                        self.dbgwaits.append((ename, o.idx, o.tag, dict(waits),
                                              [(d.eng, d.idx, d.tag, d.sem) for d in o.deps]))
                    for key, val in waits.items():
                        s = dsem[key[1]] if key[0] == "d" else esem[key[1]]
                        eng.wait_ge(s, val)
                        known[key] = val
                    ins = o.fn(eng)
                    if o.sem is not None:
                        ins.then_inc(dsem[o.sem], 16)
                    elif o.signal:
                        ins.then_inc(esem[ename], 1)
            return f

        block.tensor(body("pe"))
        block.scalar(body("act"))
        block.vector(body("dve"))
        block.gpsimd(body("pool"))
        block.sync(body("sp"))


def build(NB, S, DBG=False):
    NT = S // T
    nc = bass.Bass("TRN2", target_bir_lowering=False)

    def din(name, shape):
        return nc.dram_tensor(name, list(shape), F32, kind="ExternalInput").ap()

    x_d = din("x", (NB, S, D))
    norm1_g = din("norm1_g", (2, D))
    w_in = din("w_in", (2, D, 2560))
    conv_a_w = din("conv_a_w", (2, 3, 256))
    conv_b_w = din("conv_b_w", (2, 31, 256))
    conv_b_b = din("conv_b_b", (2, 256))
    ln_b_g = din("ln_b_g", (2, 256))
    ln_b_b = din("ln_b_b", (2, 256))
    ln_c_g = din("ln_c_g", (2, 256))
    ln_c_b = din("ln_c_b", (2, 256))
    sgu_w = din("sgu_w", (2, 4, 128, 128))
    sgu_b = din("sgu_b", (2, 4, 128))
    w_out = din("w_out", (2, D, D))
    norm2_g = din("norm2_g", (2, D))
    w_up = din("w_up", (2, D, 2 * DFF))
    conv_f_w = din("conv_f_w", (2, 3, 2 * DFF))
    w_down = din("w_down", (2, DFF, D))
    final_g = din("final_g", (D,))
    y_d = nc.dram_tensor("y", [NB, S, D], F32, kind="ExternalOutput").ap()
    wsc = nc.dram_tensor("wsc", [2 * NCH, 128, 2048], BF16, kind="Internal").ap()
    dbg = {}
    if DBG:
        dbg["hT"] = nc.dram_tensor("d_hT", [128, 8, T], BF16, kind="ExternalOutput").ap()
        dbg["mixT"] = nc.dram_tensor("d_mixT", [128, 8, T], BF16, kind="ExternalOutput").ap()
        dbg["x1"] = nc.dram_tensor("d_x1", [128, 4, D], F32, kind="ExternalOutput").ap()
        dbg["actT"] = nc.dram_tensor("d_actT", [128, NFB, T], BF16, kind="ExternalOutput").ap()
        dbg["x2"] = nc.dram_tensor("d_x2", [128, 4, D], F32, kind="ExternalOutput").ap()

    P = Prog()
    es = ExitStack()
    with es:
        def sb(name, shape, dt=F32):
            return es.enter_context(nc.sbuf_tensor(name, list(shape), dt))

        xb = [sb(f"xb{k}", (128, 4, D)) for k in range(2)]
        hn = [sb(f"hn{k}", (128, D), BF16) for k in range(2)]
        hT = sb("hT", (128, 8, T), BF16)
        mixT = sb("mixT", (128, 8, T), BF16)
        actT = sb("actT", (128, NFB, T), BF16)
        kT = [sb(f"kT{l}", (128, 2, S), BF16) for l in range(2)]
        vS = [sb(f"vS{l}", (128, S // 128, 256), BF16) for l in range(2)]
        qTp = sb("qTp", (128, 4, T), BF16)
        wsl = [sb(f"wsl{k}", (128, 2048), BF16) for k in range(NSLOT)]
        st32 = [sb(f"st32_{k}", (128, 2048)) for k in range(2)]
        st16 = [sb(f"st16_{k}", (128, 2048), BF16) for k in range(2)]
        Fb = [sb(f"F{k}", (128, 516)) for k in range(12)]
        Hb = [sb(f"H{k}", (128, T), BF16) for k in range(4)]
        vnp = [sb(f"vnp{k}", (128, 4, 128), BF16) for k in range(4)]
        ssq = sb("ssq", (128, 4))
        rstd = sb("rstd", (128, 4))
        stat = sb("stat", (128, 4, 8))
        epsr = sb("epsr", (128, 1))
        epsl = sb("epsl", (128, 1))
        one1 = sb("one1", (128, 1))
        ones16 = sb("ones16", (128, 128), BF16)
        avg16 = sb("avg16", (128, 128), BF16)
        ident = sb("ident", (128, 128), BF16)
        ntri = sb("ntri", (128, 128), BF16)
        mask01 = sb("mask01", (128, 128), BF16)
        maskge = sb("maskge", (128, 128), BF16)
        sel = sb("sel", (1, 2, 128), BF16)
        gT1 = [sb(f"gT1_{l}", (128, 8)) for l in range(2)]
        gT2 = [sb(f"gT2_{l}", (128, 8)) for l in range(2)]
        caw = [sb(f"caw{l}", (128, 2, 3)) for l in range(2)]
        cbw = [sb(f"cbw{l}", (128, 2, 31)) for l in range(2)]
        cbb = [sb(f"cbb{l}", (128, 2)) for l in range(2)]
        lbg = [sb(f"lbg{l}", (128, 2)) for l in range(2)]
        lbb = [sb(f"lbb{l}", (128, 2)) for l in range(2)]
        lcg = [sb(f"lcg{l}", (128, 256)) for l in range(2)]
        lcb = [sb(f"lcb{l}", (128, 256)) for l in range(2)]
        cfw = [sb(f"cfw{l}", (128, 2 * NFB, 3)) for l in range(2)]
        WsT = [sb(f"WsT{l}", (128, 4, 128), BF16) for l in range(2)]
        sgub = [sb(f"sgub{l}", (1, 4, 128), BF16) for l in range(2)]
        caH = [sb(f"caH{l}", (128, 2, 2)) for l in range(2)]
        gluH = [sb(f"gluH{l}", (128, 2, 30), BF16) for l in range(2)]
        G16 = [sb(f"G16_{k}", (128, 544), BF16) for k in range(2)]
        G16o = [sb(f"G16o_{k}", (128, 544), BF16) for k in range(2)]
        uH = [sb(f"uH{l}", (128, NFB, 2, 2)) for l in range(2)]
        B = [es.enter_context(nc.psum_tensor(f"B{k}", [128, 512], F32)) for k in range(8)]

        esem = {e: es.enter_context(nc.semaphore(f"se_{e}")) for e in ENGS}
        dnames = ([f"w{k}" for k in range(NSLOT)] + ["ld0", "ld1", "so0", "so1", "xl0", "xl1", "ys0", "ys1", "par", "sg0", "sg1", "sg2", "sg3", "sgb", "dbg"])
        dsem = {n: es.enter_context(nc.semaphore(f"sd_{n}")) for n in dnames}

        def ACT(fn, r, w):
            return P.op("act", fn, r, w)

        def DVE(fn, r, w):
            return P.op("dve", fn, r, w)

        def POOL(fn, r, w):
            return P.op("pool", fn, r, w)

        def PE(fn, r, w):
            return P.op("pe", fn, r, w)

        def act_fn(out, in_, func, **kw):
            return lambda e: e.activation(out=out, in_=in_, func=func, **kw)

        def mm(out, lhsT, rhs, start, stop):
            return lambda e: e.matmul(out, lhsT=lhsT, rhs=rhs, start=start, stop=stop)

        POOL(lambda e: e.memset(ones16[:], 1.0), [], ["ones16"])
        POOL(lambda e: e.memset(avg16[:], 1.0 / 256.0), [], ["avg16"])
        POOL(lambda e: e.memset(epsr[:], RMS_EPS), [], ["epsr"])
        POOL(lambda e: e.memset(epsl[:], LN_EPS), [], ["epsl"])
        POOL(lambda e: e.memset(one1[:], 1.0), [], ["one1"])
        POOL(lambda e: e.affine_select(out=ident[:], in_=ones16[:], pattern=[[-1, 128]], compare_op=ALU.is_equal,
                                       fill=0.0, base=0, channel_multiplier=1), ["ones16"], ["ident"])
        POOL(lambda e: e.memset(ntri[:], -1.0), [], ["ntri"])
        POOL(lambda e: e.affine_select(out=ntri[:], in_=ntri[:], pattern=[[-1, 128]], compare_op=ALU.is_ge,
                                       fill=0.0, base=0, channel_multiplier=1), ["ntri"], ["ntri"])
        POOL(lambda e: e.affine_select(out=mask01[:], in_=ones16[:], pattern=[[1, 128]], compare_op=ALU.is_gt,
                                       fill=0.0, base=0, channel_multiplier=-1), ["ones16"], ["mask01"])
        POOL(lambda e: e.affine_select(out=maskge[:], in_=ones16[:], pattern=[[1, 128]], compare_op=ALU.is_ge,
                                       fill=0.0, base=0, channel_multiplier=-1), ["ones16"], ["maskge"])
        POOL(lambda e: e.memset(sel[:], 0.0), [], ["sel"])
        POOL(lambda e: e.memset(sel[:, 0, 0:64], 1.0), ["sel"], ["sel"])
        POOL(lambda e: e.memset(sel[:, 1, 64:128], 1.0), ["sel"], ["sel"])
        POOL(lambda e: e.memset(qTp[:], 0.0), [], ["qTp0", "qTp1", "qTp2", "qTp3"])
        for j in range(4):
            POOL(lambda e, j=j: e.memset(vnp[j][:], 0.0), [], [f"vnp{j}"])

        def pload(out, in_, w):
            P.op("act", lambda e: e.dma_start(out=out, in_=in_), [], w, sem=None)

        par_ops = []

        def par(out, in_):
            par_ops.append((out, in_))

        nc_ctx = nc.allow_non_contiguous_dma(reason="small one-time parameter loads")
        es.enter_context(nc_ctx)
        for l in range(2):
            par(gT1[l][:], norm1_g[l].rearrange("(c p) -> p c", p=128))
            par(gT2[l][:], norm2_g[l].rearrange("(c p) -> p c", p=128))
            for k in range(3):
                par(caw[l][:, :, k], conv_a_w[l, k].rearrange("(c p) -> p c", p=128))
            for k in range(31):
                par(cbw[l][:, :, k], conv_b_w[l, k].rearrange("(c p) -> p c", p=128))
            par(cbb[l][:], conv_b_b[l].rearrange("(c p) -> p c", p=128))
            par(lbg[l][:], ln_b_g[l].rearrange("(c p) -> p c", p=128))
            par(lbb[l][:], ln_b_b[l].rearrange("(c p) -> p c", p=128))
            par(lcg[l][:], ln_c_g[l].partition_broadcast(128))
            par(lcb[l][:], ln_c_b[l].partition_broadcast(128))
            for k in range(3):
                for q in range(4):
                    par(cfw[l][:, q * 11:(q + 1) * 11, k],
                        conv_f_w[l, k, q * 11 * 128:(q + 1) * 11 * 128].rearrange("(c p) -> p c", p=128))
        npar = len(par_ops)
        par_recs = []
        for (o_, i_) in par_ops:
            par_recs.append(P.op("act", lambda e, o_=o_, i_=i_: e.dma_start(out=o_, in_=i_), [], [], sem="par"))
        last = par_recs[-1]
        for k in ["gT1", "gT2", "caw", "cbw", "cbb", "lbg", "lbb", "lcg", "lcb", "cfw"]:
            P.lastw[k] = last
            P.readers[k] = []

        def DMA(q, out, in_, r, w, sem):
            return P.op(q, lambda e: e.dma_start(out=out, in_=in_), r, w, sem=sem)

        for l in range(2):
            for g in range(4):
                DMA("pool", Fb[g][:, 0:128], sgu_w[l, g], [], [f"F{g}"], f"sg{g}")
            for g in range(4):
                ACT(act_fn(Hb[g][:, 0:128], Fb[g][:, 0:128], AF.Copy), [f"F{g}"], [f"H{g}"])
                PE(lambda e, g=g: e.transpose(B[g][:].bitcast(BF16)[:, 0:128], Hb[g][:, 0:128], ident[:]),
                   [f"H{g}", "ident"], [f"B{g}"])
                DVE(lambda e, l=l, g=g: e.tensor_tensor(out=WsT[l][:, g, :], in0=B[g][:].bitcast(BF16)[:, 0:128],
                                                        in1=maskge[:], op=ALU.mult),
                    [f"B{g}", "maskge"], [f"WsT{l}"])
            DMA("pool", sgub[l][:], sgu_b[l:l + 1], [], [f"sgub{l}"], "sgb")

        def chunk_src(l, k):
            if k < 10:
                src = w_in[l][:, k * 256:(k + 1) * 256].rearrange("(dc p) e -> p dc e", p=128)
                return [(lambda t: t[:].rearrange("p (dc e) -> p dc e", dc=8), src)], gT1[l]
            if k < 14:
                c = k - 10
                src = w_out[l][c * 256:(c + 1) * 256, :].rearrange("(b p) d -> p b d", p=128)
                return [(lambda t: t[:].rearrange("p (b d) -> p b d", b=2), src)], None
            if k < 36:
                fb = k - 14
                s1 = w_up[l][:, fb * 128:(fb + 1) * 128].rearrange("(dc p) e -> p dc e", p=128)
                s2 = w_up[l][:, DFF + fb * 128:DFF + (fb + 1) * 128].rearrange("(dc p) e -> p dc e", p=128)
                return [(lambda t: t[:].rearrange("p (dc e) -> p dc e", dc=8)[:, :, 0:128], s1),
                        (lambda t: t[:].rearrange("p (dc e) -> p dc e", dc=8)[:, :, 128:256], s2)], gT2[l]
            if k >= 47:
                return [], "diag"
            c = k - 36
            src = w_down[l][c * 256:(c + 1) * 256, :].rearrange("(b p) d -> p b d", p=128)
            return [(lambda t: t[:].rearrange("p (b d) -> p b d", b=2), src)], None

        ORDER = [7, 8, 9, 5, 6, 3, 4, 0, 1, 2, 47, 48, 49, 50] + list(range(10, 47))
        PRE = 8

        def prepass_load(g):
            l, k = g // NCH, ORDER[g % NCH]
            s = g % 2
            parts, gain = chunk_src(l, k)
            for pi, (dv, src) in enumerate(parts):
                wn_ = [(f"st32_{s}", pi)] if len(parts) == 2 else [(f"st32_{s}", 0), (f"st32_{s}", 1)]
                DMA("pool", dv(st32[s]), src, [], wn_, f"ld{s}")

        def prepass(g, do_load=True):
            l, k = g // NCH, ORDER[g % NCH]
            s = g % 2
            parts, gain = chunk_src(l, k)
            if do_load:
                prepass_load(g)
            if gain == "diag":
                c = k - 47
                for mm_ in range(16):
                    m = c * 16 + mm_
                    if m >= 62:
                        break
                    cb_, kk = divmod(m, 31)
                    DVE(lambda e, s=s, mm_=mm_, cb_=cb_, kk=kk: e.tensor_scalar(
                        out=st16[s][:, mm_ * 128:(mm_ + 1) * 128], in0=ident[:], scalar1=cbw[l][:, cb_, kk:kk + 1],
                        scalar2=None, op0=ALU.mult), ["ident", "cbw"], [(f"st16_{s}", mm_ // 2)])
                DMA("pool", wsc[l * NCH + k], st16[s][:], [(f"st16_{s}", dc) for dc in range(8)], [("wsc", l, k)], f"so{s}")
                return
            rd = [(f"st32_{s}", 0), (f"st32_{s}", 1)]
            on_act = (g % 2 == 1)
            CENG = DVE if CAST_MIXED else POOL
            if not CAST_MIXED:
                on_act = False
            allw = [(f"st16_{s}", dc) for dc in range(8)]
            if gain is None:
                if on_act:
                    ACT(act_fn(st16[s][:], st32[s][:], AF.Copy), rd, allw)
                else:
                    CENG(lambda e, s=s: e.tensor_copy(out=st16[s][:], in_=st32[s][:]), rd, allw)
            else:
                gname = "gT1" if k < 10 else "gT2"
                for dc in range(8):
                    o_ = st16[s][:, dc * 256:(dc + 1) * 256]
                    i_ = st32[s][:, dc * 256:(dc + 1) * 256]
                    if on_act:
                        ACT(act_fn(o_, i_, AF.Copy, scale=gain[:, dc:dc + 1]), rd + [gname], [(f"st16_{s}", dc)])
                    else:
                        CENG(lambda e, o_=o_, i_=i_, gain=gain, dc=dc: e.tensor_scalar(
                            out=o_, in0=i_, scalar1=gain[:, dc:dc + 1], scalar2=None, op0=ALU.mult),
                            rd + [gname], [(f"st16_{s}", dc)])
            DMA("pool", wsc[l * NCH + k], st16[s][:], allw, [("wsc", l, k)], f"so{s}")

        if PRE_INTERLEAVE:
            for g in range(PRE):
                prepass(g)
        else:
            prepass_load(0)
            prepass_load(1)
            for g in range(2 * NCH):
                prepass(g, do_load=False)
                if g + 2 < 2 * NCH:
                    prepass_load(g + 2)

        aE = st32[0][:, 0:512]
        aL = [st32[0][:, 512:1024], st32[0][:, 1024:1536]]
        aSP = [st16[0][:, 0:512], st16[0][:, 512:1024]]
        aA = [st16[0][:, 1024:1536], st16[0][:, 1536:2048]]
        uS = [st16[1][:, 0:512], st16[1][:, 512:1024]]
        carry = [st32[1][:, 0:512], st32[1][:, 512:1024]]
        fence32b = P.lastw.get(("st16_1", 0))
        assert fence32b is not None
        for nm_ in ["carry0", "carry1"]:
            P.lastw[nm_] = fence32b
        fgb = st32[1][:, 1024:2048]
        P.lastw["fgb"] = fence32b
        DMA("sp", fgb, final_g.partition_broadcast(128), [], ["fgb"], "sgb")
        fence32 = P.lastw.get(("st16_0", 0))
        fence16 = P.lastw.get(("wsc", 1, ORDER[(2 * NCH - 2) % NCH]))
        fence16b = P.lastw.get(("wsc", 1, ORDER[(2 * NCH - 1) % NCH]))
        assert fence32 is not None and fence16 is not None and fence16b is not None
        for nm_ in ["aE", "aL0", "aL1"]:
            P.lastw[nm_] = fence32
        for nm_ in ["aSP0", "aSP1", "aA0", "aA1"]:
            P.lastw[nm_] = fence16
        for nm_ in ["uS0", "uS1"]:
            P.lastw[nm_] = fence16b

        wstate = {"n": 0}

        def wfetch(l, k):
            sl = wstate["n"] % NSLOT
            wstate["n"] += 1
            DMA("sp", wsl[sl][:], wsc[l * NCH + k], [("wsc", l, k)], [f"wsl{sl}"], f"w{sl}")
            return sl

        stream = []
        fetched = {"i": 0}
        slot_of = {}

        def prefetch_upto(pos):
            while fetched["i"] <= pos and fetched["i"] < len(stream):
                l_, k_ = stream[fetched["i"]]
                slot_of[fetched["i"]] = wfetch(l_, k_)
                fetched["i"] += 1

        upos = {"i": 0}

        def next_chunk(prefetch=True):
            pos = upos["i"]
            upos["i"] += 1
            if PRE_INTERLEAVE and pos + PRE < 2 * NCH:
                prepass(pos + PRE)
            if prefetch:
                prefetch_upto(pos + NSLOT - 1)
            sl = slot_of[pos]
            return wsl[sl], f"wsl{sl}"

        def catch_up():
            prefetch_upto(upos["i"] - 1 + NSLOT - 1)

        for b in range(NB):
            for i in range(NT):
                for l in range(2):
                    for k in ORDER:
                        stream.append((l, k))

        def rms_stats(xt, xname):
            DVE(lambda e: e.memset(ssq[:], 0.0), [], [("ssq", j) for j in range(4)])

            def lnexp(j):
                ACT(act_fn(rstd[:, j:j + 1], ssq[:, j:j + 1], AF.Ln, bias=epsr[:, 0:1], scale=1.0 / D),
                    [("ssq", j), "epsr"], [("rstd", j)])
                ACT(act_fn(rstd[:, j:j + 1], rstd[:, j:j + 1], AF.Exp, scale=-0.5), [("rstd", j)], [("rstd", j)])

            for j in range(4):
                ACT(act_fn(Fb[11][:].bitcast(BF16)[:, 0:D], xt[:, j, :], AF.Square, accum_out=ssq[:, j:j + 1]),
                    [(xname, j)], [("ssq", j)])
                if j >= 1:
                    lnexp(j - 1)
            lnexp(3)

        def norm_T(xt, xname):
            rms_stats(xt, xname)
            for j in range(4):
                hb = hn[j % 2]
                hname = f"hn{j % 2}"
                DVE(lambda e, j=j, hb=hb: e.tensor_scalar(out=hb[:], in0=xt[:, j, :], scalar1=rstd[:, j:j + 1],
                                                          scalar2=None, op0=ALU.mult), [(xname, j), ("rstd", j)], [hname])
                bk = j
                for c in range(8):
                    PE(lambda e, c=c, hb=hb, bk=bk: e.transpose(B[bk][:].bitcast(BF16)[:, c * 128:(c + 1) * 128],
                                                                hb[:, c * 128:(c + 1) * 128], ident[:]),
                       [hname, "ident"], [f"B{bk}"])
                ev = ACT if j % 2 == 0 else DVE
                if j % 2 == 0:
                    ACT(act_fn(hT[:, :, j * 128:(j + 1) * 128],
                               B[bk][:].bitcast(BF16).rearrange("p (c t) -> p c t", c=8), AF.Copy),
                        [f"B{bk}"], [("hT", j)])
                else:
                    DVE(lambda e, j=j, bk=bk: e.tensor_copy(out=hT[:, :, j * 128:(j + 1) * 128],
                                                            in_=B[bk][:].bitcast(BF16).rearrange("p (c t) -> p c t", c=8)),
                        [f"B{bk}"], [("hT", j)])

        def fm_block(wt, wname, col0, bank):
            wv = wt[:].rearrange("p (dc e) -> p dc e", dc=8)
            for dc in range(8):
                PE(mm(B[bank][:], wv[:, dc, col0:col0 + 128], hT[:, dc, :], dc == 0, dc == 7),
                   [wname] + [("hT", j_) for j_ in range(4)], [f"B{bank}"])

        def tm_block(wt, wname, j, bank, half):
            wv = wt[:].rearrange("p (dc e) -> p dc e", dc=8)
            for dc in range(8):
                PE(mm(B[bank][:, half * 256:(half + 1) * 256], hT[:, dc, j * 128:(j + 1) * 128], wv[:, dc, :],
                      dc == 0, dc == 7), [wname, ("hT", j)], [f"B{bank}"])

        def dwconv(eng, buf, bname, acc, aname, wt, wname, widx, K, first_bias=None):
            wsel = lambda k: wt[:, widx, k:k + 1]
            if first_bias is None:
                eng(lambda e: e.tensor_scalar(out=acc[:, 0:T], in0=buf[:, K - 1:K - 1 + T], scalar1=wsel(K - 1),
                                              scalar2=None, op0=ALU.mult), [bname, wname], [aname])
            else:
                eng(lambda e: e.tensor_scalar(out=acc[:, 0:T], in0=buf[:, K - 1:K - 1 + T], scalar1=wsel(K - 1),
                                              scalar2=first_bias, op0=ALU.mult, op1=ALU.add),
                    [bname, wname, "cbb"], [aname])
            for k in range(K - 2, -1, -1):
                eng(lambda e, k=k: e.scalar_tensor_tensor(out=acc[:, 0:T], in0=buf[:, k:k + T], scalar=wsel(k),
                                                          in1=acc[:, 0:T], op0=ALU.mult, op1=ALU.add),
                    [bname, wname, aname], [aname])

        def halo_in(eng, buf, bname, hal, hname, H, first):
            if first:
                eng(lambda e: e.memset(buf[:, 0:H], 0.0), [], [bname])
            else:
                eng(lambda e: e.tensor_copy(out=buf[:, 0:H], in_=hal), [hname], [bname])

        def halo_out(eng, buf, bname, hal, hname, H):
            eng(lambda e: e.tensor_copy(out=hal, in_=buf[:, T:T + H]), [bname], [hname])

        def proj_residual(xt, xname, nchunks, lhs_of, lres_of):
            nkb = 2 * nchunks
            for c in range(nchunks - 2):
                wt, wn = next_chunk()
                wv = wt[:].rearrange("p (b d) -> p b d", b=2)
                for kbl in range(2):
                    kb = 2 * c + kbl
                    for j in range(4):
                        for dh in range(2):
                            bk = j * 2 + dh
                            PE(mm(B[bk][:], lhs_of(kb, j), wv[:, kbl, dh * 512:(dh + 1) * 512], kb == 0, False),
                               [lres_of(kb), wn], [f"B{bk}"])
            wA = next_chunk()
            wB = next_chunk(prefetch=False)
            for j in range(4):
                for dh in range(2):
                    bk = j * 2 + dh
                    for ci, (wt, wn) in enumerate((wA, wB)):
                        wv = wt[:].rearrange("p (b d) -> p b d", b=2)
                        for kbl in range(2):
                            kb = 2 * (nchunks - 2 + ci) + kbl
                            PE(mm(B[bk][:], lhs_of(kb, j), wv[:, kbl, dh * 512:(dh + 1) * 512], kb == 0, kb == nkb - 1),
                               [lres_of(kb), wn], [f"B{bk}"])
                    DVE(lambda e, j=j, dh=dh, bk=bk: e.tensor_tensor(out=xt[:, j, dh * 512:(dh + 1) * 512],
                                                                     in0=xt[:, j, dh * 512:(dh + 1) * 512],
                                                                     in1=B[bk][:], op=ALU.add),
                        [(xname, j), f"B{bk}"], [(xname, j)])
            catch_up()

        def layer(xt, xname, l, i):
            first = (i == 0)
            norm_T(xt, xname)
            nb = {"n": 0}

            def nbank():
                b_ = 2 + nb["n"] % 6
                nb["n"] += 1
                return b_

            wt, wn = next_chunk()
            for hp in range(2):
                bk = nbank()
                fm_block(wt, wn, hp * 128, bk)
                for gg in range(2):
                    h = 2 * hp + gg
                    DVE(lambda e, gg=gg, h=h, bk=bk: e.tensor_scalar(
                        out=qTp[gg * 64:(gg + 1) * 64, h, :], in0=B[bk][gg * 64:(gg + 1) * 64, :], scalar1=0.125,
                        scalar2=None, op0=ALU.mult), [f"B{bk}"], [f"qTp{h}"])
            wt, wn = next_chunk()
            for hp in range(2):
                bk = nbank()
                fm_block(wt, wn, hp * 128, bk)
                DVE(lambda e, hp=hp, bk=bk: e.tensor_copy(out=kT[l][:, hp, i * T:(i + 1) * T], in_=B[bk][:]),
                    [f"B{bk}"], [f"kT{l}"])
            wt, wn = next_chunk()
            for j in range(4):
                bk, half = nbank(), 0
                tm_block(wt, wn, j, bk, half)
                DVE(lambda e, j=j, bk=bk, half=half: e.tensor_copy(out=vS[l][:, i * 4 + j, :],
                                                                   in_=B[bk][:, half * 256:(half + 1) * 256]),
                    [f"B{bk}"], [f"vS{l}"])
            wt, wn = next_chunk()
            for hp in range(2):
                bk = nbank()
                fm_block(wt, wn, hp * 128, bk)
                ACT(act_fn(uS[hp], B[bk][:], AF.Gelu), [f"B{bk}"], [f"uS{hp}"])
            wt, wn = next_chunk()
            for j in range(4):
                bk, half = nbank(), 0
                tm_block(wt, wn, j, bk, half)
                vg = Fb[j]
                vn_ = f"F{j}"
                ACT(act_fn(vg[:, 0:256], B[bk][:, half * 256:(half + 1) * 256], AF.Gelu), [f"B{bk}"], [vn_])
            for j in range(4):
                vg = Fb[j]
                vn_ = f"F{j}"
                DVE(lambda e, j=j, vg=vg: e.bn_stats(out=stat[:, j, 0:6], in_=vg[:, 0:256]), [vn_], [("stat", j)])
                DVE(lambda e, j=j: e.bn_aggr(out=stat[:, j, 6:8], in_=stat[:, j, 0:6]), [("stat", j)], [("stat", j)])
            for j in range(4):
                ACT(act_fn(stat[:, j, 7:8], stat[:, j, 7:8], AF.Ln, bias=epsl[:, 0:1]), [("stat", j), "epsl"], [("stat", j)])
            for j in range(4):
                ACT(act_fn(stat[:, j, 7:8], stat[:, j, 7:8], AF.Exp, scale=-0.5), [("stat", j)], [("stat", j)])

            def gen_mix():
                for j in range(4):
                    vg = Fb[j]
                    vn_ = f"F{j}"
                    DVE(lambda e, j=j, vg=vg: e.tensor_scalar(out=vg[:, 0:256], in0=vg[:, 0:256], scalar1=stat[:, j, 6:7],
                                                              scalar2=stat[:, j, 7:8], op0=ALU.subtract, op1=ALU.mult),
                        [vn_, ("stat", j)], [vn_])
                    POOL(lambda e, vg=vg: e.tensor_tensor(out=vg[:, 0:256], in0=vg[:, 0:256], in1=lcg[l][:], op=ALU.mult),
                         [vn_, "lcg"], [vn_])
                    for gg in range(2):
                        POOL(lambda e, j=j, vg=vg, gg=gg: e.tensor_tensor(
                            out=vnp[j][:, gg:4:2, gg * 64:(gg + 1) * 64],
                            in0=vg[:, 0:256].rearrange("p (g c) -> p g c", g=4)[:, gg:4:2, :],
                            in1=lcb[l][:].rearrange("p (g c) -> p g c", g=4)[:, gg:4:2, :], op=ALU.add),
                            [vn_, "lcb"], [f"vnp{j}"])
                    yield
                for hp in range(2):
                    for j in range(4):
                        for gg in range(2):
                            g = 2 * hp + gg
                            PE(mm(B[hp][:, j * 128:(j + 1) * 128], vnp[j][:, g, :], WsT[l][:, g, :], gg == 0, False),
                               [f"vnp{j}", f"WsT{l}"], [f"B{hp}"])
                        for gg in range(2):
                            g = 2 * hp + gg
                            PE(mm(B[hp][:, j * 128:(j + 1) * 128], sel[:, gg, :], sgub[l][:, g, :], False, gg == 1),
                               ["sel", f"sgub{l}"], [f"B{hp}"])
                    DVE(lambda e, hp=hp: e.tensor_tensor(out=mixT[:, 4 + hp, :], in0=uS[hp], in1=B[hp][:], op=ALU.mult),
                        [f"uS{hp}", f"B{hp}"], [("mixT", 4 + hp)])
                    yield
                mb = {"n": 0}

                def mbank():
                    b_ = mb["n"] % 2
                    mb["n"] += 1
                    return b_

                wt, wn = next_chunk()
                for cb in range(2):
                    bk = mbank()
                    fm_block(wt, wn, cb * 128, bk)
                    DVE(lambda e, cb=cb, bk=bk: e.tensor_copy(out=Fb[cb][:, 0:T], in_=B[bk][:]), [f"B{bk}"], [f"F{cb}"])
                    yield
                wt, wn = next_chunk()
                for cb in range(2):
                    bk = mbank()
                    fm_block(wt, wn, cb * 128, bk)
                    sg = Fb[2 + cb]
                    sn = f"F{2 + cb}"
                    ACT(act_fn(sg[:, 0:T], B[bk][:], AF.Exp, scale=-1.0), [f"B{bk}"], [sn])
                    ACT(act_fn(sg[:, 0:T], sg[:, 0:T], AF.Ln, bias=one1[:, 0:1]), [sn, "one1"], [sn])
                    ACT(act_fn(sg[:, 0:T], sg[:, 0:T], AF.Exp, scale=-1.0), [sn], [sn])
                    gl = G16[cb]
                    gn = f"G16_{cb}"
                    halo_in(POOL, gl, gn, gluH[l][:, cb, :], f"gluH{l}", 30, first)
                    POOL(lambda e, cb=cb, gl=gl: e.tensor_tensor(out=gl[:, 30:30 + T], in0=Fb[cb][:, 0:T],
                                                                 in1=Fb[2 + cb][:, 0:T], op=ALU.mult),
                         [f"F{cb}", sn], [gn])
                    halo_out(POOL, gl, gn, gluH[l][:, cb, :], f"gluH{l}", 30)
                    POOL(lambda e, cb=cb, gl=gl: e.tensor_copy(out=G16o[cb][:, 0:541], in_=gl[:, 1:542]),
                         [gn], [f"G16o_{cb}"])
                    yield
                wt, wn = next_chunk()
                for cb in range(2):
                    bk = mbank()
                    fm_block(wt, wn, cb * 128, bk)
                    DVE(lambda e, cb=cb, bk=bk: e.tensor_copy(out=Fb[cb][:, 0:T], in_=B[bk][:]), [f"B{bk}"], [f"F{cb}"])
                    yield
                wt, wn = next_chunk()
                for cb in range(2):
                    bk = mbank()
                    fm_block(wt, wn, cb * 128, bk)
                    DVE(lambda e, cb=cb, bk=bk: e.tensor_copy(out=Fb[2 + cb][:, 0:T], in_=B[bk][:]), [f"B{bk}"], [f"F{2 + cb}"])
                    yield
                wt, wn = next_chunk()
                for cb in range(2):
                    bk = mbank()
                    fm_block(wt, wn, cb * 128, bk)
                    ca = Fb[4 + cb]
                    cn = f"F{4 + cb}"
                    halo_in(POOL, ca, cn, caH[l][:, cb, :], f"caH{l}", 2, first)
                    DVE(lambda e, cb=cb, bk=bk, ca=ca: e.tensor_tensor(out=ca[:, 2:2 + T], in0=Fb[2 + cb][:, 0:T],
                                                                        in1=B[bk][:], op=ALU.mult),
                        [f"F{2 + cb}", f"B{bk}"], [cn])
                    halo_out(POOL, ca, cn, caH[l][:, cb, :], f"caH{l}", 2)
                    yield
                    dwconv(DVE, ca, cn, Fb[6 + cb], f"F{6 + cb}", caw[l], "caw", cb, 3)
                    POOL(lambda e, cb=cb: e.tensor_tensor(out=mixT[:, cb, :], in0=Fb[cb][:, 0:T], in1=Fb[6 + cb][:, 0:T],
                                                          op=ALU.mult), [f"F{cb}", f"F{6 + cb}"], [("mixT", cb)])
                    yield
                dwt = None
                for m in range(62):
                    cb, k = divmod(m, 31)
                    if m % 16 == 0:
                        dwt = next_chunk()
                    gl = G16[cb]
                    gn = f"G16_{cb}"
                    rhs_ = gl[:, k:k + T] if k % 2 == 0 else G16o[cb][:, k - 1:k - 1 + T]
                    PE(mm(B[cb][:], dwt[0][:, (m % 16) * 128:(m % 16 + 1) * 128], rhs_, k == 0, k == 30),
                       [gn, f"G16o_{cb}", dwt[1]], [f"B{cb}"])
                    if k % 8 == 7:
                        yield
                    if k == 30:
                        acc, an = Fb[6 + cb], f"F{6 + cb}"
                        ACT(act_fn(Hb[2 + cb][:], B[cb][:], AF.Square, bias=cbb[l][:, cb:cb + 1]), [f"B{cb}", "cbb"],
                            [f"H{2 + cb}", f"B{cb}"])
                        DVE(lambda e, cb=cb, acc=acc: e.tensor_scalar(out=acc[:, 0:T], in0=B[cb][:], scalar1=cbb[l][:, cb:cb + 1],
                                                                      scalar2=None, op0=ALU.add), [f"B{cb}", "cbb"], [an, f"B{cb}"])
                        DVE(lambda e, cb=cb, acc=acc: e.tensor_copy(out=Hb[cb][:], in_=acc[:, 0:T]), [an], [f"H{cb}"])
                        yield
                for cb in range(2):
                    PE(mm(B[0][:], avg16[:], Hb[cb][:], cb == 0, cb == 1), [f"H{cb}", "avg16"], ["B0"])
                for cb in range(2):
                    PE(mm(B[1][:], avg16[:], Hb[2 + cb][:], cb == 0, cb == 1), [f"H{2 + cb}", "avg16"], ["B1"])
                ACT(act_fn(Fb[0][:, 0:T], B[0][:], AF.Square), ["B0"], ["F0", "B0"])
                DVE(lambda e: e.tensor_tensor(out=Fb[1][:, 0:T], in0=B[1][:], in1=Fb[0][:, 0:T], op=ALU.subtract),
                    ["B1", "F0"], ["F1"])
                ACT(act_fn(Fb[1][:, 0:T], Fb[1][:, 0:T], AF.Ln, bias=epsl[:, 0:1]), ["F1", "epsl"], ["F1"])
                ACT(act_fn(Fb[1][:, 0:T], Fb[1][:, 0:T], AF.Exp, scale=-0.5), ["F1"], ["F1"])
                yield
                for cb in range(2):
                    tt, tn = Fb[2 + cb], f"F{2 + cb}"
                    sg, sn = Fb[4 + cb], f"F{4 + cb}"
                    DVE(lambda e, cb=cb, tt=tt: e.tensor_tensor(out=tt[:, 0:T], in0=Fb[6 + cb][:, 0:T], in1=B[0][:],
                                                                op=ALU.subtract), [f"F{6 + cb}", "B0"], [tn])
                    POOL(lambda e, tt=tt: e.tensor_tensor(out=tt[:, 0:T], in0=tt[:, 0:T], in1=Fb[1][:, 0:T], op=ALU.mult),
                         [tn, "F1"], [tn])
                    ACT(act_fn(tt[:, 0:T], tt[:, 0:T], AF.Identity, scale=lbg[l][:, cb:cb + 1], bias=lbb[l][:, cb:cb + 1]),
                        [tn, "lbg", "lbb"], [tn])
                    ACT(act_fn(sg[:, 0:T], tt[:, 0:T], AF.Exp, scale=-1.0), [tn], [sn])
                    ACT(act_fn(sg[:, 0:T], sg[:, 0:T], AF.Ln, bias=one1[:, 0:1]), [sn, "one1"], [sn])
                    ACT(act_fn(sg[:, 0:T], sg[:, 0:T], AF.Exp, scale=-1.0), [sn], [sn])
                    POOL(lambda e, cb=cb, tt=tt, sg=sg: e.tensor_tensor(out=mixT[:, 2 + cb, :], in0=tt[:, 0:T],
                                                                        in1=sg[:, 0:T], op=ALU.mult),
                         [tn, sn], [("mixT", 2 + cb)])
                    yield

            gen = gen_mix()
            NYIELD = 36

            def advance(k):
                for _ in range(k):
                    try:
                        next(gen)
                    except StopIteration:
                        return False
                return True

            steps = []
            for h in range(4):
                for kb in range(4 * i + 3, -1, -1):
                    steps.append((h, kb))
            nst = len(steps)

            def geom(h, kb):
                jj = kb - 4 * i
                c0 = max(jj, 0) * 128
                return c0, T - c0, jj >= 0

            def S1(n):
                h, kb = steps[n]
                c0, N, diag = geom(h, kb)
                hp = h // 2
                zb = 2 + n % 2
                E, En = aE, "aE"
                SPt, SPn = aSP[n % 2], f"aSP{n % 2}"
                PE(mm(B[zb][:, 0:N], kT[l][:, hp, kb * 128:(kb + 1) * 128], qTp[:, h, c0:T], True, True),
                   [f"kT{l}", f"qTp{h}"], [f"B{zb}"])
                ACT(act_fn(E[:, 0:N], B[zb][:, 0:N], AF.Exp), [f"B{zb}"], [En])
                ACT(act_fn(SPt[:, 0:N], E[:, 0:N], AF.Ln, bias=one1[:, 0:1]), [En, "one1"], [SPn])
                if diag:
                    POOL(lambda e: e.tensor_tensor(out=SPt[:, 0:128], in0=SPt[:, 0:128], in1=mask01[:], op=ALU.mult),
                         [SPn, "mask01"], [SPn])

            def S2(n):
                h, kb = steps[n]
                c0, N, diag = geom(h, kb)
                hp = h // 2
                cbk = 4 + n % 2
                rbk = 6
                SPt, SPn = aSP[n % 2], f"aSP{n % 2}"
                L, Ln_ = aL[n % 2], f"aL{n % 2}"
                A, An = aA[n % 2], f"aA{n % 2}"
                cr, crn = carry[h % 2], f"carry{h % 2}"
                PE(mm(B[cbk][:, 0:N], ntri[:], SPt[:, 0:N], True, False), [SPn, "ntri"], [f"B{cbk}"])
                PE(mm(B[cbk][:, 0:N], kT[l][:, hp, kb * 128:(kb + 1) * 128], qTp[:, h, c0:T], False, True),
                   [f"kT{l}", f"qTp{h}"], [f"B{cbk}"])
                if kb > 0:
                    PE(mm(B[rbk][:, 0:N], ones16[:], SPt[:, 0:N], True, True), [SPn, "ones16"], [f"B{rbk}"])
                topk = (kb == 4 * i + 3)
                if topk:
                    DVE(lambda e: e.memset(cr, 0.0), [], [crn])
                DVE(lambda e: e.tensor_tensor(out=L[:, 0:N], in0=B[cbk][:, 0:N], in1=cr[:, c0:T], op=ALU.subtract),
                    [f"B{cbk}", crn], [Ln_])
                if kb > 0:
                    DVE(lambda e: e.tensor_tensor(out=cr[:, c0:T], in0=B[rbk][:, 0:N], in1=cr[:, c0:T], op=ALU.add),
                        [f"B{rbk}", crn], [crn])
                ACT(act_fn(A[:, 0:N], L[:, 0:N], AF.Exp), [Ln_], [An])
                if diag:
                    POOL(lambda e: e.tensor_tensor(out=A[:, 0:128], in0=A[:, 0:128], in1=mask01[:], op=ALU.mult),
                         [An, "mask01"], [An])

            def S3(n):
                h, kb = steps[n]
                c0, N, diag = geom(h, kb)
                hp, gg = h // 2, h % 2
                A, An = aA[n % 2], f"aA{n % 2}"
                PE(mm(B[7][gg * 64:(gg + 1) * 64, c0:T], vS[l][:, kb, h * 64:(h + 1) * 64], A[:, 0:N],
                      kb == 4 * i + 3, kb == 0), [An, f"vS{l}"], ["B7"])
                if kb == 0 and gg == 1:
                    ACT(act_fn(mixT[:, 6 + hp, :], B[7][:], AF.Copy), ["B7"], [("mixT", 6 + hp)])

            per = -(-NYIELD // nst)
            advance(4)
            for n in range(nst + 2):
                if n < nst:
                    S1(n)
                if 0 <= n - 1 < nst:
                    S2(n - 1)
                if 0 <= n - 2 < nst:
                    S3(n - 2)
                advance(per)
            while advance(1):
                pass

            if DBG and l == 0 and i == 0 and not dbg.get("done1"):
                DMA("pool", dbg["hT"], hT[:], [("hT", j_) for j_ in range(4)], [], "dbg")
                DMA("pool", dbg["mixT"], mixT[:], [("mixT", e_) for e_ in range(8)], [], "dbg")
            proj_residual(xt, xname, 4, lambda kb, j: mixT[:, kb, j * 128:(j + 1) * 128], lambda kb: ("mixT", kb))
            if DBG and l == 0 and i == 0 and not dbg.get("done1"):
                DMA("pool", dbg["x1"], xt[:], [(xname, j_) for j_ in range(4)], [], "dbg")
            norm_T(xt, xname)
            ffn_pend = []

            def ffn_finish(fb, par_, tb, tbn):
                ACT(act_fn(Hb[par_][:], tb[0][:, 0:T], AF.Silu), [tbn[0]], [f"H{par_}"])
                POOL(lambda e: e.tensor_tensor(out=actT[:, fb, :], in0=Hb[par_][:], in1=tb[1][:, 0:T], op=ALU.mult),
                     [f"H{par_}", tbn[1]], [("actT", fb)])

            def ffn_bufs(fb):
                par_ = fb % 3
                return (par_, [Fb[4 * par_], Fb[4 * par_ + 1]], [f"F{4 * par_}", f"F{4 * par_ + 1}"],
                        [Fb[4 * par_ + 2], Fb[4 * par_ + 3]], [f"F{4 * par_ + 2}", f"F{4 * par_ + 3}"])

            def ffn_halo_in(fb):
                par_, ub, ubn, tb, tbn = ffn_bufs(fb)
                for gv in range(2):
                    halo_in(POOL, ub[gv], ubn[gv], uH[l][:, fb, gv, :], f"uH{l}", 2, first)

            ffn_halo_in(0)
            for fb in range(NFB):
                P.curtag = f"ffn{fb}"
                wt, wn = next_chunk()
                par_, ub, ubn, tb, tbn = ffn_bufs(fb)
                for gv in range(2):
                    bk = 2 + 2 * par_ + gv
                    fm_block(wt, wn, gv * 128, bk)
                    widx = gv * NFB + fb
                    ACT(act_fn(ub[gv][:, 2:2 + T], B[bk][:], AF.Copy), [f"B{bk}"], [ubn[gv]])
                    ACT(act_fn(tb[gv][:, 0:T], B[bk][:], AF.Copy, scale=cfw[l][:, widx, 2:3]), [f"B{bk}", "cfw"], [tbn[gv]])
                    halo_out(POOL, ub[gv], ubn[gv], uH[l][:, fb, gv, :], f"uH{l}", 2)
                    for k in (1, 0):
                        DVE(lambda e, k=k, gv=gv, widx=widx, tb=tb, ub=ub: e.scalar_tensor_tensor(
                            out=tb[gv][:, 0:T], in0=ub[gv][:, k:k + T], scalar=cfw[l][:, widx, k:k + 1],
                            in1=tb[gv][:, 0:T], op0=ALU.mult, op1=ALU.add), [ubn[gv], tbn[gv], "cfw"], [tbn[gv]])
                if fb + 1 < NFB:
                    ffn_halo_in(fb + 1)
                if ffn_pend:
                    ffn_finish(*ffn_pend[0])
                    ffn_pend.clear()
                ffn_pend.append((fb, par_, tb, tbn))
            if ffn_pend:
                ffn_finish(*ffn_pend[0])
                ffn_pend.clear()
            P.curtag = "wdown"
            proj_residual(xt, xname, 11, lambda kb, j: actT[:, kb, j * 128:(j + 1) * 128], lambda kb: ("actT", kb))
            if DBG and l == 0 and i == 0 and not dbg.get("done1"):
                DMA("pool", dbg["actT"], actT[:], [("actT", f_) for f_ in range(NFB)], [], "dbg")
                dbg["last"] = DMA("pool", dbg["x2"], xt[:], [(xname, j_) for j_ in range(4)], [], "dbg")
                dbg["done1"] = True

        it = 0
        nxt = DMA("sp", xb[0][:], x_d[0, 0:T, :].rearrange("(j p) d -> p j d", p=128), [], [("xb0", j_) for j_ in range(4)], "xl0")
        stores = []
        for b in range(NB):
            for i in range(NT):
                cur = it % 2
                xt, xname = xb[cur], f"xb{cur}"
                if it + 1 < NB * NT:
                    b2, i2 = divmod(it + 1, NT)
                    o2 = (it + 1) % 2
                    DMA("sp", xb[o2][:], x_d[b2, i2 * T:(i2 + 1) * T, :].rearrange("(j p) d -> p j d", p=128),
                        [], [(f"xb{o2}", j_) for j_ in range(4)], f"xl{o2}")
                for l in range(2):
                    layer(xt, xname, l, i)
                rms_stats(xt, xname)
                for j in range(4):
                    DVE(lambda e, j=j, xt=xt: e.scalar_tensor_tensor(out=xt[:, j, :], in0=xt[:, j, :],
                                                                      scalar=rstd[:, j:j + 1], in1=fgb,
                                                                      op0=ALU.mult, op1=ALU.mult),
                        [(xname, j), ("rstd", j), "fgb"], [(xname, j)])
                stores.append(DMA("pool", y_d[b, i * T:(i + 1) * T, :].rearrange("(j p) d -> p j d", p=128), xt[:],
                                  [(xname, j_) for j_ in range(4)], [], f"ys{cur}"))
                it += 1
        fin = P.op("pool", lambda e: e.engine_nop(), [], [])
        fin.deps = set(stores[-2:]) if len(stores) >= 2 else set(stores)
        if DBG:
            fin.deps.add(dbg["last"])

        block = es.enter_context(nc.Block())
        P.emit(nc, block, esem, dsem)
        build.P = P
    return nc


_INPUT_NAMES = ["x", "norm1_g", "w_in", "conv_a_w", "conv_b_w", "conv_b_b", "ln_b_g", "ln_b_b", "ln_c_g", "ln_c_b",
                "sgu_w", "sgu_b", "w_out", "norm2_g", "w_up", "conv_f_w", "w_down", "final_g"]

_CACHE = {}


def run(inputs, n_cores, NB, S, DBG=False):
    key = (NB, S, DBG)
    if key not in _CACHE:
        _CACHE[key] = build(NB, S, DBG)
    nc = _CACHE[key]
    x = np.ascontiguousarray(np.asarray(inputs["x"], dtype=np.float32))
    in_maps = []
    for c in range(n_cores):
        m = {k: np.ascontiguousarray(np.asarray(inputs[k], dtype=np.float32)) for k in _INPUT_NAMES if k != "x"}
        m["x"] = np.ascontiguousarray(x[c * NB:(c + 1) * NB])
        in_maps.append(m)
    res = run_bass_kernel_spmd(nc, in_maps, core_ids=list(range(n_cores)))
    if DBG:
        return np.concatenate([r["y"] for r in res.results], axis=0), res.results[0]
    return np.concatenate([r["y"] for r in res.results], axis=0)


def kernel(**inputs):
    x = inputs["x"]
    Btot, S, _ = x.shape
    n_cores = 8
    return run(inputs, n_cores, Btot // n_cores, S).astype(np.float32)
```
